# Optimizing a Trainium2 kernel written in Bass

```python
import jax, jax.numpy as jnp
from jax import lax
import numpy as np

D_MODEL = 2048
BATCH = 4
SEQ = 4096
DEPTH = 1

CTX_LEN = 256
GRID_W = 64
EPS = 1e-6

GLA_HEADS = 4
GLA_DK = 256
GLA_DV = 512
GLA_KEY = GLA_HEADS * GLA_DK
GLA_VAL = GLA_HEADS * GLA_DV
GLA_RANK = 16
GLA_GATE_NORM = 16.0
GLA_CHUNK = 64

RNN_WIDTH = D_MODEL
RNN_BLOCKS = 16
RNN_BLOCK_W = RNN_WIDTH // RNN_BLOCKS
CONV_W = 4
CONV_PAD = (2, 1)
LRU_C = 8.0

IN_SIZES = (GLA_KEY, GLA_KEY, GLA_VAL, GLA_VAL, GLA_RANK, GLA_RANK, RNN_WIDTH, RNN_WIDTH, D_MODEL, D_MODEL)
D_IN = sum(IN_SIZES)

kernel_name = "hybrid_gla_rglru_parallel_prefix_block"


def rmsnorm(x, gain):
    xf = x.astype(jnp.float32)
    y = xf * lax.rsqrt(jnp.mean(xf * xf, axis=-1, keepdims=True) + EPS)
    return (y * gain.astype(jnp.float32)).astype(x.dtype)


def split_heads(z, dh):
    b, t, _ = z.shape
    return z.reshape(b, t, -1, dh).transpose(0, 2, 1, 3)


def gla_chunked(q, k, v, log_a, s0):
    b, h, t, _ = q.shape
    dv = v.shape[-1]
    n = t // GLA_CHUNK
    rs = lambda z: z.reshape(b, h, n, GLA_CHUNK, z.shape[-1]).astype(jnp.float32)
    q, k, v, log_a = rs(q), rs(k), rs(v), rs(log_a)
    cum = jnp.cumsum(log_a, axis=3)
    last = cum[:, :, :, -1:, :]
    q_e = q * jnp.exp(cum)
    k_e = k * jnp.exp(-cum)
    k_d = k * jnp.exp(last - cum)
    mask = jnp.tril(jnp.ones((GLA_CHUNK, GLA_CHUNK), dtype=bool))
    scores = jnp.where(mask, jnp.einsum('bhncd,bhnsd->bhncs', q_e, k_e), 0.0)
    o_intra = jnp.einsum('bhncs,bhnse->bhnce', scores, v)

    def step(state, xs):
        qc, kc, vc, dc = xs
        o = jnp.einsum('bhcd,bhde->bhce', qc, state)
        state = state * dc[..., None] + jnp.einsum('bhcd,bhce->bhde', kc, vc)
        return state, o

    xs = (jnp.moveaxis(q_e, 2, 0), jnp.moveaxis(k_d, 2, 0), jnp.moveaxis(v, 2, 0),
          jnp.moveaxis(jnp.exp(last[:, :, :, 0, :]), 2, 0))
    s_final, o_inter = lax.scan(step, s0.astype(jnp.float32), xs)
    o = o_intra + jnp.moveaxis(o_inter, 0, 2)
    return o.reshape(b, h, t, dv), s_final


def gla_bidir(q, k, v, la_f, la_b, s0_f, s0_b):
    o_f, s_f = gla_chunked(q, k, v, la_f, s0_f)
    rev = lambda z: jnp.flip(z, axis=2)
    o_b, s_b = gla_chunked(rev(q), rev(k), rev(v), rev(la_b), s0_b)
    return o_f + rev(o_b), s_f, s_b


def linear_scan(a, u, h0):
    def combine(left, right):
        a_l, u_l = left
        a_r, u_r = right
        return a_l * a_r, a_r * u_l + u_r
    a_cum, h = lax.associative_scan(combine, (a, u), axis=1)
    return h + a_cum * h0[:, None, :]


def block_diag(x, w):
    xb = x.reshape(x.shape[:-1] + (RNN_BLOCKS, RNN_BLOCK_W))
    return jnp.einsum('bthi,hij->bthj', xb, w.astype(jnp.float32)).reshape(x.shape)


def rglru_coeffs(x, w_a, b_a, w_x, b_x, lam):
    r = jax.nn.sigmoid(block_diag(x, w_a) + b_a.astype(jnp.float32))
    i = jax.nn.sigmoid(block_diag(x, w_x) + b_x.astype(jnp.float32))
    log_a = -LRU_C * jax.nn.softplus(-lam.astype(jnp.float32)) * r
    return jnp.exp(log_a), jnp.sqrt(-jnp.expm1(2.0 * log_a)) * (i * x)


def rglru_bidir(xc, w_a, b_a, w_x, b_x, lam, h0_f, h0_b):
    x32 = xc.astype(jnp.float32)
    a_f, u_f = rglru_coeffs(x32, w_a[0], b_a[0], w_x[0], b_x[0], lam[0])
    h_f = linear_scan(a_f, u_f, h0_f)
    a_b, u_b = rglru_coeffs(x32, w_a[1], b_a[1], w_x[1], b_x[1], lam[1])
    h_b = jnp.flip(linear_scan(jnp.flip(a_b, 1), jnp.flip(u_b, 1), h0_b), 1)
    return h_f + h_b, h_f[:, -1], h_b[:, 0]


def depthwise_conv(x, w, bias):
    y = lax.conv_general_dilated(x, w[:, None, :].astype(x.dtype), window_strides=(1,), padding=[CONV_PAD],
                                 dimension_numbers=('NWC', 'WIO', 'NWC'), feature_group_count=x.shape[-1])
    return y + bias.astype(x.dtype)


def project_in(h, shift, scale, g_norm, w_in):
    hn = rmsnorm(h, g_norm) * (1.0 + scale) + shift
    bounds = [int(v) for v in np.cumsum(IN_SIZES)[:-1]]
    return jnp.split(hn @ w_in, bounds, axis=-1)


def mixers(parts, s_f, s_b, h_f, h_b, w_gla_a, b_gla_a, w_conv, b_conv, w_rg_a, b_rg_a, w_rg_x, b_rg_x, lam):
    q = split_heads(parts[0], GLA_DK)
    k = split_heads(parts[1], GLA_DK) * GLA_DK ** -0.5
    v = split_heads(parts[2], GLA_DV)
    la_f = split_heads(jax.nn.log_sigmoid((parts[4] @ w_gla_a[0] + b_gla_a[0]).astype(jnp.float32)) / GLA_GATE_NORM, GLA_DK)
    la_b = split_heads(jax.nn.log_sigmoid((parts[5] @ w_gla_a[1] + b_gla_a[1]).astype(jnp.float32)) / GLA_GATE_NORM, GLA_DK)
    o_gla, s_f, s_b = gla_bidir(q, k, v, la_f, la_b, s_f, s_b)
    xc = depthwise_conv(parts[6], w_conv, b_conv)
    y_rnn, h_f, h_b = rglru_bidir(xc, w_rg_a, b_rg_a, w_rg_x, b_rg_x, lam, h_f, h_b)
    return o_gla, y_rnn, s_f, s_b, h_f, h_b


def merge_out(parts, o_gla, y_rnn, g_gla_out, w_o_gla, w_o_rnn, b_merge, w_out):
    dt = parts[3].dtype
    b, h, t, dv = o_gla.shape
    o = rmsnorm(o_gla, g_gla_out).transpose(0, 2, 1, 3).reshape(b, t, h * dv).astype(dt)
    y_gla = (o * jax.nn.silu(parts[3])) @ w_o_gla
    y_lru = (y_rnn.astype(dt) * jax.nn.silu(parts[7])) @ w_o_rnn
    gate_gla = jax.nn.sigmoid(parts[8] + b_merge[:D_MODEL])
    gate_lru = jax.nn.sigmoid(parts[9] + b_merge[D_MODEL:])
    return (gate_gla * y_gla + gate_lru * y_lru) @ w_out


def setup_inputs(seed: int = 0) -> dict:
    key = jax.random.key(seed)
    ks = jax.random.split(key, 24)
    f32 = jnp.float32
    nrm = lambda k, shape, fan_in: jax.random.normal(k, shape, f32) * fan_in ** -0.5
    x = jax.random.normal(ks[0], (BATCH, SEQ, D_MODEL), f32)
    c = jax.random.normal(ks[1], (BATCH, D_MODEL), f32)
    ctx = jax.random.normal(ks[2], (BATCH, CTX_LEN, D_MODEL), f32)
    c_ctx = jax.random.normal(ks[3], (D_MODEL,), f32)
    w_ada = nrm(ks[4], (DEPTH, D_MODEL, 3 * D_MODEL), D_MODEL) * 0.5
    b_ada = 0.02 * jax.random.normal(ks[5], (DEPTH, 3 * D_MODEL), f32)
    g_norm = 1.0 + 0.05 * jax.random.normal(ks[6], (DEPTH, D_MODEL), f32)
    w_in = nrm(ks[7], (DEPTH, D_MODEL, D_IN), D_MODEL)
    w_gla_a = nrm(ks[8], (DEPTH, 2, GLA_RANK, GLA_KEY), GLA_RANK)
    b_gla_a = 0.5 * jax.random.normal(ks[9], (DEPTH, 2, GLA_KEY), f32)
    g_gla_out = 1.0 + 0.05 * jax.random.normal(ks[10], (DEPTH, GLA_DV), f32)
    w_conv = nrm(ks[11], (DEPTH, CONV_W, RNN_WIDTH), CONV_W)
    b_conv = 0.02 * jax.random.normal(ks[12], (DEPTH, RNN_WIDTH), f32)
    w_rg_a = nrm(ks[13], (DEPTH, 2, RNN_BLOCKS, RNN_BLOCK_W, RNN_BLOCK_W), RNN_BLOCK_W)
    b_rg_a = 0.1 * jax.random.normal(ks[14], (DEPTH, 2, RNN_WIDTH), f32)
    w_rg_x = nrm(ks[15], (DEPTH, 2, RNN_BLOCKS, RNN_BLOCK_W, RNN_BLOCK_W), RNN_BLOCK_W)
    b_rg_x = 0.1 * jax.random.normal(ks[16], (DEPTH, 2, RNN_WIDTH), f32)
    u = jax.random.uniform(ks[17], (DEPTH, 2, RNN_WIDTH), f32, minval=0.9, maxval=0.999)
    a0 = u ** (1.0 / LRU_C)
    lam = jnp.log(a0) - jnp.log1p(-a0)
    w_o_gla = nrm(ks[18], (DEPTH, GLA_VAL, D_MODEL), GLA_VAL)
    w_o_rnn = nrm(ks[19], (DEPTH, RNN_WIDTH, D_MODEL), RNN_WIDTH)
    b_merge = 0.1 * jax.random.normal(ks[20], (DEPTH, 2 * D_MODEL), f32)
    w_out = nrm(ks[21], (DEPTH, D_MODEL, D_MODEL), D_MODEL)
    g_final = 1.0 + 0.05 * jax.random.normal(ks[22], (D_MODEL,), f32)
    return {"x": x, "c": c, "ctx": ctx, "c_ctx": c_ctx, "w_ada": w_ada, "b_ada": b_ada, "g_norm": g_norm,
            "w_in": w_in, "w_gla_a": w_gla_a, "b_gla_a": b_gla_a, "g_gla_out": g_gla_out, "w_conv": w_conv,
            "b_conv": b_conv, "w_rg_a": w_rg_a, "b_rg_a": b_rg_a, "w_rg_x": w_rg_x, "b_rg_x": b_rg_x, "lam": lam,
            "w_o_gla": w_o_gla, "w_o_rnn": w_o_rnn, "b_merge": b_merge, "w_out": w_out, "g_final": g_final}


def reference(x, c, ctx, c_ctx, w_ada, b_ada, g_norm, w_in, w_gla_a, b_gla_a, g_gla_out, w_conv, b_conv,
              w_rg_a, b_rg_a, w_rg_x, b_rg_x, lam, w_o_gla, w_o_rnn, b_merge, w_out, g_final):
    bsz, n_tok, _ = x.shape
    rows = n_tok // GRID_W
    assert rows * GRID_W == n_tok
    zeros_s = jnp.zeros((bsz, GLA_HEADS, GLA_DK, GLA_DV), jnp.float32)
    zeros_h = jnp.zeros((bsz, RNN_WIDTH), jnp.float32)
    for l in range(DEPTH):
        mod_x = jax.nn.silu(c) @ w_ada[l] + b_ada[l]
        mod_c = jax.nn.silu(c_ctx) @ w_ada[l] + b_ada[l]
        sh_x, sc_x, gt_x = [m[:, None, :] for m in jnp.split(mod_x, 3, axis=-1)]
        sh_c, sc_c, gt_c = jnp.split(mod_c, 3, axis=-1)
        lp = (w_gla_a[l], b_gla_a[l], w_conv[l], b_conv[l], w_rg_a[l], b_rg_a[l], w_rg_x[l], b_rg_x[l], lam[l])
        op = (g_gla_out[l], w_o_gla[l], w_o_rnn[l], b_merge[l], w_out[l])
        parts_c = project_in(ctx, sh_c, sc_c, g_norm[l], w_in[l])
        o_c, y_c, s_f, s_b, h_f, h_b = mixers(parts_c, zeros_s, zeros_s, zeros_h, zeros_h, *lp)
        parts_x = project_in(x, sh_x, sc_x, g_norm[l], w_in[l])
        o_x, y_x, _, _, _, _ = mixers(parts_x, s_f, s_b, h_f, h_b, *lp)
        x = x + gt_x * merge_out(parts_x, o_x, y_x, *op)
        if l + 1 < DEPTH:
            ctx = ctx + gt_c * merge_out(parts_c, o_c, y_c, *op)
    return rmsnorm(x, g_final)
```

```python
from contextlib import ExitStack

import numpy as np
import concourse.bass as bass
import concourse.mybir as mybir
from concourse.bass_utils import run_bass_kernel_spmd

F32 = mybir.dt.float32
BF16 = mybir.dt.bfloat16
AF = mybir.ActivationFunctionType
ALU = mybir.AluOpType

D = 2048
KC = 16
P = 128
D_IN = 14368
QO, KO, VO, GGO, LIO, LGO, MGO, MLO = 0, 1024, 2048, 4096, 6176, 8224, 10272, 12320
EPS = 1e-6
T_OWN = 2048
NT_OWN = 16
PF_BADA, PF_GN, PF_BCONV, PF_WCONV, PF_BRG, PF_LAM, PF_BM, PF_N = 0, 48, 64, 80, 160, 224, 256, 288


class Buf:
    __slots__ = ("name", "lw", "rd", "sem", "cnt", "excl")

    def __init__(self, name, excl=False):
        self.name = name
        self.excl = excl
        self.lw = None
        self.rd = {}
        self.sem = None
        self.cnt = 0


class Op:
    __slots__ = ("eng", "idx", "sig", "tick", "fn", "waits", "dma", "sembuf", "val")


class Sched:
    ENG = ("pe", "act", "dve", "pool", "sp")

    def __init__(self, nc):
        self.nc = nc
        self.ops = {e: [] for e in self.ENG}
        self.waited = {e: {} for e in self.ENG}
        self.dma_bufs = []
        self.final_waits = []
        self.marks = []

    def emit(self, eng, fn, reads=(), writes=(), dma=None):
        op = Op()
        op.eng = eng
        op.idx = len(self.ops[eng])
        op.sig = False
        op.tick = 0
        op.fn = fn
        op.dma = dma is not None
        op.sembuf = dma
        op.val = 0
        deps = []
        for b in reads:
            if b.lw is not None:
                deps.append(b.lw)
            if b.excl:
                deps.extend(o for k, o in b.rd.items() if k != ("e", eng))
        for b in writes:
            if b.lw is not None:
                deps.append(b.lw)
            deps.extend(b.rd.values())
        waits = []
        wd = self.waited[eng]
        for d in deps:
            if d.dma:
                key = ("d", id(d.sembuf))
                v = d.val
            else:
                if d.eng == "pe" and eng == "pe":
                    continue
                key = ("e", d.eng)
                v = d.idx
            if wd.get(key, -1) >= v:
                continue
            wd[key] = v
            waits.append(d)
            if not d.dma:
                d.sig = True
        op.waits = waits
        if dma is not None:
            if dma.cnt == 0:
                self.dma_bufs.append(dma)
            dma.cnt += 16
            op.val = dma.cnt
            rkey = ("d", id(dma))
        else:
            rkey = ("e", eng)
        for b in reads:
            b.rd[rkey] = op
        for b in writes:
            b.lw = op
            b.rd = {}
        self.ops[eng].append(op)
        return op

    def mark(self, label):
        self.marks.append((label, {e: len(self.ops[e]) for e in self.ENG}))

    def barrier(self):
        lasts = {e: (self.ops[e][-1] if self.ops[e] else None) for e in self.ENG}
        for e in self.ENG:
            for o in reversed(self.ops[e]):
                if o.fn is not None and not o.dma:
                    lasts[e] = o
                    break
            else:
                lasts[e] = None
        dmas = [(b, b.cnt) for b in self.dma_bufs]
        for f in self.ENG:
            op = Op()
            op.eng = f
            op.idx = len(self.ops[f])
            op.sig = False
            op.tick = 0
            op.fn = None
            op.dma = False
            op.sembuf = None
            op.val = 0
            waits = []
            wd = self.waited[f]
            for e in self.ENG:
                d = lasts[e]
                if d is None or e == f:
                    continue
                if wd.get(("e", e), -1) >= d.idx:
                    continue
                wd[("e", e)] = d.idx
                d.sig = True
                waits.append(d)
            for b, c in dmas:
                key = ("d", id(b))
                if wd.get(key, -1) >= c:
                    continue
                wd[key] = c
                fake = Op()
                fake.dma = True
                fake.sembuf = b
                fake.val = c
                waits.append(fake)
            op.waits = waits
            self.ops[f].append(op)

    def replay(self, es):
        nc = self.nc
        esem = {e: es.enter_context(nc.semaphore("es_" + e)) for e in self.ENG}
        for i, b in enumerate(self.dma_bufs):
            b.sem = es.enter_context(nc.semaphore("ds%d" % i))
        for e in self.ENG:
            c = 0
            for o in self.ops[e]:
                if o.sig and not o.dma:
                    c += 1
                    o.tick = c
        finals = list(self.final_waits)
        block = es.enter_context(nc.Block())

        def run(e, engh, extra=None):
            for o in self.ops[e]:
                for d in o.waits:
                    if d.dma:
                        engh.wait_ge(d.sembuf.sem, d.val)
                    else:
                        engh.wait_ge(esem[d.eng], d.tick)
                if o.fn is None:
                    continue
                ins = o.fn(engh)
                if o.dma:
                    ins.then_inc(o.sembuf.sem, 16)
                elif o.sig:
                    ins.then_inc(esem[e], 1)
            if extra:
                for d in extra:
                    engh.wait_ge(d.sembuf.sem, d.val)

        @block.tensor
        def _(t):
            run("pe", t)

        @block.scalar
        def _(t):
            run("act", t)

        @block.vector
        def _(t):
            run("dve", t)

        @block.gpsimd
        def _(t):
            run("pool", t)

        @block.sync
        def _(t):
            run("sp", t, finals)


def build_nc():
    nc = bass.Bass("TRN2", target_bir_lowering=False)
    S = Sched(nc)
    es = ExitStack()

    def din(name, shape):
        return nc.dram_tensor(name, list(shape), F32, kind="ExternalInput").ap()

    xl = din("xl", [4096, D])
    ctxl = din("ctxl", [256, D])
    ccT = din("ccT", [P, 32])
    pf_d = din("pf", [P, PF_N])
    consts_d = din("consts", [P, 384])
    w_ada = din("w_ada", [D, 3 * D])
    bgate_d = din("bgate", [1, D])
    w_in = din("w_in", [D, D_IN])
    w_la = din("w_la", [D, 64])
    wga_d = din("wga", [64, 1024])
    wrg_d = din("wrg", [16, P, 4 * P])
    ggo_d = din("ggo", [1, 512])
    gfin_d = din("gfin", [1, D])
    w_o_gla = din("w_o_gla", [D, D])
    w_o_rnn = din("w_o_rnn", [D, D])
    w_out = din("w_out", [D, D])
    yl = nc.dram_tensor("yl", [T_OWN, D], F32, kind="ExternalOutput").ap()

    ob_d = nc.dram_tensor("ob_d", [4, NT_OWN, P, 512], F32, kind="Internal").ap()
    mgla_d = nc.dram_tensor("mgla_d", [16, P, T_OWN], BF16, kind="Internal").ap()
    mlru_d = nc.dram_tensor("mlru_d", [2, 16, P, T_OWN], BF16, kind="Internal").ap()
    sg_d = nc.dram_tensor("sg_d", [2, 16, P, T_OWN], BF16, kind="Internal").ap()
    st_d = nc.dram_tensor("st_d", [2, 4, P, 1024], F32, kind="Internal").ap()
    ob_b = [[Buf("ob%d_%d" % (h, n)) for n in range(NT_OWN)] for h in range(4)]
    mgla_b = [Buf("mgla%d" % i) for i in range(16)]
    mlru_b = [[Buf("mlru%d_%d" % (d, i)) for i in range(16)] for d in range(2)]
    sg_b = [[Buf("sg%d_%d" % (w, i)) for i in range(16)] for w in range(2)]
    st_b = [[Buf("st%d_%d" % (d, h)) for h in range(4)] for d in range(2)]
    yl_b = Buf("yl")

    sb_n = [0]

    def sb(stack, name, shape, dt=F32):
        sb_n[0] += 1
        return stack.enter_context(nc.sbuf_tensor("s%d_%s" % (sb_n[0], name), list(shape), dt))

    w_in_v = w_in.rearrange("(kc p) n -> p kc n", p=P)
    w_la_v = w_la.rearrange("(kc p) n -> p kc n", p=P)
    w_ada_v = w_ada.rearrange("(kc p) n -> p kc n", p=P)

    ps_big = [es.enter_context(nc.psum_tensor("psbig%d" % i, [P, 2048], F32)) for i in range(2)]
    ps_t = [ps_big[i // 4][:, (i % 4) * 512:(i % 4 + 1) * 512] for i in range(8)]
    ps_b = [Buf("psb%d" % i, excl=True) for i in range(8)]
    ps_rr = [0]

    def psum():
        i = ps_rr[0] % 8
        ps_rr[0] += 1
        return ps_t[i], ps_b[i]

    def psum2():
        if ps_rr[0] % 2:
            ps_rr[0] += 1
        i = ps_rr[0] % 8
        ps_rr[0] += 2
        return ps_big[i // 4][:, (i % 4) * 512:(i % 4) * 512 + 1024], [ps_b[i], ps_b[i + 1]]

    pers = es
    cst = sb(pers, "cst", [P, 384])
    cst_b = Buf("cst")
    ident = cst[:, 0:128]
    pf = sb(pers, "pfm", [P, PF_N])
    pf_b = Buf("pf")
    tri = sb(pers, "tri", [P, 256], BF16)
    tri_b = Buf("tri")
    mod = sb(pers, "mod", [P, 96])
    mod_b = Buf("mod")
    Ax = sb(pers, "Ax", [P, 64])
    Ax_b = Buf("Ax")
    cl = sb(pers, "cl", [P, 64])
    cl_b = Buf("cl")
    hcl = sb(pers, "hcl", [P, 32])
    hcl_b = Buf("hcl")
    hbrg = sb(pers, "hbrg", [P, 64])
    hbrg_b = Buf("hbrg")
    hbc = sb(pers, "hbc", [P, 16])
    hbc_b = Buf("hbc")
    identb = sb(pers, "identb", [P, P], BF16)
    identb_b = Buf("identb")
    hst = sb(pers, "hst", [P, 32])
    hst_b = [[Buf("hst%d_%d" % (d, j)) for j in range(16)] for d in range(2)]
    grow_d = nc.dram_tensor("grow_d", [1, D], F32, kind="Internal").ap()
    growd_b = Buf("growd")
    ones = sb(pers, "ones", [P, 128])
    ones_b = Buf("ones")
    wga = sb(pers, "wga", [64, 1024], BF16)
    wga_b = Buf("wga")
    scT = sb(pers, "scT", [P, 32], BF16)
    scT_b = Buf("scT")
    small = sb(pers, "small", [P, 64])
    small_bufs = [Buf("small%d" % i) for i in range(64)]
    small_rr = [0]

    def smallcol():
        i = small_rr[0] % 64
        small_rr[0] += 1
        return small[:, i:i + 1], small_bufs[i]

    S.emit("sp", lambda e: e.dma_start(out=cst[:], in_=consts_d), writes=[cst_b], dma=cst_b)
    S.emit("sp", lambda e: e.dma_start(out=pf[:], in_=pf_d), writes=[pf_b], dma=pf_b)
    S.emit("pool", lambda e: e.dma_start(out=wga[:], in_=wga_d), writes=[wga_b], dma=wga_b)
    S.emit("dve", lambda e: e.tensor_scalar(out=tri[:], in0=cst[:, 128:384], scalar1=-1.0 / 16.0, scalar2=None,
                                            op0=ALU.mult), reads=[cst_b], writes=[tri_b])
    S.emit("dve", lambda e: e.memset(ones[:], 1.0), writes=[ones_b])
    S.emit("dve", lambda e: e.memset(hst[:], 0.0), writes=[b for r in hst_b for b in r])

    with ExitStack() as ph:
        ccs = sb(ph, "ccs", [P, 32])
        ccs_b = Buf("ccs")
        S.emit("sp", lambda e: e.dma_start(out=ccs[:], in_=ccT), writes=[ccs_b], dma=ccs_b)
        S.emit("act", lambda e: e.activation(out=scT[:], in_=ccs[:], func=AF.Silu), reads=[ccs_b], writes=[scT_b])
        scT3 = scT[:].rearrange("p (k r) -> p k r", r=2)
        wb = [sb(ph, "wadab%d" % i, [P, KC, 512], BF16) for i in range(2)]
        wb_b = [Buf("wadab%d" % i) for i in range(2)]
        brow = sb(ph, "brow", [1, D])
        brow_b = Buf("brow")
        grow = sb(ph, "grow", [1, D])
        grow_b = Buf("grow")
        S.emit("sp", lambda e: e.dma_start(out=brow[:], in_=bgate_d), writes=[brow_b], dma=brow_b)
        psA, psA_b = psum()
        for ng in range(12):
            w, w_b = wb[ng % 2], wb_b[ng % 2]
            S.emit("pool", lambda e, w=w, ng=ng: e.dma_start(out=w[:], in_=w_ada_v[:, :, ng * 512:(ng + 1) * 512]),
                   writes=[w_b], dma=w_b)
            for j in range(4):
                n = ng * 4 + j
                for kc in range(KC):
                    S.emit("pe", lambda e, w=w, j=j, kc=kc, n=n: e.matmul(
                        psA[:, 2 * n:2 * n + 2], lhsT=w[:, kc, j * 128:(j + 1) * 128], rhs=scT3[:, kc, :],
                        start=(kc == 0), stop=(kc == KC - 1)), reads=[w_b, scT_b], writes=[psA_b])
            if ng >= 8:
                psG, psG_b = psum()
                for kc in range(KC):
                    S.emit("pe", lambda e, w=w, kc=kc, psG=psG: e.matmul(
                        psG[0:1, :], lhsT=scT3[:, kc, 0:1], rhs=w[:, kc, :],
                        start=(kc == 0), stop=(kc == KC - 1)), reads=[w_b, scT_b], writes=[psG_b])
                c0 = (ng - 8) * 512
                S.emit("dve", lambda e, psG=psG, c0=c0: e.tensor_tensor(
                    out=grow[0:1, c0:c0 + 512], in0=psG[0:1, :], in1=brow[0:1, c0:c0 + 512], op=ALU.add),
                    reads=[psG_b, brow_b], writes=[grow_b])
        S.emit("sp", lambda e: e.dma_start(out=grow_d, in_=grow[:]), reads=[grow_b], writes=[growd_b], dma=grow_b)
        psA3 = psA[:, 0:96].rearrange("p (n r) -> p n r", r=2)
        mod3 = mod[:].rearrange("p (n r) -> p n r", r=2)
        for r in range(2):
            S.emit("dve", lambda e, r=r: e.tensor_tensor(out=mod3[:, :, r], in0=psA3[:, :, r],
                                                         in1=pf[:, PF_BADA:PF_BADA + 48], op=ALU.add),
                   reads=[psA_b, pf_b], writes=[mod_b])
        for r in range(2):
            S.emit("dve", lambda e, r=r: e.scalar_tensor_tensor(
                out=Ax[:, 32 * r:32 * r + 16], in0=mod3[:, 16:32, r], scalar=1.0, in1=pf[:, PF_GN:PF_GN + 16],
                op0=ALU.add, op1=ALU.mult), reads=[mod_b, pf_b], writes=[Ax_b])
            S.emit("dve", lambda e, r=r: e.tensor_copy(out=Ax[:, 32 * r + 16:32 * r + 32], in_=mod3[:, 0:16, r]),
                   reads=[mod_b], writes=[Ax_b])
        tmpl = sb(ph, "tmpl", [P, 32])
        tmpl_b = Buf("tmpl")
        S.emit("act", lambda e: e.activation(out=tmpl[:], in_=pf[:, PF_LAM:PF_LAM + 32], func=AF.Exp, scale=-1.0),
               reads=[pf_b], writes=[tmpl_b])
        S.emit("act", lambda e: e.activation(out=tmpl[:], in_=tmpl[:], func=AF.Ln, bias=1.0),
               reads=[tmpl_b], writes=[tmpl_b])
        S.emit("dve", lambda e: e.tensor_scalar(out=cl[:, 0:32], in0=tmpl[:], scalar1=-8.0, scalar2=None, op0=ALU.mult),
               reads=[tmpl_b], writes=[cl_b])
        S.emit("dve", lambda e: e.tensor_scalar(out=cl[:, 32:64], in0=tmpl[:], scalar1=-16.0, scalar2=None, op0=ALU.mult),
               reads=[tmpl_b], writes=[cl_b])
        S.emit("dve", lambda e: e.tensor_scalar(out=hcl[:], in0=tmpl[:], scalar1=-4.0, scalar2=None, op0=ALU.mult),
               reads=[tmpl_b], writes=[hcl_b])
        S.emit("dve", lambda e: e.tensor_scalar(out=hbrg[:], in0=pf[:, PF_BRG:PF_BRG + 64], scalar1=0.5, scalar2=None,
                                                op0=ALU.mult), reads=[pf_b], writes=[hbrg_b])
        S.emit("dve", lambda e: e.tensor_scalar(out=hbc[:], in0=pf[:, PF_BCONV:PF_BCONV + 16], scalar1=0.5, scalar2=None,
                                                op0=ALU.mult), reads=[pf_b], writes=[hbc_b])
        S.emit("dve", lambda e: e.tensor_copy(out=identb[:], in_=cst[:, 0:128]), reads=[cst_b], writes=[identb_b])
        S.barrier()

    with ExitStack() as ph:
        NTH = 17
        hnT = sb(ph, "hnT", [P, KC, NTH * P], BF16)
        hn_b = [[Buf("hn%d_%d" % (i, g)) for g in range(4)] for i in range(NTH)]
        lra = sb(ph, "lra", [64, NTH * P], BF16)
        lra_b = Buf("lra")
        wla = sb(ph, "wla", [P, KC, 64], BF16)
        wla_b = Buf("wla")
        S.emit("pool", lambda e: e.dma_start(out=wla[:], in_=w_la_v), writes=[wla_b], dma=wla_b)
        xt_rr = [0]

        def rstd_from_ssq(ssq, ssq_b, n):
            r, r_b = smallcol()
            S.emit("dve", lambda e: e.tensor_scalar(out=r, in0=ssq, scalar1=1.0 / n, scalar2=EPS, op0=ALU.mult,
                                                    op1=ALU.add), reads=[ssq_b], writes=[r_b])
            S.emit("act", lambda e: e.activation(out=r, in_=r, func=AF.Sqrt), reads=[r_b], writes=[r_b])
            S.emit("dve", lambda e: e.reciprocal(out=r, in_=r), reads=[r_b], writes=[r_b])
            return r, r_b

        def build_hn(src, row0, ntiles, modoff, tile0):
            with ExitStack() as bs:
                build_hn_(bs, src, row0, ntiles, modoff, tile0)
                S.barrier()

        def build_hn_(bs, src, row0, ntiles, modoff, tile0):
            xt = [sb(bs, "xt%d" % i, [P, D]) for i in range(2)]
            xt_b = [Buf("xt%d" % i) for i in range(2)]
            xns = [sb(bs, "xn%d" % i, [P, D]) for i in range(2)]
            xns_b = [Buf("xn%d" % i) for i in range(2)]
            junk = sb(bs, "junk", [P, D], BF16)
            junk_b = Buf("junk")
            def stats(t):
                xi = xt_rr[0] % 2
                xt_rr[0] += 1
                x_, x_b = xt[xi], xt_b[xi]
                r0 = row0 + t * P
                S.emit("sp", lambda e: e.dma_start(out=x_[:], in_=src[r0:r0 + P, :]), writes=[x_b], dma=x_b)
                ssq, ssq_b = smallcol()
                S.emit("act", lambda e: e.activation(out=junk[:], in_=x_[:], func=AF.Square, accum_out=ssq),
                       reads=[x_b], writes=[junk_b, ssq_b])
                r, r_b = rstd_from_ssq(ssq, ssq_b, D)
                xn, xn_b = xns[t % 2], xns_b[t % 2]
                S.emit("dve", lambda e: e.tensor_scalar(out=xn[:], in0=x_[:], scalar1=r, scalar2=None, op0=ALU.mult),
                       reads=[x_b, r_b], writes=[xn_b])

            def trans(t):
                xn, xn_b = xns[t % 2], xns_b[t % 2]
                col = (tile0 + t) * P
                for g in range(4):
                    hb = hn_b[tile0 + t][g]
                    pt, pt_b = psum()
                    for j in range(4):
                        kc = 4 * g + j
                        S.emit("pe", lambda e, pt=pt, j=j, kc=kc: e.transpose(
                            out=pt[:, j * P:(j + 1) * P], in_=xn[:, kc * P:(kc + 1) * P], identity=ident),
                            reads=[xn_b, cst_b], writes=[pt_b])
                    for j in range(4):
                        kc = 4 * g + j
                        a_ap = Ax[:, modoff + kc:modoff + kc + 1]
                        s_ap = Ax[:, modoff + 16 + kc:modoff + 16 + kc + 1]
                        if g % 2 == 0:
                            S.emit("dve", lambda e, pt=pt, j=j, kc=kc, a_ap=a_ap, s_ap=s_ap: e.tensor_scalar(
                                out=hnT[:, kc, col:col + P], in0=pt[:, j * P:(j + 1) * P], scalar1=a_ap, scalar2=s_ap,
                                op0=ALU.mult, op1=ALU.add), reads=[pt_b, Ax_b], writes=[hb])
                        else:
                            S.emit("act", lambda e, pt=pt, j=j, kc=kc, a_ap=a_ap, s_ap=s_ap: e.activation(
                                out=hnT[:, kc, col:col + P], in_=pt[:, j * P:(j + 1) * P], func=AF.Identity,
                                scale=a_ap, bias=s_ap), reads=[pt_b, Ax_b], writes=[hb])

            stats(0)
            for t in range(ntiles):
                if t + 1 < ntiles:
                    stats(t + 1)
                trans(t)

        def hn_bufs(c0, c1):
            return [b for tb in hn_b[c0 // P:(c1 + P - 1) // P] for b in tb]

        def build_lr(c0, T):
            S.emit("dve", lambda e: e.memset(lra[:, c0:c0 + T], 1.0), writes=[lra_b])
            for g0 in range(0, T, 512):
                n = min(512, T - g0)
                pt, pt_b = psum()
                a = c0 + g0
                for kc in range(KC):
                    S.emit("pe", lambda e, pt=pt, kc=kc, a=a, n=n: e.matmul(
                        pt[0:64, 0:n], lhsT=wla[:, kc, :], rhs=hnT[:, kc, a:a + n], start=(kc == 0),
                        stop=(kc == KC - 1)), reads=[wla_b] + hn_bufs(a, a + n), writes=[pt_b])
                S.emit("act", lambda e, pt=pt, a=a, n=n: e.activation(out=lra[0:16, a:a + n], in_=pt[0:16, 0:n],
                                                                      func=AF.Copy), reads=[pt_b], writes=[lra_b])
                S.emit("act", lambda e, pt=pt, a=a, n=n: e.activation(out=lra[32:48, a:a + n], in_=pt[32:48, 0:n],
                                                                      func=AF.Copy), reads=[pt_b], writes=[lra_b])

        def gla_stage(gs, c0, nch, dirs, full, zero_state):
            WA = sb(gs, "WA", [P, KC, 512], BF16)
            WA_b = Buf("WA")
            WB = sb(gs, "WB", [P, KC, 512], BF16)
            WB_b = Buf("WB")
            T = nch * P
            kfm = sb(gs, "kfm", [P, 2, T], BF16)
            kfm_b = Buf("kfm")
            vtm = sb(gs, "vtm", [P, nch, 512], BF16)
            vtm_b = [Buf("vtm%d" % i) for i in range(nch)]
            def mk(name, shape, dt=F32, n=2):
                return ([sb(gs, "%s%d" % (name, i), shape, dt) for i in range(n)],
                        [Buf("%s%d" % (name, i)) for i in range(n)])

            if full:
                qfm = sb(gs, "qfm", [P, 2, T], BF16)
                qfm_b = Buf("qfm")
                ggb = sb(gs, "ggb", [P, 512])
                ggb_b = Buf("ggb")
                ggr = sb(gs, "ggr", [1, 512])
                ggr_b = Buf("ggr")
                S.emit("sp", lambda e: e.dma_start(out=ggr[:], in_=ggo_d), writes=[ggr_b], dma=ggr_b)
                pt, pt_b = psum()
                S.emit("pe", lambda e: e.matmul(pt[:, :], lhsT=ones[0:1, :], rhs=ggr[0:1, :], start=True, stop=True),
                       reads=[ones_b, ggr_b], writes=[pt_b])
                S.emit("act", lambda e: e.activation(out=ggb[:], in_=pt[:, :], func=AF.Copy), reads=[pt_b],
                       writes=[ggb_b])
                obt, obt_b = mk("obt", [P, 512])
                osum, osum_b = mk("osum", [P, 512], n=3)
                jk = sb(gs, "jk", [P, 512], BF16)
                jk_b = Buf("jk")
                mT, mT_b = mk("mT", [P, 4, P], BF16)
                E1, E1_b = mk("E1", [P, 256], n=4)
                qe, qe_b = mk("qe", [P, 2, P], BF16, n=4)
                ke, ke_b = mk("ke", [P, 2, P], BF16)
                sc, sc_b = mk("sc", [P, P], BF16)
                rs_, rs_b = mk("rs", [P, 1], n=4)
                Sbf = [[sb(gs, "Sbf%d_%d" % (d, pp), [P, 2, 512], BF16) for pp in range(2)] for d in range(2)]
                Sbf_b = [[[Buf("Sbf%d_%d_%d" % (d, pp, dc)) for dc in range(2)] for pp in range(2)] for d in range(2)]
            else:
                off = sb(gs, "off", [P, 2])
                off_b = Buf("off")
                Sacc = sb(gs, "Sacc", [P, 2, 512])
                Sacc_b = Buf("Sacc")
            Sst = [sb(gs, "Sst%d" % d, [P, 2, 512]) for d in range(2)]
            Sst_b = [Buf("Sst%d" % d) for d in range(2)]
            ez, ez_b = mk("ez", [P, 256])
            nl, nl_b = mk("nl", [P, 256], BF16)
            E2, E2_b = mk("E2", [P, 256])
            dcy, dcy_b = mk("dcy", [P, 2], n=4)
            kd, kd_b = mk("kd", [P, 2, P])
            kdt, kdt_b = mk("kdt", [P, 256], BF16)
            PZ, PC, PK, PV0, PV1, PO0, PO1, PM = range(8)

            def proj_fm(dst, dst_b, dcg, wcol, scale):
                for g0 in range(0, T, 512):
                    n = min(512, T - g0)
                    pt, pt_b = psum()
                    a = c0 + g0
                    for kc in range(KC):
                        S.emit("pe", lambda e, pt=pt, kc=kc, a=a, n=n: e.matmul(
                            pt[:, 0:n], lhsT=WA[:, kc, wcol:wcol + P], rhs=hnT[:, kc, a:a + n], start=(kc == 0),
                            stop=(kc == KC - 1)), reads=[WA_b] + hn_bufs(a, a + n), writes=[pt_b])
                    S.emit("act", lambda e, pt=pt, g0=g0, n=n: e.activation(
                        out=dst[:, dcg, g0:g0 + n], in_=pt[:, 0:n], func=AF.Copy, scale=scale),
                        reads=[pt_b], writes=[dst_b])

            for h in range(4):
                if full:
                    S.emit("pool", lambda e, h=h: e.dma_start(out=WA[:, :, 0:256],
                                                              in_=w_in_v[:, :, QO + h * 256:QO + (h + 1) * 256]),
                           writes=[WA_b], dma=WA_b)
                S.emit("pool", lambda e, h=h: e.dma_start(out=WA[:, :, 256:512],
                                                          in_=w_in_v[:, :, KO + h * 256:KO + (h + 1) * 256]),
                       writes=[WA_b], dma=WA_b)
                S.emit("pool", lambda e, h=h: e.dma_start(out=WB[:], in_=w_in_v[:, :, VO + h * 512:VO + (h + 1) * 512]),
                       writes=[WB_b], dma=WB_b)
                if full:
                    for dc in range(2):
                        proj_fm(qfm, qfm_b, dc, dc * P, 1.0)
                for dc in range(2):
                    proj_fm(kfm, kfm_b, dc, 256 + dc * P, 1.0 / 16.0)
                for n in range(nch):
                    pt, pt_b = psum()
                    a = c0 + n * P
                    for kc in range(KC):
                        S.emit("pe", lambda e, pt=pt, kc=kc, a=a: e.matmul(
                            pt[:, :], lhsT=hnT[:, kc, a:a + P], rhs=WB[:, kc, :], start=(kc == 0),
                            stop=(kc == KC - 1)), reads=[WB_b] + hn_bufs(a, a + P), writes=[pt_b])
                    if n % 2 == 0:
                        S.emit("act", lambda e, pt=pt, n=n: e.activation(out=vtm[:, n, :], in_=pt[:, :], func=AF.Copy),
                               reads=[pt_b], writes=[vtm_b[n]])
                    else:
                        S.emit("dve", lambda e, pt=pt, n=n: e.tensor_copy(out=vtm[:, n, :], in_=pt[:, :]),
                               reads=[pt_b], writes=[vtm_b[n]])
                for d in dirs:
                    if zero_state:
                        S.emit("dve", lambda e, d=d: e.memset(Sst[d][:], 0.0), writes=[Sst_b[d]])
                    else:
                        S.emit("sp", lambda e, d=d, h=h: e.dma_start(
                            out=Sst[d][:].rearrange("p a b -> p (a b)"), in_=st_d[d, h]),
                            reads=[st_b[d][h]], writes=[Sst_b[d]], dma=Sst_b[d])
                    if full:
                        S.emit("act", lambda e, d=d: e.activation(out=Sbf[d][1][:], in_=Sst[d][:], func=AF.Copy),
                               reads=[Sst_b[d]], writes=Sbf_b[d][1])

                def pipeline(h, seq):
                    NPOS = len(seq)

                    def A1(p):
                        d, n, q = seq[p]
                        a = c0 + n * P
                        base = 32 * d
                        i = p % 2
                        S.emit("pe", lambda e: e.matmul(
                            ps_t[PZ][:, 0:256], lhsT=lra[base:base + 17, a:a + P],
                            rhs=wga[base:base + 17, h * 256:(h + 1) * 256], start=True, stop=True),
                            reads=[lra_b, wga_b], writes=[ps_b[PZ]])
                        S.emit("act", lambda e: e.activation(out=ez[i][:], in_=ps_t[PZ][:, 0:256], func=AF.Exp, scale=-1.0),
                               reads=[ps_b[PZ]], writes=[ez_b[i]])
                        S.emit("act", lambda e: e.activation(out=nl[i][:], in_=ez[i][:], func=AF.Ln, bias=1.0),
                               reads=[ez_b[i]], writes=[nl_b[i]])

                    def A2(p):
                        d, n, q = seq[p]
                        i = p % 2
                        i4 = p % 4
                        TR = tri[:, d * P:(d + 1) * P]
                        pc = ps_t[PC]
                        for dc in range(2):
                            S.emit("pe", lambda e, dc=dc: e.matmul(
                                pc[:, dc * P:(dc + 1) * P], lhsT=nl[i][:, dc * P:(dc + 1) * P], rhs=TR, start=True, stop=True),
                                reads=[nl_b[i], tri_b], writes=[ps_b[PC]])
                        S.emit("act", lambda e: e.activation(out=E2[i][:], in_=pc[:, 0:256], func=AF.Exp, scale=-1.0),
                               reads=[ps_b[PC]], writes=[E2_b[i]])
                        lastcol = (P - 1) if d == 0 else 0
                        pc3 = pc[:, 0:256].rearrange("p (a b) -> p a b", b=P)
                        if full:
                            S.emit("act", lambda e: e.activation(out=E1[i4][:], in_=pc[:, 0:256], func=AF.Exp),
                                   reads=[ps_b[PC]], writes=[E1_b[i4]])
                        else:
                            for dc in range(2):
                                col = dc * P + lastcol
                                S.emit("act", lambda e, dc=dc, col=col: e.activation(
                                    out=dcy[i4][:, dc:dc + 1], in_=pc[:, col:col + 1], func=AF.Exp, bias=off[:, dc:dc + 1]),
                                    reads=[ps_b[PC], off_b], writes=[dcy_b[i4]])
                            S.emit("act", lambda e: e.activation(out=ez[i][:, 0:2], in_=pc3[:, :, lastcol], func=AF.Copy),
                                   reads=[ps_b[PC]], writes=[ez_b[i]])
                            S.emit("dve", lambda e: e.tensor_tensor(out=off[:], in0=off[:], in1=ez[i][:, 0:2], op=ALU.add),
                                   reads=[off_b, ez_b[i]], writes=[off_b])

                    def A2b(p):
                        d, n, q = seq[p]
                        i = p % 2
                        i4 = p % 4
                        tk = slice(n * P, (n + 1) * P)
                        if full:
                            S.emit("dve", lambda e: e.tensor_tensor(
                                out=qe[i4][:], in0=qfm[:, :, tk], in1=E1[i4][:].rearrange("p (a b) -> p a b", b=P), op=ALU.mult),
                                reads=[qfm_b, E1_b[i4]], writes=[qe_b[i4]])
                            S.emit("dve", lambda e: e.tensor_tensor(
                                out=ke[i][:], in0=kfm[:, :, tk], in1=E2[i][:].rearrange("p (a b) -> p a b", b=P), op=ALU.mult),
                                reads=[kfm_b, E2_b[i]], writes=[ke_b[i]])
                        lastcol_ = (P - 1) if d == 0 else 0
                        for dc in range(2):
                            if full:
                                dsc, dsc_b = E1[i4][:, dc * P + lastcol_:dc * P + lastcol_ + 1], E1_b[i4]
                            else:
                                dsc, dsc_b = dcy[i4][:, dc:dc + 1], dcy_b[i4]
                            S.emit("dve", lambda e, dc=dc, dsc=dsc: e.scalar_tensor_tensor(
                                out=kd[i][:, dc, :], in0=E2[i][:, dc * P:(dc + 1) * P], scalar=dsc,
                                in1=kfm[:, dc, tk], op0=ALU.mult, op1=ALU.mult),
                                reads=[E2_b[i], dsc_b, kfm_b], writes=[kd_b[i]])

                    def A3(p):
                        d, n, q = seq[p]
                        i = p % 2
                        i4 = p % 4
                        MK = cst[:, 128 + d * P:128 + (d + 1) * P]
                        pk = ps_t[PK]
                        for dc in range(2):
                            S.emit("pe", lambda e, dc=dc: e.transpose(
                                out=pk[:, dc * P:(dc + 1) * P], in_=kd[i][:, dc, :], identity=ident),
                                reads=[kd_b[i], cst_b], writes=[ps_b[PK]])
                        if full:
                            for dc in range(2):
                                S.emit("pe", lambda e, dc=dc: e.matmul(
                                    pk[:, 256:384], lhsT=ke[i][:, dc, :], rhs=qe[i4][:, dc, :], start=(dc == 0), stop=(dc == 1)),
                                    reads=[ke_b[i], qe_b[i4]], writes=[ps_b[PK]])
                        S.emit("dve", lambda e: e.tensor_copy(out=kdt[i][:], in_=pk[:, 0:256]),
                               reads=[ps_b[PK]], writes=[kdt_b[i]])
                        if full:
                            S.emit("dve", lambda e: e.tensor_tensor(out=sc[i][:], in0=pk[:, 256:384], in1=MK, op=ALU.mult),
                                   reads=[ps_b[PK], cst_b], writes=[sc_b[i]])

                    def B1(p):
                        d, n, q = seq[p]
                        i = p % 2
                        i4 = p % 4
                        for dc, bank in ((0, PV0), (1, PV1)):
                            S.emit("pe", lambda e, dc=dc, bank=bank: e.matmul(
                                ps_t[bank][:, :], lhsT=kdt[i][:, dc * P:(dc + 1) * P], rhs=vtm[:, n, :], start=True, stop=True),
                                reads=[kdt_b[i], vtm_b[n]], writes=[ps_b[bank]])
                        if not full:
                            for dc, bank in ((0, PV0), (1, PV1)):
                                S.emit("dve", lambda e, dc=dc, bank=bank: e.tensor_tensor(
                                    out=Sacc[:, dc, :], in0=Sacc[:, dc, :], in1=ps_t[bank][:, :], op=ALU.add),
                                    reads=[Sacc_b, ps_b[bank]], writes=[Sacc_b])
                            return
                        pp, pq = q % 2, (q - 1) % 2
                        po, po_b = ps_t[PO0 + i], ps_b[PO0 + i]
                        S.emit("pe", lambda e: e.matmul(po[:, :], lhsT=sc[i][:], rhs=vtm[:, n, :], start=True, stop=False),
                               reads=[sc_b[i], vtm_b[n]], writes=[po_b])
                        for dc in range(2):
                            S.emit("pe", lambda e, dc=dc: e.matmul(
                                po[:, :], lhsT=qe[i4][:, dc, :], rhs=Sbf[d][pq][:, dc, :], start=False, stop=(dc == 1)),
                                reads=[qe_b[i4], Sbf_b[d][pq][dc]], writes=[po_b])
                        lastcol_ = (P - 1) if d == 0 else 0
                        for dc, bank in ((0, PV0), (1, PV1)):
                            dsc = E1[i4][:, dc * P + lastcol_:dc * P + lastcol_ + 1]
                            S.emit("dve", lambda e, dc=dc, bank=bank, dsc=dsc: e.scalar_tensor_tensor(
                                out=Sst[d][:, dc, :], in0=Sst[d][:, dc, :], scalar=dsc, in1=ps_t[bank][:, :],
                                op0=ALU.mult, op1=ALU.add), reads=[Sst_b[d], E1_b[i4], ps_b[bank]], writes=[Sst_b[d]])
                        S.emit("act", lambda e: e.activation(out=Sbf[d][pp][:], in_=Sst[d][:], func=AF.Copy),
                               reads=[Sst_b[d]], writes=Sbf_b[d][pp])

                    def B2(p):
                        d, n, q = seq[p]
                        i = p % 2
                        i3 = p % 3
                        i4 = p % 4
                        po, po_b = ps_t[PO0 + i], ps_b[PO0 + i]
                        if d == 1:
                            S.emit("act", lambda e: e.activation(out=obt[i][:], in_=po[:, :], func=AF.Copy),
                                   reads=[po_b], writes=[obt_b[i]])
                            S.emit("sp", lambda e: e.dma_start(out=ob_d[h, n], in_=obt[i][:]),
                                   reads=[obt_b[i]], writes=[ob_b[h][n]], dma=obt_b[i])
                            return
                        S.emit("sp", lambda e: e.dma_start(out=obt[i][:], in_=ob_d[h, n]),
                               reads=[ob_b[h][n]], writes=[obt_b[i]], dma=obt_b[i])
                        S.emit("dve", lambda e: e.tensor_tensor(out=osum[i3][:], in0=po[:, :], in1=obt[i][:], op=ALU.add),
                               reads=[po_b, obt_b[i]], writes=[osum_b[i3]])
                        r_ = rs_[i4][:, 0:1]
                        S.emit("act", lambda e: e.activation(out=jk[:], in_=osum[i3][:], func=AF.Square, accum_out=r_),
                               reads=[osum_b[i3]], writes=[jk_b, rs_b[i4]])
                        S.emit("act", lambda e: e.activation(out=r_, in_=r_, func=AF.Ln, scale=1.0 / 512.0, bias=EPS),
                               reads=[rs_b[i4]], writes=[rs_b[i4]])
                        S.emit("act", lambda e: e.activation(out=r_, in_=r_, func=AF.Exp, scale=-0.5),
                               reads=[rs_b[i4]], writes=[rs_b[i4]])

                    def B2b(p):
                        d, n, q = seq[p]
                        if d == 1:
                            return
                        i3 = p % 3
                        i4 = p % 4
                        S.emit("dve", lambda e: e.scalar_tensor_tensor(
                            out=osum[i3][:], in0=osum[i3][:], scalar=rs_[i4][:, 0:1], in1=ggb[:], op0=ALU.mult, op1=ALU.mult),
                            reads=[osum_b[i3], rs_b[i4], ggb_b], writes=[osum_b[i3]])

                    def B3(p):
                        d, n, q = seq[p]
                        if d == 1:
                            return
                        i = p % 2
                        i3 = p % 3
                        tk = slice(n * P, (n + 1) * P)
                        pm = ps_t[PM]
                        for j in range(4):
                            S.emit("pe", lambda e, j=j: e.transpose(
                                out=pm[:, j * P:(j + 1) * P], in_=osum[i3][:, j * P:(j + 1) * P], identity=ident),
                                reads=[osum_b[i3], cst_b], writes=[ps_b[PM]])
                        S.emit("act", lambda e: e.activation(
                            out=mT[i][:].rearrange("p a b -> p (a b)"), in_=pm[:, :], func=AF.Copy),
                            reads=[ps_b[PM]], writes=[mT_b[i]])
                        S.emit("sp", lambda e: e.dma_start(
                            out=mgla_d[h * 4:(h + 1) * 4, :, tk].rearrange("j p t -> p j t"), in_=mT[i][:]),
                            reads=[mT_b[i]], writes=[mgla_b[h * 4 + j] for j in range(4)], dma=mT_b[i])

                    if full:
                        stages = ((B1, 4), (A3, 3), (A2b, 2), (A2, 1), (A1, 0), (B2, 5), (B2b, 6), (B3, 7))
                    else:
                        stages = ((B1, 4), (A3, 3), (A2b, 2), (A2, 1), (A1, 0))
                    maxd = max(dl for _, dl in stages)
                    for k in range(NPOS + maxd):
                        for fn, dl in stages:
                            p = k - dl
                            if 0 <= p < NPOS:
                                fn(p)

                if full:
                    seq = [(1, n, q) for q, n in enumerate(range(nch - 1, -1, -1))] + \
                          [(0, n, q) for q, n in enumerate(range(nch))]
                    pipeline(h, seq)
                else:
                    for d in dirs:
                        S.emit("dve", lambda e: e.memset(off[:], 0.0), writes=[off_b])
                        S.emit("dve", lambda e: e.memset(Sacc[:], 0.0), writes=[Sacc_b])
                        rev = range(nch - 1, -1, -1) if d == 0 else range(nch)
                        pipeline(h, [(d, n, q) for q, n in enumerate(rev)])
                        tot, tot_b = dcy[0], dcy_b[0]
                        S.emit("act", lambda e, tot=tot: e.activation(out=tot[:], in_=off[:], func=AF.Exp),
                               reads=[off_b], writes=[tot_b])
                        for dc in range(2):
                            S.emit("dve", lambda e, d=d, dc=dc, tot=tot: e.scalar_tensor_tensor(
                                out=Sst[d][:, dc, :], in0=Sst[d][:, dc, :], scalar=tot[:, dc:dc + 1], in1=Sacc[:, dc, :],
                                op0=ALU.mult, op1=ALU.add), reads=[Sst_b[d], tot_b, Sacc_b], writes=[Sst_b[d]])
                        S.emit("sp", lambda e, d=d, h=h: e.dma_start(
                            out=st_d[d, h], in_=Sst[d][:].rearrange("p a b -> p (a b)")),
                            reads=[Sst_b[d]], writes=[st_b[d][h]], dma=Sst_b[d])

        def lru_stage(ls, c0, T, dirs, full, halo_l, halo_r):
            nd = len(dirs)
            wl = [sb(ls, "wl%d" % i, [P, KC, P], BF16) for i in range(2)]
            wl_b = [Buf("wl%d" % i) for i in range(2)]
            wrj = [sb(ls, "wrj%d" % i, [P, 4, P], BF16) for i in range(3)]
            wrj_b = [Buf("wrj%d" % i) for i in range(3)]
            dg = [sb(ls, "dg%d" % i, [P, 5, P], BF16) for i in range(2)]
            dg_b = [Buf("dg%d" % i) for i in range(2)]
            xin = [sb(ls, "xin%d" % i, [P, T + 4], BF16) for i in range(2)]
            xin_b = [Buf("xin%d" % i) for i in range(2)]
            xcb = [sb(ls, "xcb%d" % i, [P, T], BF16) for i in range(2)]
            xcb_b = [Buf("xcb%d" % i) for i in range(2)]
            NS = 3
            rr = [sb(ls, "rr%d" % i, [P, T]) for i in range(NS)]
            rr_b = [Buf("rr%d" % i) for i in range(NS)]
            ig = [sb(ls, "ig%d" % i, [P, T]) for i in range(NS)]
            ig_b = [Buf("ig%d" % i) for i in range(NS)]
            aa = [sb(ls, "aa%d" % i, [P, T]) for i in range(NS)]
            aa_b = [Buf("aa%d" % i) for i in range(NS)]
            hh = [sb(ls, "hh%d" % i, [P, T], BF16 if full else F32) for i in range(2)]
            hh_b = [Buf("hh%d" % i) for i in range(2)]
            groups = [(g0, min(512, T - g0)) for g0 in range(0, T, 512)]
            groups2 = [(g0, min(1024, T - g0)) for g0 in range(0, T, 1024)]
            units = [(j, d) for j in range(16) for d in dirs]
            NU = len(units)
            ev = [0]

            def evac_copy(dst_ap, src_ap, rbufs, wbufs):
                ev[0] += 1
                if ev[0] % 2 == 0:
                    S.emit("act", lambda e: e.activation(out=dst_ap, in_=src_ap, func=AF.Copy), reads=rbufs, writes=wbufs)
                else:
                    S.emit("dve", lambda e: e.tensor_copy(out=dst_ap, in_=src_ap), reads=rbufs, writes=wbufs)

            def a_setup(j):
                s_ = j % 2
                S.emit("pool", lambda e: e.dma_start(out=wl[s_][:], in_=w_in_v[:, :, LIO + j * P:LIO + (j + 1) * P]),
                       writes=[wl_b[s_]], dma=wl_b[s_])
                w3 = j % 3
                S.emit("pool", lambda e: e.dma_start(out=wrj[w3][:].rearrange("p a b -> p (a b)"), in_=wrg_d[j]),
                       writes=[wrj_b[w3]], dma=wrj_b[w3])
                for t in range(5):
                    col = PF_WCONV + t * 16 + j
                    S.emit("dve", lambda e, t=t, col=col: e.tensor_scalar(
                        out=dg[s_][:, t, :], in0=cst[:, 0:128], scalar1=pf[:, col:col + 1], scalar2=None, op0=ALU.mult),
                        reads=[cst_b, pf_b], writes=[dg_b[s_]])
                if not halo_l:
                    S.emit("dve", lambda e: e.memset(xin[s_][:, 0:2], 0.0), writes=[xin_b[s_]])
                if not halo_r:
                    S.emit("dve", lambda e: e.memset(xin[s_][:, T + 2:T + 4], 0.0), writes=[xin_b[s_]])

            def a_pieces(j):
                s_ = j % 2
                out = []
                specs = [(c0 + g0, n, 2 + g0) for (g0, n) in groups]
                if halo_l:
                    specs.append((c0 - 2, 2, 0))
                if halo_r:
                    specs.append((c0 + T, 2, T + 2))
                for (a, n, o) in specs:
                    def pe(a=a, n=n):
                        pt, pt_b = psum()
                        for kc in range(KC):
                            S.emit("pe", lambda e, kc=kc: e.matmul(
                                pt[:, 0:n], lhsT=wl[s_][:, kc, :], rhs=hnT[:, kc, a:a + n], start=(kc == 0),
                                stop=(kc == KC - 1)), reads=[wl_b[s_]] + hn_bufs(a, a + n), writes=[pt_b])
                        return pt, pt_b

                    def ev(pt, pt_b, n=n, o=o):
                        S.emit("dve", lambda e: e.tensor_copy(out=xin[s_][:, o:o + n], in_=pt[:, 0:n]),
                               reads=[pt_b], writes=[xin_b[s_]])
                    out.append((pe, ev))
                return out

            def c_pieces(j):
                s_ = j % 2
                out = []
                for (g0, n) in groups:
                    def pe(g0=g0, n=n):
                        pt, pt_b = psum()
                        for t in range(5):
                            S.emit("pe", lambda e, t=t: e.matmul(
                                pt[:, 0:n], lhsT=dg[s_][:, t, :], rhs=xin[s_][:, g0 + t:g0 + t + n], start=(t == 0),
                                stop=(t == 4)), reads=[dg_b[s_], xin_b[s_]], writes=[pt_b])
                        return pt, pt_b

                    def ev(pt, pt_b, g0=g0, n=n):
                        S.emit("dve", lambda e: e.tensor_scalar(
                            out=xcb[s_][:, g0:g0 + n], in0=pt[:, 0:n], scalar1=pf[:, PF_BCONV + j:PF_BCONV + j + 1],
                            scalar2=None, op0=ALU.add), reads=[pt_b, pf_b], writes=[xcb_b[s_]])
                    out.append((pe, ev))
                return out

            def run_pieces(pieces):
                pend = [(ev, pe()) for (pe, ev) in pieces]
                return pend

            def run_evacs(pend):
                for ev, (pt, pt_b) in pend:
                    ev(pt, pt_b)

            def stage_g_half(u, hf):
                if hf >= len(groups2):
                    return
                j, d = units[u]
                s_ = j % 2
                i3 = u % NS
                g0, n2 = groups2[hf]
                for (which, dst, dst_b) in ((0, rr[i3], rr_b[i3]), (1, ig[i3], ig_b[i3])):
                    wi = 2 * d + which
                    bcol = wi * 16 + j
                    pt2, pt2_b = psum2()
                    for h0_ in range(0, n2, 512):
                        n = min(512, n2 - h0_)
                        S.emit("pe", lambda e, pt2=pt2, h0_=h0_, n=n, wi=wi: e.matmul(
                            pt2[:, h0_:h0_ + n], lhsT=wrj[j % 3][:, wi, :], rhs=xcb[s_][:, g0 + h0_:g0 + h0_ + n],
                            start=True, stop=True), reads=[wrj_b[j % 3], xcb_b[s_]], writes=[pt2_b[h0_ // 512]])
                    S.emit("act", lambda e, pt2=pt2, dst=dst, bcol=bcol: e.activation(
                        out=dst[:, g0:g0 + n2], in_=pt2[:, 0:n2], func=AF.Tanh, scale=0.5, bias=hbrg[:, bcol:bcol + 1]),
                        reads=pt2_b + [hbrg_b], writes=[dst_b])

            def stage_eq(u):
                j, d = units[u]
                s_ = j % 2
                i3 = u % NS
                hc = hcl[:, d * 16 + j:d * 16 + j + 1]
                c2 = cl[:, d * 16 + j:d * 16 + j + 1]
                S.emit("act", lambda e: e.activation(out=aa[i3][:], in_=rr[i3][:], func=AF.Exp, scale=hc, bias=hc),
                       reads=[rr_b[i3], hcl_b], writes=[aa_b[i3]])
                S.emit("act", lambda e: e.activation(out=rr[i3][:], in_=rr[i3][:], func=AF.Exp, scale=c2, bias=c2),
                       reads=[rr_b[i3], cl_b], writes=[rr_b[i3]])
                S.emit("dve", lambda e: e.scalar_tensor_tensor(
                    out=ig[i3][:], in0=ig[i3][:], scalar=1.0, in1=xcb[s_][:], op0=ALU.add, op1=ALU.mult),
                    reads=[ig_b[i3], xcb_b[s_]], writes=[ig_b[i3]])
                S.emit("act", lambda e: e.activation(out=rr[i3][:], in_=rr[i3][:], func=AF.Sqrt, scale=-0.25, bias=0.25),
                       reads=[rr_b[i3]], writes=[rr_b[i3]])
                S.emit("dve", lambda e: e.tensor_tensor(out=ig[i3][:], in0=ig[i3][:], in1=rr[i3][:], op=ALU.mult),
                       reads=[ig_b[i3], rr_b[i3]], writes=[ig_b[i3]])

            def stage_sc(u):
                j, d = units[u]
                i3 = u % NS
                i2 = u % 2
                h0 = hst[:, d * 16 + j:d * 16 + j + 1]
                if d == 0:
                    S.emit("dve", lambda e: e.tensor_tensor_scan(
                        out=hh[i2][:], data0=aa[i3][:], data1=ig[i3][:], initial=h0, op0=ALU.mult, op1=ALU.add),
                        reads=[aa_b[i3], ig_b[i3], hst_b[0][j]], writes=[hh_b[i2]])
                    if not full:
                        S.emit("dve", lambda e: e.tensor_copy(out=h0, in_=hh[i2][:, T - 1:T]),
                               reads=[hh_b[i2]], writes=[hst_b[0][j]])
                else:
                    S.emit("dve", lambda e: e.tensor_tensor_scan(
                        out=hh[i2][:, ::-1], data0=aa[i3][:, ::-1], data1=ig[i3][:, ::-1], initial=h0, op0=ALU.mult,
                        op1=ALU.add), reads=[aa_b[i3], ig_b[i3], hst_b[1][j]], writes=[hh_b[i2]])
                    if not full:
                        S.emit("dve", lambda e: e.tensor_copy(out=h0, in_=hh[i2][:, 0:1]),
                               reads=[hh_b[i2]], writes=[hst_b[1][j]])
                if full:
                    S.emit("sp", lambda e: e.dma_start(out=mlru_d[d, j], in_=hh[i2][:]), reads=[hh_b[i2]],
                           writes=[mlru_b[d][j]], dma=hh_b[i2])

            for j0 in (0, 1):
                a_setup(j0)
                run_evacs(run_pieces(a_pieces(j0)))
            run_evacs(run_pieces(c_pieces(0)))
            slot_pieces = {}
            for k in range(NU + 2):
                if 0 <= k - 2 < NU:
                    stage_sc(k - 2)
                if 0 <= k - 1 < NU:
                    stage_eq(k - 1)
                li, jb = k % nd, k // nd
                if li == 0:
                    cp = c_pieces(jb + 1) if jb + 1 < 16 else []
                    ap = a_pieces(jb + 2) if jb + 2 < 16 else []
                    if jb + 2 < 16:
                        a_setup(jb + 2)
                    for q in range(nd):
                        slot_pieces[q] = [cp[q::nd], ap[q::nd]]
                if k < NU:
                    stage_g_half(k, 0)
                if k < NU:
                    for batch in slot_pieces.get(li, []):
                        run_evacs(run_pieces(batch))
                    stage_g_half(k, 1)

        def merge_gate_stage(ms, silu_jobs):
            wm = [sb(ms, "wm%d" % i, [P, KC, P], BF16) for i in range(2)]
            wm_b = [Buf("wm%d" % i) for i in range(2)]
            sgb = [sb(ms, "sgb%d" % i, [P, T_OWN], BF16) for i in range(2)]
            sgb_b = [Buf("sgb%d" % i) for i in range(2)]
            yld = [sb(ms, "yld%d" % i, [P, T_OWN], BF16) for i in range(2)]
            yld_b = [Buf("yld%d" % i) for i in range(2)]
            yl2 = [sb(ms, "yl2%d" % i, [P, T_OWN], BF16) for i in range(2)]
            yl2_b = [Buf("yl2%d" % i) for i in range(2)]
            jobs = []
            add2 = {}
            for job in silu_jobs:
                off, dt_, db_ = job[0], job[1], job[2]
                for i in range(16):
                    jobs.append(("silu", off + i * P, None, dt_, db_, i))
                    if len(job) > 3:
                        add2[len(jobs) - 1] = (job[3], job[4])
            for fc in range(16):
                for which, off in ((0, MGO), (1, MLO)):
                    jobs.append(("sig", off + fc * P, PF_BM + which * 16 + fc, sg_d[which], sg_b[which], fc))
            for k, (kind, col, bcol, dt_, db_, i) in enumerate(jobs):
                w, w_b = wm[k % 2], wm_b[k % 2]
                o, o_b = sgb[k % 2], sgb_b[k % 2]
                S.emit("pool", lambda e, w=w, col=col: e.dma_start(out=w[:], in_=w_in_v[:, :, col:col + P]),
                       writes=[w_b], dma=w_b)
                if kind == "silu":
                    yl_, yl_b_ = yld[k % 2], yld_b[k % 2]
                    S.emit("sp", lambda e, yl_=yl_, dt_=dt_, i=i: e.dma_start(out=yl_[:], in_=dt_[i]),
                           reads=[db_[i]], writes=[yl_b_], dma=yl_b_)
                    if k in add2:
                        d2, d2b = add2[k]
                        y2_, y2_b_ = yl2[k % 2], yl2_b[k % 2]
                        S.emit("sp", lambda e, y2_=y2_, d2=d2, i=i: e.dma_start(out=y2_[:], in_=d2[i]),
                               reads=[d2b[i]], writes=[y2_b_], dma=y2_b_)
                        S.emit("dve", lambda e, yl_=yl_, y2_=y2_: e.tensor_tensor(out=yl_[:], in0=yl_[:], in1=y2_[:], op=ALU.add),
                               reads=[yl_b_, y2_b_], writes=[yl_b_])
                for g0 in range(0, T_OWN, 512):
                    pt, pt_b = psum()
                    for kc in range(KC):
                        S.emit("pe", lambda e, pt=pt, w=w, kc=kc, g0=g0: e.matmul(
                            pt[:, :], lhsT=w[:, kc, :], rhs=hnT[:, kc, g0:g0 + 512], start=(kc == 0),
                            stop=(kc == KC - 1)), reads=[w_b] + hn_bufs(g0, g0 + 512), writes=[pt_b])
                    if kind == "sig":
                        S.emit("act", lambda e, pt=pt, o=o, g0=g0, bcol=bcol: e.activation(
                            out=o[:, g0:g0 + 512], in_=pt[:, :], func=AF.Sigmoid, bias=pf[:, bcol:bcol + 1]),
                            reads=[pt_b, pf_b], writes=[o_b])
                    else:
                        S.emit("act", lambda e, pt=pt, o=o, g0=g0: e.activation(
                            out=o[:, g0:g0 + 512], in_=pt[:, :], func=AF.Silu), reads=[pt_b], writes=[o_b])
                if kind == "silu":
                    S.emit("dve", lambda e, o=o, yl_=yl_: e.tensor_tensor(out=o[:], in0=o[:], in1=yl_[:], op=ALU.mult),
                           reads=[o_b, yl_b_], writes=[o_b])
                S.emit("sp", lambda e, o=o, dt_=dt_, i=i: e.dma_start(out=dt_[i], in_=o[:]),
                       reads=[o_b], writes=[db_[i]], dma=o_b)

        S.mark("ctx_hn")
        build_hn(ctxl, 0, 2, 32, 0)
        build_lr(0, 256)
        S.mark("ctx_gla")
        with ExitStack() as st:
            gla_stage(st, 0, 2, (0, 1), False, True)
            S.barrier()
        S.mark("ctx_lru")
        with ExitStack() as st:
            lru_stage(st, 0, 256, (0, 1), False, False, False)
            S.barrier()
        S.mark("oth_hn")
        build_hn(xl, T_OWN - P, 17, 0, 0)
        build_lr(P, T_OWN)
        S.mark("oth_gla")
        with ExitStack() as st:
            gla_stage(st, P, 16, (1,), False, False)
            S.barrier()
        S.mark("oth_lru")
        with ExitStack() as st:
            lru_stage(st, P, T_OWN, (1,), False, True, False)
            S.barrier()
        S.mark("own_hn")
        build_hn(xl, 0, 17, 0, 0)
        build_lr(0, T_OWN)
        S.mark("own_gla")
        with ExitStack() as st:
            gla_stage(st, 0, 16, (0, 1), True, False)
            S.barrier()
        S.mark("own_lru")
        with ExitStack() as st:
            lru_stage(st, 0, T_OWN, (0, 1), True, False, True)
            S.barrier()
        S.mark("mgate")
        with ExitStack() as st:
            merge_gate_stage(st, [(GGO, mgla_d, mgla_b), (LGO, mlru_d[0], mlru_b[0], mlru_d[1], mlru_b[1])])
            S.barrier()
        S.mark("p6")

    with ExitStack() as ph:
        mres = sb(ph, "mres", [P, KC, T_OWN], BF16)
        mres_b = [Buf("mres%d" % i) for i in range(KC)]
        macc = sb(ph, "macc", [P, KC, T_OWN], BF16)
        macc_b = [Buf("macc%d" % i) for i in range(KC)]
        with ExitStack() as p6:
            wo = [sb(p6, "wo%d" % i, [P, KC, P], BF16) for i in range(2)]
            wo_b = [Buf("wo%d" % i) for i in range(2)]
            sgl = [sb(p6, "sgl%d" % i, [P, T_OWN], BF16) for i in range(2)]
            sgl_b = [Buf("sgl%d" % i) for i in range(2)]
            tmp = sb(p6, "tmp6", [P, 512])
            tmp_b = Buf("tmp6")
            for which, (md, mdb, wsrc) in enumerate(((mgla_d, mgla_b, w_o_gla), (mlru_d[0], mlru_b[0], w_o_rnn))):
                wv = wsrc.rearrange("(kc p) n -> p kc n", p=P)
                for ec in range(KC):
                    S.emit("sp", lambda e, md=md, ec=ec: e.dma_start(out=mres[:, ec, :], in_=md[ec]),
                           reads=[mdb[ec]], writes=[mres_b[ec]], dma=mres_b[ec])
                for fc in range(16):
                    w, w_b = wo[fc % 2], wo_b[fc % 2]
                    sgt_, sgt_b_ = sgl[fc % 2], sgl_b[fc % 2]
                    S.emit("pool", lambda e, w=w, wv=wv, fc=fc: e.dma_start(out=w[:], in_=wv[:, :, fc * P:(fc + 1) * P]),
                           writes=[w_b], dma=w_b)
                    S.emit("sp", lambda e, sgt_=sgt_, which=which, fc=fc: e.dma_start(out=sgt_[:], in_=sg_d[which, fc]),
                           reads=[sg_b[which][fc]], writes=[sgt_b_], dma=sgt_b_)
                    for g0 in range(0, T_OWN, 512):
                        pt, pt_b = psum()
                        for ec in range(KC):
                            S.emit("pe", lambda e, pt=pt, w=w, ec=ec, g0=g0: e.matmul(
                                pt[:, :], lhsT=w[:, ec, :], rhs=mres[:, ec, g0:g0 + 512], start=(ec == 0),
                                stop=(ec == KC - 1)), reads=[w_b, mres_b[ec]], writes=[pt_b])
                        if which == 0:
                            S.emit("dve", lambda e, pt=pt, sgt_=sgt_, fc=fc, g0=g0: e.tensor_tensor(
                                out=macc[:, fc, g0:g0 + 512], in0=pt[:, :], in1=sgt_[:, g0:g0 + 512], op=ALU.mult),
                                reads=[pt_b, sgt_b_], writes=[macc_b[fc]])
                        else:
                            S.emit("dve", lambda e, pt=pt, sgt_=sgt_, g0=g0: e.tensor_tensor(
                                out=tmp[:], in0=pt[:, :], in1=sgt_[:, g0:g0 + 512], op=ALU.mult),
                                reads=[pt_b, sgt_b_], writes=[tmp_b])
                            S.emit("dve", lambda e, fc=fc, g0=g0: e.tensor_tensor(
                                out=macc[:, fc, g0:g0 + 512], in0=macc[:, fc, g0:g0 + 512], in1=tmp[:], op=ALU.add),
                                reads=[tmp_b, macc_b[fc]], writes=[macc_b[fc]])
            S.barrier()
        S.mark("p7")
        with ExitStack() as p7:
            wov = w_out.rearrange("(kc p) n -> p kc n", p=P)
            for g in range(4):
                S.emit("pool", lambda e, g=g: e.dma_start(out=mres[:, :, g * 512:(g + 1) * 512],
                                                          in_=wov[:, :, g * 512:(g + 1) * 512]),
                       writes=mres_b, dma=mres_b[g])
            Gb = sb(p7, "Gb", [P, D])
            Gb_b = Buf("Gb")
            Gf = sb(p7, "Gf", [P, D])
            Gf_b = Buf("Gf")
            gfr = sb(p7, "gfr", [1, D])
            gfr_b = Buf("gfr")
            for (rsrc, rsrc_b, dst, dst_b) in ((grow_d, [growd_b], Gb, Gb_b), (gfin_d, [], Gf, Gf_b)):
                S.emit("sp", lambda e, rsrc=rsrc: e.dma_start(out=gfr[:], in_=rsrc), reads=rsrc_b, writes=[gfr_b],
                       dma=gfr_b)
                row, row_b = gfr, gfr_b
                for g in range(4):
                    pt, pt_b = psum()
                    S.emit("pe", lambda e, pt=pt, row=row, g=g: e.matmul(
                        pt[:, :], lhsT=ones[0:1, :], rhs=row[0:1, g * 512:(g + 1) * 512], start=True, stop=True),
                        reads=[ones_b, row_b], writes=[pt_b])
                    S.emit("act", lambda e, pt=pt, dst=dst, g=g: e.activation(out=dst[:, g * 512:(g + 1) * 512],
                                                                              in_=pt[:, :], func=AF.Copy),
                           reads=[pt_b], writes=[dst_b])
            xo = sb(p7, "xo", [P, D])
            xo_b = Buf("xo")
            rt = sb(p7, "rt", [P, D])
            rt_b = Buf("rt")
            for t in range(NT_OWN):
                S.emit("sp", lambda e, t=t: e.dma_start(out=xo[:], in_=xl[t * P:(t + 1) * P, :]), writes=[xo_b], dma=xo_b)
                for g in range(4):
                    pt, pt_b = psum()
                    for kc in range(KC):
                        S.emit("pe", lambda e, pt=pt, kc=kc, t=t, g=g: e.matmul(
                            pt[:, :], lhsT=macc[:, kc, t * P:(t + 1) * P], rhs=mres[:, kc, g * 512:(g + 1) * 512],
                            start=(kc == 0), stop=(kc == KC - 1)), reads=[macc_b[kc], mres_b[kc]], writes=[pt_b])
                    S.emit("dve", lambda e, pt=pt, g=g: e.tensor_tensor(
                        out=rt[:, g * 512:(g + 1) * 512], in0=pt[:, :], in1=Gb[:, g * 512:(g + 1) * 512], op=ALU.mult),
                        reads=[pt_b, Gb_b], writes=[rt_b])
                S.emit("dve", lambda e: e.tensor_tensor(out=rt[:], in0=rt[:], in1=xo[:], op=ALU.add),
                       reads=[rt_b, xo_b], writes=[rt_b])
                ssq, ssq_b = smallcol()
                S.emit("act", lambda e, ssq=ssq: e.activation(out=xo[:], in_=rt[:], func=AF.Square, accum_out=ssq),
                       reads=[rt_b], writes=[xo_b, ssq_b])
                r, r_b = smallcol()
                S.emit("dve", lambda e, r=r, ssq=ssq: e.tensor_scalar(out=r, in0=ssq, scalar1=1.0 / D, scalar2=EPS,
                                                                      op0=ALU.mult, op1=ALU.add),
                       reads=[ssq_b], writes=[r_b])
                S.emit("act", lambda e, r=r: e.activation(out=r, in_=r, func=AF.Sqrt), reads=[r_b], writes=[r_b])
                S.emit("dve", lambda e, r=r: e.reciprocal(out=r, in_=r), reads=[r_b], writes=[r_b])
                S.emit("dve", lambda e, r=r: e.scalar_tensor_tensor(out=rt[:], in0=rt[:], scalar=r, in1=Gf[:],
                                                                    op0=ALU.mult, op1=ALU.mult),
                       reads=[rt_b, r_b, Gf_b], writes=[rt_b])
                op = S.emit("sp", lambda e, t=t: e.dma_start(out=yl[t * P:(t + 1) * P, :], in_=rt[:]),
                            reads=[rt_b], writes=[yl_b], dma=rt_b)
                S.final_waits.append(op)
            S.barrier()

    S.mark("end")
    S.replay(es)
    es.close()
    nc._marks = S.marks
    return nc


def _fm(v):
    v = np.asarray(v, np.float32).reshape(-1, P)
    return np.ascontiguousarray(v.T)


_NC_CACHE = {}


def kernel(x, c, ctx, c_ctx, w_ada, b_ada, g_norm, w_in, w_gla_a, b_gla_a, g_gla_out, w_conv, b_conv,
           w_rg_a, b_rg_a, w_rg_x, b_rg_x, lam, w_o_gla, w_o_rnn, b_merge, w_out, g_final):
    f = lambda a: np.ascontiguousarray(np.asarray(a, np.float32))
    x, c, ctx, c_ctx = f(x), f(c), f(ctx), f(c_ctx)
    w_ada0, b_ada0, g_norm0, w_in0 = f(w_ada)[0], f(b_ada)[0], f(g_norm)[0], f(w_in)[0]
    w_gla_a0, b_gla_a0, g_gla_out0 = f(w_gla_a)[0], f(b_gla_a)[0], f(g_gla_out)[0]
    w_conv0, b_conv0 = f(w_conv)[0], f(b_conv)[0]
    w_rg_a0, b_rg_a0, w_rg_x0, b_rg_x0, lam0 = f(w_rg_a)[0], f(b_rg_a)[0], f(w_rg_x)[0], f(b_rg_x)[0], f(lam)[0]
    w_o_gla0, w_o_rnn0, b_merge0, w_out0, g_final0 = f(w_o_gla)[0], f(w_o_rnn)[0], f(b_merge)[0], f(w_out)[0], f(g_final)

    consts = np.zeros((P, 384), np.float32)
    consts[:, 0:128] = np.eye(P, dtype=np.float32)
    s_idx = np.arange(P)[:, None]
    c_idx = np.arange(P)[None, :]
    consts[:, 128:256] = (s_idx <= c_idx)
    consts[:, 256:384] = (s_idx >= c_idx)

    per_half = []
    for hf in range(2):
        dF, dB = (0, 1) if hf == 0 else (1, 0)
        w_la = np.zeros((D, 64), np.float32)
        lo = 6144
        w_la[:, 0:16] = w_in0[:, lo + 16 * dF: lo + 16 * dF + 16]
        w_la[:, 32:48] = w_in0[:, lo + 16 * dB: lo + 16 * dB + 16]
        wga = np.zeros((64, 1024), np.float32)
        wga[0:16] = w_gla_a0[dF]
        wga[16] = b_gla_a0[dF]
        wga[32:48] = w_gla_a0[dB]
        wga[48] = b_gla_a0[dB]
        wr = np.stack([w_rg_a0[dF], w_rg_x0[dF], w_rg_a0[dB], w_rg_x0[dB]], 0)
        wrg = np.ascontiguousarray(wr.transpose(1, 2, 0, 3)).reshape(16, P, 4 * P)
        taps = np.zeros((5, D), np.float32)
        if hf == 0:
            taps[0:4] = w_conv0
        else:
            taps[1:5] = w_conv0[::-1]
        pf = np.concatenate([
            _fm(b_ada0), _fm(g_norm0), _fm(b_conv0),
            np.concatenate([_fm(taps[t]) for t in range(5)], 1),
            _fm(b_rg_a0[dF]), _fm(b_rg_x0[dF]), _fm(b_rg_a0[dB]), _fm(b_rg_x0[dB]),
            _fm(lam0[dF]), _fm(lam0[dB]), _fm(b_merge0)], 1)
        assert pf.shape == (P, PF_N)
        per_half.append(dict(w_la=w_la, wga=wga, wrg=wrg, pf=np.ascontiguousarray(pf)))

    shared = dict(consts=consts, w_ada=w_ada0, bgate=np.ascontiguousarray(b_ada0[None, 2 * D:3 * D]), w_in=w_in0,
                  ggo=np.ascontiguousarray(g_gla_out0[None, :]), gfin=np.ascontiguousarray(g_final0[None, :]),
                  w_o_gla=w_o_gla0, w_o_rnn=w_o_rnn0, w_out=w_out0)
    in_maps = []
    for b in range(4):
        for hf in range(2):
            if hf == 0:
                xl_, ctxl_ = x[b], ctx[b]
            else:
                xl_, ctxl_ = np.ascontiguousarray(x[b][::-1]), np.ascontiguousarray(ctx[b][::-1])
            cc = np.stack([c[b], c_ctx], 0)
            ccT = np.ascontiguousarray(cc.reshape(2, KC, P).transpose(2, 1, 0)).reshape(P, 32)
            m = dict(shared)
            m.update(per_half[hf])
            m.update(xl=xl_, ctxl=ctxl_, ccT=ccT)
            in_maps.append(m)

    if "nc" not in _NC_CACHE:
        _NC_CACHE["nc"] = build_nc()
    nc = _NC_CACHE["nc"]
    res = run_bass_kernel_spmd(nc, in_maps, core_ids=list(range(8)))
    out = np.empty((4, 4096, D), np.float32)
    for b in range(4):
        for hf in range(2):
            y = np.asarray(res.results[b * 2 + hf]["yl"], np.float32)
            if hf == 0:
                out[b, 0:T_OWN] = y
            else:
                out[b, T_OWN:] = y[::-1]
    return out
```

```python
from contextlib import ExitStack

import numpy as np
import concourse.bass as bass
import concourse.mybir as mybir
from concourse.bass_utils import run_bass_kernel_spmd

F32 = mybir.dt.float32
BF16 = mybir.dt.bfloat16
AF = mybir.ActivationFunctionType
ALU = mybir.AluOpType

D = 2048
KC = 16
P = 128
D_IN = 14368
QO, KO, VO, GGO, LIO, LGO, MGO, MLO = 0, 1024, 2048, 4096, 6176, 8224, 10272, 12320
EPS = 1e-6
T_OWN = 2048
NT_OWN = 16
PF_BADA, PF_GN, PF_BCONV, PF_WCONV, PF_BRG, PF_LAM, PF_BM, PF_N = 0, 48, 64, 80, 160, 224, 256, 288


class Buf:
    __slots__ = ("name", "lw", "rd", "sem", "cnt", "excl")

    def __init__(self, name, excl=False):
        self.name = name
        self.excl = excl
        self.lw = None
        self.rd = {}
        self.sem = None
        self.cnt = 0


class Op:
    __slots__ = ("eng", "idx", "sig", "tick", "fn", "waits", "dma", "sembuf", "val")


class Sched:
    ENG = ("pe", "act", "dve", "pool", "sp")

    def __init__(self, nc):
        self.nc = nc
        self.ops = {e: [] for e in self.ENG}
        self.waited = {e: {} for e in self.ENG}
        self.dma_bufs = []
        self.final_waits = []
        self.marks = []

    def emit(self, eng, fn, reads=(), writes=(), dma=None):
        op = Op()
        op.eng = eng
        op.idx = len(self.ops[eng])
        op.sig = False
        op.tick = 0
        op.fn = fn
        op.dma = dma is not None
        op.sembuf = dma
        op.val = 0
        deps = []
        for b in reads:
            if b.lw is not None:
                deps.append(b.lw)
            if b.excl:
                deps.extend(o for k, o in b.rd.items() if k != ("e", eng))
        for b in writes:
            if b.lw is not None:
                deps.append(b.lw)
            deps.extend(b.rd.values())
        waits = []
        wd = self.waited[eng]
        for d in deps:
            if d.dma:
                key = ("d", id(d.sembuf))
                v = d.val
            else:
                if d.eng == "pe" and eng == "pe":
                    continue
                key = ("e", d.eng)
                v = d.idx
            if wd.get(key, -1) >= v:
                continue
            wd[key] = v
            waits.append(d)
            if not d.dma:
                d.sig = True
        op.waits = waits
        if dma is not None:
            if dma.cnt == 0:
                self.dma_bufs.append(dma)
            dma.cnt += 16
            op.val = dma.cnt
            rkey = ("d", id(dma))
        else:
            rkey = ("e", eng)
        for b in reads:
            b.rd[rkey] = op
        for b in writes:
            b.lw = op
            b.rd = {}
        self.ops[eng].append(op)
        return op

    def mark(self, label):
        self.marks.append((label, {e: len(self.ops[e]) for e in self.ENG}))

    def barrier(self):
        lasts = {e: (self.ops[e][-1] if self.ops[e] else None) for e in self.ENG}
        for e in self.ENG:
            for o in reversed(self.ops[e]):
                if o.fn is not None and not o.dma:
                    lasts[e] = o
                    break
            else:
                lasts[e] = None
        dmas = [(b, b.cnt) for b in self.dma_bufs]
        for f in self.ENG:
            op = Op()
            op.eng = f
            op.idx = len(self.ops[f])
            op.sig = False
            op.tick = 0
            op.fn = None
            op.dma = False
            op.sembuf = None
            op.val = 0
            waits = []
            wd = self.waited[f]
            for e in self.ENG:
                d = lasts[e]
                if d is None or e == f:
                    continue
                if wd.get(("e", e), -1) >= d.idx:
                    continue
                wd[("e", e)] = d.idx
                d.sig = True
                waits.append(d)
            for b, c in dmas:
                key = ("d", id(b))
                if wd.get(key, -1) >= c:
                    continue
                wd[key] = c
                fake = Op()
                fake.dma = True
                fake.sembuf = b
                fake.val = c
                waits.append(fake)
            op.waits = waits
            self.ops[f].append(op)

    def replay(self, es):
        nc = self.nc
        esem = {e: es.enter_context(nc.semaphore("es_" + e)) for e in self.ENG}
        for i, b in enumerate(self.dma_bufs):
            b.sem = es.enter_context(nc.semaphore("ds%d" % i))
        for e in self.ENG:
            c = 0
            for o in self.ops[e]:
                if o.sig and not o.dma:
                    c += 1
                    o.tick = c
        finals = list(self.final_waits)
        block = es.enter_context(nc.Block())

        def run(e, engh, extra=None):
            for o in self.ops[e]:
                for d in o.waits:
                    if d.dma:
                        engh.wait_ge(d.sembuf.sem, d.val)
                    else:
                        engh.wait_ge(esem[d.eng], d.tick)
                if o.fn is None:
                    continue
                ins = o.fn(engh)
                if o.dma:
                    ins.then_inc(o.sembuf.sem, 16)
                elif o.sig:
                    ins.then_inc(esem[e], 1)
            if extra:
                for d in extra:
                    engh.wait_ge(d.sembuf.sem, d.val)

        @block.tensor
        def _(t):
            run("pe", t)

        @block.scalar
        def _(t):
            run("act", t)

        @block.vector
        def _(t):
            run("dve", t)

        @block.gpsimd
        def _(t):
            run("pool", t)

        @block.sync
        def _(t):
            run("sp", t, finals)


def build_nc():
    nc = bass.Bass("TRN2", target_bir_lowering=False)
    S = Sched(nc)
    es = ExitStack()

    def din(name, shape):
        return nc.dram_tensor(name, list(shape), F32, kind="ExternalInput").ap()

    xl = din("xl", [4096, D])
    ctxl = din("ctxl", [256, D])
    ccT = din("ccT", [P, 32])
    pf_d = din("pf", [P, PF_N])
    consts_d = din("consts", [P, 384])
    w_ada = din("w_ada", [D, 3 * D])
    bgate_d = din("bgate", [1, D])
    w_in = din("w_in", [D, D_IN])
    w_la = din("w_la", [D, 64])
    wga_d = din("wga", [64, 1024])
    wrg_d = din("wrg", [16, P, 4 * P])
    ggo_d = din("ggo", [1, 512])
    gfin_d = din("gfin", [1, D])
    w_o_gla = din("w_o_gla", [D, D])
    w_o_rnn = din("w_o_rnn", [D, D])
    w_out = din("w_out", [D, D])
    yl = nc.dram_tensor("yl", [T_OWN, D], F32, kind="ExternalOutput").ap()

    ob_d = nc.dram_tensor("ob_d", [4, NT_OWN, P, 512], F32, kind="Internal").ap()
    mgla_d = nc.dram_tensor("mgla_d", [16, P, T_OWN], BF16, kind="Internal").ap()
    mlru_d = nc.dram_tensor("mlru_d", [2, 16, P, T_OWN], BF16, kind="Internal").ap()
    sg_d = nc.dram_tensor("sg_d", [2, 16, P, T_OWN], BF16, kind="Internal").ap()
    st_d = nc.dram_tensor("st_d", [2, 4, P, 1024], F32, kind="Internal").ap()
    ob_b = [[Buf("ob%d_%d" % (h, n)) for n in range(NT_OWN)] for h in range(4)]
    mgla_b = [Buf("mgla%d" % i) for i in range(16)]
    mlru_b = [[Buf("mlru%d_%d" % (d, i)) for i in range(16)] for d in range(2)]
    sg_b = [[Buf("sg%d_%d" % (w, i)) for i in range(16)] for w in range(2)]
    st_b = [[Buf("st%d_%d" % (d, h)) for h in range(4)] for d in range(2)]
    yl_b = Buf("yl")

    sb_n = [0]

    def sb(stack, name, shape, dt=F32):
        sb_n[0] += 1
        return stack.enter_context(nc.sbuf_tensor("s%d_%s" % (sb_n[0], name), list(shape), dt))

    w_in_v = w_in.rearrange("(kc p) n -> p kc n", p=P)
    w_la_v = w_la.rearrange("(kc p) n -> p kc n", p=P)
    w_ada_v = w_ada.rearrange("(kc p) n -> p kc n", p=P)

    ps_big = [es.enter_context(nc.psum_tensor("psbig%d" % i, [P, 2048], F32)) for i in range(2)]
    ps_t = [ps_big[i // 4][:, (i % 4) * 512:(i % 4 + 1) * 512] for i in range(8)]
    ps_b = [Buf("psb%d" % i, excl=True) for i in range(8)]
    ps_rr = [0]

    def psum():
        i = ps_rr[0] % 8
        ps_rr[0] += 1
        return ps_t[i], ps_b[i]

    def psum2():
        if ps_rr[0] % 2:
            ps_rr[0] += 1
        i = ps_rr[0] % 8
        ps_rr[0] += 2
        return ps_big[i // 4][:, (i % 4) * 512:(i % 4) * 512 + 1024], [ps_b[i], ps_b[i + 1]]

    pers = es
    cst = sb(pers, "cst", [P, 384])
    cst_b = Buf("cst")
    ident = cst[:, 0:128]
    pf = sb(pers, "pfm", [P, PF_N])
    pf_b = Buf("pf")
    tri = sb(pers, "tri", [P, 256], BF16)
    tri_b = Buf("tri")
    mod = sb(pers, "mod", [P, 96])
    mod_b = Buf("mod")
    Ax = sb(pers, "Ax", [P, 64])
    Ax_b = Buf("Ax")
    cl = sb(pers, "cl", [P, 64])
    cl_b = Buf("cl")
    hcl = sb(pers, "hcl", [P, 32])
    hcl_b = Buf("hcl")
    hbrg = sb(pers, "hbrg", [P, 64])
    hbrg_b = Buf("hbrg")
    hbc = sb(pers, "hbc", [P, 16])
    hbc_b = Buf("hbc")
    identb = sb(pers, "identb", [P, P], BF16)
    identb_b = Buf("identb")
    hst = sb(pers, "hst", [P, 32])
    hst_b = [[Buf("hst%d_%d" % (d, j)) for j in range(16)] for d in range(2)]
    grow_d = nc.dram_tensor("grow_d", [1, D], F32, kind="Internal").ap()
    growd_b = Buf("growd")
    ones = sb(pers, "ones", [P, 128])
    ones_b = Buf("ones")
    wga = sb(pers, "wga", [64, 1024], BF16)
    wga_b = Buf("wga")
    scT = sb(pers, "scT", [P, 32], BF16)
    scT_b = Buf("scT")
    small = sb(pers, "small", [P, 64])
    small_bufs = [Buf("small%d" % i) for i in range(64)]
    small_rr = [0]

    def smallcol():
        i = small_rr[0] % 64
        small_rr[0] += 1
        return small[:, i:i + 1], small_bufs[i]

    S.emit("sp", lambda e: e.dma_start(out=cst[:], in_=consts_d), writes=[cst_b], dma=cst_b)
    S.emit("sp", lambda e: e.dma_start(out=pf[:], in_=pf_d), writes=[pf_b], dma=pf_b)
    S.emit("pool", lambda e: e.dma_start(out=wga[:], in_=wga_d), writes=[wga_b], dma=wga_b)
    S.emit("dve", lambda e: e.tensor_scalar(out=tri[:], in0=cst[:, 128:384], scalar1=-1.0 / 16.0, scalar2=None,
                                            op0=ALU.mult), reads=[cst_b], writes=[tri_b])
    S.emit("dve", lambda e: e.memset(ones[:], 1.0), writes=[ones_b])
    S.emit("dve", lambda e: e.memset(hst[:], 0.0), writes=[b for r in hst_b for b in r])

    with ExitStack() as ph:
        ccs = sb(ph, "ccs", [P, 32])
        ccs_b = Buf("ccs")
        S.emit("sp", lambda e: e.dma_start(out=ccs[:], in_=ccT), writes=[ccs_b], dma=ccs_b)
        S.emit("act", lambda e: e.activation(out=scT[:], in_=ccs[:], func=AF.Silu), reads=[ccs_b], writes=[scT_b])
        scT3 = scT[:].rearrange("p (k r) -> p k r", r=2)
        wb = [sb(ph, "wadab%d" % i, [P, KC, 512], BF16) for i in range(2)]
        wb_b = [Buf("wadab%d" % i) for i in range(2)]
        brow = sb(ph, "brow", [1, D])
        brow_b = Buf("brow")
        grow = sb(ph, "grow", [1, D])
        grow_b = Buf("grow")
        S.emit("sp", lambda e: e.dma_start(out=brow[:], in_=bgate_d), writes=[brow_b], dma=brow_b)
        psA, psA_b = psum()
        for ng in range(12):
            w, w_b = wb[ng % 2], wb_b[ng % 2]
            S.emit("pool", lambda e, w=w, ng=ng: e.dma_start(out=w[:], in_=w_ada_v[:, :, ng * 512:(ng + 1) * 512]),
                   writes=[w_b], dma=w_b)
            for j in range(4):
                n = ng * 4 + j
                for kc in range(KC):
                    S.emit("pe", lambda e, w=w, j=j, kc=kc, n=n: e.matmul(
                        psA[:, 2 * n:2 * n + 2], lhsT=w[:, kc, j * 128:(j + 1) * 128], rhs=scT3[:, kc, :],
                        start=(kc == 0), stop=(kc == KC - 1)), reads=[w_b, scT_b], writes=[psA_b])
            if ng >= 8:
                psG, psG_b = psum()
                for kc in range(KC):
                    S.emit("pe", lambda e, w=w, kc=kc, psG=psG: e.matmul(
                        psG[0:1, :], lhsT=scT3[:, kc, 0:1], rhs=w[:, kc, :],
                        start=(kc == 0), stop=(kc == KC - 1)), reads=[w_b, scT_b], writes=[psG_b])
                c0 = (ng - 8) * 512
                S.emit("dve", lambda e, psG=psG, c0=c0: e.tensor_tensor(
                    out=grow[0:1, c0:c0 + 512], in0=psG[0:1, :], in1=brow[0:1, c0:c0 + 512], op=ALU.add),
                    reads=[psG_b, brow_b], writes=[grow_b])
        S.emit("sp", lambda e: e.dma_start(out=grow_d, in_=grow[:]), reads=[grow_b], writes=[growd_b], dma=grow_b)
        psA3 = psA[:, 0:96].rearrange("p (n r) -> p n r", r=2)
        mod3 = mod[:].rearrange("p (n r) -> p n r", r=2)
        for r in range(2):
            S.emit("dve", lambda e, r=r: e.tensor_tensor(out=mod3[:, :, r], in0=psA3[:, :, r],
                                                         in1=pf[:, PF_BADA:PF_BADA + 48], op=ALU.add),
                   reads=[psA_b, pf_b], writes=[mod_b])
        for r in range(2):
            S.emit("dve", lambda e, r=r: e.scalar_tensor_tensor(
                out=Ax[:, 32 * r:32 * r + 16], in0=mod3[:, 16:32, r], scalar=1.0, in1=pf[:, PF_GN:PF_GN + 16],
                op0=ALU.add, op1=ALU.mult), reads=[mod_b, pf_b], writes=[Ax_b])
            S.emit("dve", lambda e, r=r: e.tensor_copy(out=Ax[:, 32 * r + 16:32 * r + 32], in_=mod3[:, 0:16, r]),
                   reads=[mod_b], writes=[Ax_b])
        tmpl = sb(ph, "tmpl", [P, 32])
        tmpl_b = Buf("tmpl")
        S.emit("act", lambda e: e.activation(out=tmpl[:], in_=pf[:, PF_LAM:PF_LAM + 32], func=AF.Exp, scale=-1.0),
               reads=[pf_b], writes=[tmpl_b])
        S.emit("act", lambda e: e.activation(out=tmpl[:], in_=tmpl[:], func=AF.Ln, bias=1.0),
               reads=[tmpl_b], writes=[tmpl_b])
        S.emit("dve", lambda e: e.tensor_scalar(out=cl[:, 0:32], in0=tmpl[:], scalar1=-8.0, scalar2=None, op0=ALU.mult),
               reads=[tmpl_b], writes=[cl_b])
        S.emit("dve", lambda e: e.tensor_scalar(out=cl[:, 32:64], in0=tmpl[:], scalar1=-16.0, scalar2=None, op0=ALU.mult),
               reads=[tmpl_b], writes=[cl_b])
        S.emit("dve", lambda e: e.tensor_scalar(out=hcl[:], in0=tmpl[:], scalar1=-4.0, scalar2=None, op0=ALU.mult),
               reads=[tmpl_b], writes=[hcl_b])
        S.emit("dve", lambda e: e.tensor_scalar(out=hbrg[:], in0=pf[:, PF_BRG:PF_BRG + 64], scalar1=0.5, scalar2=None,
                                                op0=ALU.mult), reads=[pf_b], writes=[hbrg_b])
        S.emit("dve", lambda e: e.tensor_scalar(out=hbc[:], in0=pf[:, PF_BCONV:PF_BCONV + 16], scalar1=0.5, scalar2=None,
                                                op0=ALU.mult), reads=[pf_b], writes=[hbc_b])
        S.emit("dve", lambda e: e.tensor_copy(out=identb[:], in_=cst[:, 0:128]), reads=[cst_b], writes=[identb_b])
        S.barrier()

    with ExitStack() as ph:
        NTH = 17
        hnT = sb(ph, "hnT", [P, KC, NTH * P], BF16)
        hn_b = [[Buf("hn%d_%d" % (i, g)) for g in range(4)] for i in range(NTH)]
        lra = sb(ph, "lra", [64, NTH * P], BF16)
        lra_b = Buf("lra")
        wla = sb(ph, "wla", [P, KC, 64], BF16)
        wla_b = Buf("wla")
        S.emit("pool", lambda e: e.dma_start(out=wla[:], in_=w_la_v), writes=[wla_b], dma=wla_b)
        xt_rr = [0]

        def rstd_from_ssq(ssq, ssq_b, n):
            r, r_b = smallcol()
            S.emit("dve", lambda e: e.tensor_scalar(out=r, in0=ssq, scalar1=1.0 / n, scalar2=EPS, op0=ALU.mult,
                                                    op1=ALU.add), reads=[ssq_b], writes=[r_b])
            S.emit("act", lambda e: e.activation(out=r, in_=r, func=AF.Sqrt), reads=[r_b], writes=[r_b])
            S.emit("dve", lambda e: e.reciprocal(out=r, in_=r), reads=[r_b], writes=[r_b])
            return r, r_b

        def build_hn(src, row0, ntiles, modoff, tile0):
            with ExitStack() as bs:
                build_hn_(bs, src, row0, ntiles, modoff, tile0)
                S.barrier()

        def build_hn_(bs, src, row0, ntiles, modoff, tile0):
            xt = [sb(bs, "xt%d" % i, [P, D]) for i in range(2)]
            xt_b = [Buf("xt%d" % i) for i in range(2)]
            xns = [sb(bs, "xn%d" % i, [P, D]) for i in range(2)]
            xns_b = [Buf("xn%d" % i) for i in range(2)]
            junk = sb(bs, "junk", [P, D], BF16)
            junk_b = Buf("junk")
            def stats(t):
                xi = xt_rr[0] % 2
                xt_rr[0] += 1
                x_, x_b = xt[xi], xt_b[xi]
                r0 = row0 + t * P
                S.emit("sp", lambda e: e.dma_start(out=x_[:], in_=src[r0:r0 + P, :]), writes=[x_b], dma=x_b)
                ssq, ssq_b = smallcol()
                S.emit("act", lambda e: e.activation(out=junk[:], in_=x_[:], func=AF.Square, accum_out=ssq),
                       reads=[x_b], writes=[junk_b, ssq_b])
                r, r_b = rstd_from_ssq(ssq, ssq_b, D)
                xn, xn_b = xns[t % 2], xns_b[t % 2]
                S.emit("dve", lambda e: e.tensor_scalar(out=xn[:], in0=x_[:], scalar1=r, scalar2=None, op0=ALU.mult),
                       reads=[x_b, r_b], writes=[xn_b])

            def trans(t):
                xn, xn_b = xns[t % 2], xns_b[t % 2]
                col = (tile0 + t) * P
                for g in range(4):
                    hb = hn_b[tile0 + t][g]
                    pt, pt_b = psum()
                    for j in range(4):
                        kc = 4 * g + j
                        S.emit("pe", lambda e, pt=pt, j=j, kc=kc: e.transpose(
                            out=pt[:, j * P:(j + 1) * P], in_=xn[:, kc * P:(kc + 1) * P], identity=ident),
                            reads=[xn_b, cst_b], writes=[pt_b])
                    for j in range(4):
                        kc = 4 * g + j
                        a_ap = Ax[:, modoff + kc:modoff + kc + 1]
                        s_ap = Ax[:, modoff + 16 + kc:modoff + 16 + kc + 1]
                        if g % 2 == 0:
                            S.emit("dve", lambda e, pt=pt, j=j, kc=kc, a_ap=a_ap, s_ap=s_ap: e.tensor_scalar(
                                out=hnT[:, kc, col:col + P], in0=pt[:, j * P:(j + 1) * P], scalar1=a_ap, scalar2=s_ap,
                                op0=ALU.mult, op1=ALU.add), reads=[pt_b, Ax_b], writes=[hb])
                        else:
                            S.emit("act", lambda e, pt=pt, j=j, kc=kc, a_ap=a_ap, s_ap=s_ap: e.activation(
                                out=hnT[:, kc, col:col + P], in_=pt[:, j * P:(j + 1) * P], func=AF.Identity,
                                scale=a_ap, bias=s_ap), reads=[pt_b, Ax_b], writes=[hb])

            stats(0)
            for t in range(ntiles):
                if t + 1 < ntiles:
                    stats(t + 1)
                trans(t)

        def hn_bufs(c0, c1):
            return [b for tb in hn_b[c0 // P:(c1 + P - 1) // P] for b in tb]

        def build_lr(c0, T):
            S.emit("dve", lambda e: e.memset(lra[:, c0:c0 + T], 1.0), writes=[lra_b])
            for g0 in range(0, T, 512):
                n = min(512, T - g0)
                pt, pt_b = psum()
                a = c0 + g0
                for kc in range(KC):
                    S.emit("pe", lambda e, pt=pt, kc=kc, a=a, n=n: e.matmul(
                        pt[0:64, 0:n], lhsT=wla[:, kc, :], rhs=hnT[:, kc, a:a + n], start=(kc == 0),
                        stop=(kc == KC - 1)), reads=[wla_b] + hn_bufs(a, a + n), writes=[pt_b])
                S.emit("act", lambda e, pt=pt, a=a, n=n: e.activation(out=lra[0:16, a:a + n], in_=pt[0:16, 0:n],
                                                                      func=AF.Copy), reads=[pt_b], writes=[lra_b])
                S.emit("act", lambda e, pt=pt, a=a, n=n: e.activation(out=lra[32:48, a:a + n], in_=pt[32:48, 0:n],
                                                                      func=AF.Copy), reads=[pt_b], writes=[lra_b])

        def gla_stage(gs, c0, nch, dirs, full, zero_state):
            WA = sb(gs, "WA", [P, KC, 512], BF16)
            WA_b = Buf("WA")
            WB = sb(gs, "WB", [P, KC, 512], BF16)
            WB_b = Buf("WB")
            T = nch * P
            kfm = sb(gs, "kfm", [P, 2, T], BF16)
            kfm_b = Buf("kfm")
            vtm = sb(gs, "vtm", [P, nch, 512], BF16)
            vtm_b = [Buf("vtm%d" % i) for i in range(nch)]
            def mk(name, shape, dt=F32, n=2):
                return ([sb(gs, "%s%d" % (name, i), shape, dt) for i in range(n)],
                        [Buf("%s%d" % (name, i)) for i in range(n)])

            if full:
                qfm = sb(gs, "qfm", [P, 2, T], BF16)
                qfm_b = Buf("qfm")
                ggb = sb(gs, "ggb", [P, 512])
                ggb_b = Buf("ggb")
                ggr = sb(gs, "ggr", [1, 512])
                ggr_b = Buf("ggr")
                S.emit("sp", lambda e: e.dma_start(out=ggr[:], in_=ggo_d), writes=[ggr_b], dma=ggr_b)
                pt, pt_b = psum()
                S.emit("pe", lambda e: e.matmul(pt[:, :], lhsT=ones[0:1, :], rhs=ggr[0:1, :], start=True, stop=True),
                       reads=[ones_b, ggr_b], writes=[pt_b])
                S.emit("act", lambda e: e.activation(out=ggb[:], in_=pt[:, :], func=AF.Copy), reads=[pt_b],
                       writes=[ggb_b])
                obt, obt_b = mk("obt", [P, 512])
                osum, osum_b = mk("osum", [P, 512], n=3)
                jk = sb(gs, "jk", [P, 512], BF16)
                jk_b = Buf("jk")
                mT, mT_b = mk("mT", [P, 4, P], BF16)
                E1, E1_b = mk("E1", [P, 256], n=4)
                qe, qe_b = mk("qe", [P, 2, P], BF16, n=4)
                ke, ke_b = mk("ke", [P, 2, P], BF16)
                sc, sc_b = mk("sc", [P, P], BF16)
                rs_, rs_b = mk("rs", [P, 1], n=4)
                Sbf = [[sb(gs, "Sbf%d_%d" % (d, pp), [P, 2, 512], BF16) for pp in range(2)] for d in range(2)]
                Sbf_b = [[[Buf("Sbf%d_%d_%d" % (d, pp, dc)) for dc in range(2)] for pp in range(2)] for d in range(2)]
            else:
                off = sb(gs, "off", [P, 2])
                off_b = Buf("off")
                Sacc = sb(gs, "Sacc", [P, 2, 512])
                Sacc_b = Buf("Sacc")
            Sst = [sb(gs, "Sst%d" % d, [P, 2, 512]) for d in range(2)]
            Sst_b = [Buf("Sst%d" % d) for d in range(2)]
            ez, ez_b = mk("ez", [P, 256])
            nl, nl_b = mk("nl", [P, 256], BF16)
            E2, E2_b = mk("E2", [P, 256])
            dcy, dcy_b = mk("dcy", [P, 2], n=4)
            kd, kd_b = mk("kd", [P, 2, P])
            kdt, kdt_b = mk("kdt", [P, 256], BF16)
            PZ, PC, PK, PV0, PV1, PO0, PO1, PM = range(8)

            def proj_fm(dst, dst_b, dcg, wcol, scale):
                for g0 in range(0, T, 512):
                    n = min(512, T - g0)
                    pt, pt_b = psum()
                    a = c0 + g0
                    for kc in range(KC):
                        S.emit("pe", lambda e, pt=pt, kc=kc, a=a, n=n: e.matmul(
                            pt[:, 0:n], lhsT=WA[:, kc, wcol:wcol + P], rhs=hnT[:, kc, a:a + n], start=(kc == 0),
                            stop=(kc == KC - 1)), reads=[WA_b] + hn_bufs(a, a + n), writes=[pt_b])
                    S.emit("act", lambda e, pt=pt, g0=g0, n=n: e.activation(
                        out=dst[:, dcg, g0:g0 + n], in_=pt[:, 0:n], func=AF.Copy, scale=scale),
                        reads=[pt_b], writes=[dst_b])

            for h in range(4):
                if full:
                    S.emit("pool", lambda e, h=h: e.dma_start(out=WA[:, :, 0:256],
                                                              in_=w_in_v[:, :, QO + h * 256:QO + (h + 1) * 256]),
                           writes=[WA_b], dma=WA_b)
                S.emit("pool", lambda e, h=h: e.dma_start(out=WA[:, :, 256:512],
                                                          in_=w_in_v[:, :, KO + h * 256:KO + (h + 1) * 256]),
                       writes=[WA_b], dma=WA_b)
                S.emit("pool", lambda e, h=h: e.dma_start(out=WB[:], in_=w_in_v[:, :, VO + h * 512:VO + (h + 1) * 512]),
                       writes=[WB_b], dma=WB_b)
                if full:
                    for dc in range(2):
                        proj_fm(qfm, qfm_b, dc, dc * P, 1.0)
                for dc in range(2):
                    proj_fm(kfm, kfm_b, dc, 256 + dc * P, 1.0 / 16.0)
                for n in range(nch):
                    pt, pt_b = psum()
                    a = c0 + n * P
                    for kc in range(KC):
                        S.emit("pe", lambda e, pt=pt, kc=kc, a=a: e.matmul(
                            pt[:, :], lhsT=hnT[:, kc, a:a + P], rhs=WB[:, kc, :], start=(kc == 0),
                            stop=(kc == KC - 1)), reads=[WB_b] + hn_bufs(a, a + P), writes=[pt_b])
                    if n % 2 == 0:
                        S.emit("act", lambda e, pt=pt, n=n: e.activation(out=vtm[:, n, :], in_=pt[:, :], func=AF.Copy),
                               reads=[pt_b], writes=[vtm_b[n]])
                    else:
                        S.emit("dve", lambda e, pt=pt, n=n: e.tensor_copy(out=vtm[:, n, :], in_=pt[:, :]),
                               reads=[pt_b], writes=[vtm_b[n]])
                for d in dirs:
                    if zero_state:
                        S.emit("dve", lambda e, d=d: e.memset(Sst[d][:], 0.0), writes=[Sst_b[d]])
                    else:
                        S.emit("sp", lambda e, d=d, h=h: e.dma_start(
                            out=Sst[d][:].rearrange("p a b -> p (a b)"), in_=st_d[d, h]),
                            reads=[st_b[d][h]], writes=[Sst_b[d]], dma=Sst_b[d])
                    if full:
                        S.emit("act", lambda e, d=d: e.activation(out=Sbf[d][1][:], in_=Sst[d][:], func=AF.Copy),
                               reads=[Sst_b[d]], writes=Sbf_b[d][1])

                def pipeline(h, seq):
                    NPOS = len(seq)

                    def A1(p):
                        d, n, q = seq[p]
                        a = c0 + n * P
                        base = 32 * d
                        i = p % 2
                        S.emit("pe", lambda e: e.matmul(
                            ps_t[PZ][:, 0:256], lhsT=lra[base:base + 17, a:a + P],
                            rhs=wga[base:base + 17, h * 256:(h + 1) * 256], start=True, stop=True),
                            reads=[lra_b, wga_b], writes=[ps_b[PZ]])
                        S.emit("act", lambda e: e.activation(out=ez[i][:], in_=ps_t[PZ][:, 0:256], func=AF.Exp, scale=-1.0),
                               reads=[ps_b[PZ]], writes=[ez_b[i]])
                        S.emit("act", lambda e: e.activation(out=nl[i][:], in_=ez[i][:], func=AF.Ln, bias=1.0),
                               reads=[ez_b[i]], writes=[nl_b[i]])

                    def A2(p):
                        d, n, q = seq[p]
                        i = p % 2
                        i4 = p % 4
                        TR = tri[:, d * P:(d + 1) * P]
                        pc = ps_t[PC]
                        for dc in range(2):
                            S.emit("pe", lambda e, dc=dc: e.matmul(
                                pc[:, dc * P:(dc + 1) * P], lhsT=nl[i][:, dc * P:(dc + 1) * P], rhs=TR, start=True, stop=True),
                                reads=[nl_b[i], tri_b], writes=[ps_b[PC]])
                        S.emit("act", lambda e: e.activation(out=E2[i][:], in_=pc[:, 0:256], func=AF.Exp, scale=-1.0),
                               reads=[ps_b[PC]], writes=[E2_b[i]])
                        lastcol = (P - 1) if d == 0 else 0
                        pc3 = pc[:, 0:256].rearrange("p (a b) -> p a b", b=P)
                        if full:
                            S.emit("act", lambda e: e.activation(out=E1[i4][:], in_=pc[:, 0:256], func=AF.Exp),
                                   reads=[ps_b[PC]], writes=[E1_b[i4]])
                        else:
                            for dc in range(2):
                                col = dc * P + lastcol
                                S.emit("act", lambda e, dc=dc, col=col: e.activation(
                                    out=dcy[i4][:, dc:dc + 1], in_=pc[:, col:col + 1], func=AF.Exp, bias=off[:, dc:dc + 1]),
                                    reads=[ps_b[PC], off_b], writes=[dcy_b[i4]])
                            S.emit("act", lambda e: e.activation(out=ez[i][:, 0:2], in_=pc3[:, :, lastcol], func=AF.Copy),
                                   reads=[ps_b[PC]], writes=[ez_b[i]])
                            S.emit("dve", lambda e: e.tensor_tensor(out=off[:], in0=off[:], in1=ez[i][:, 0:2], op=ALU.add),
                                   reads=[off_b, ez_b[i]], writes=[off_b])

                    def A2b(p):
                        d, n, q = seq[p]
                        i = p % 2
                        i4 = p % 4
                        tk = slice(n * P, (n + 1) * P)
                        if full:
                            S.emit("dve", lambda e: e.tensor_tensor(
                                out=qe[i4][:], in0=qfm[:, :, tk], in1=E1[i4][:].rearrange("p (a b) -> p a b", b=P), op=ALU.mult),
                                reads=[qfm_b, E1_b[i4]], writes=[qe_b[i4]])
                            S.emit("dve", lambda e: e.tensor_tensor(
                                out=ke[i][:], in0=kfm[:, :, tk], in1=E2[i][:].rearrange("p (a b) -> p a b", b=P), op=ALU.mult),
                                reads=[kfm_b, E2_b[i]], writes=[ke_b[i]])
                        lastcol_ = (P - 1) if d == 0 else 0
                        for dc in range(2):
                            if full:
                                dsc, dsc_b = E1[i4][:, dc * P + lastcol_:dc * P + lastcol_ + 1], E1_b[i4]
                            else:
                                dsc, dsc_b = dcy[i4][:, dc:dc + 1], dcy_b[i4]
                            S.emit("dve", lambda e, dc=dc, dsc=dsc: e.scalar_tensor_tensor(
                                out=kd[i][:, dc, :], in0=E2[i][:, dc * P:(dc + 1) * P], scalar=dsc,
                                in1=kfm[:, dc, tk], op0=ALU.mult, op1=ALU.mult),
                                reads=[E2_b[i], dsc_b, kfm_b], writes=[kd_b[i]])

                    def A3(p):
                        d, n, q = seq[p]
                        i = p % 2
                        i4 = p % 4
                        MK = cst[:, 128 + d * P:128 + (d + 1) * P]
                        pk = ps_t[PK]
                        for dc in range(2):
                            S.emit("pe", lambda e, dc=dc: e.transpose(
                                out=pk[:, dc * P:(dc + 1) * P], in_=kd[i][:, dc, :], identity=ident),
                                reads=[kd_b[i], cst_b], writes=[ps_b[PK]])
                        if full:
                            for dc in range(2):
                                S.emit("pe", lambda e, dc=dc: e.matmul(
                                    pk[:, 256:384], lhsT=ke[i][:, dc, :], rhs=qe[i4][:, dc, :], start=(dc == 0), stop=(dc == 1)),
                                    reads=[ke_b[i], qe_b[i4]], writes=[ps_b[PK]])
                        S.emit("dve", lambda e: e.tensor_copy(out=kdt[i][:], in_=pk[:, 0:256]),
                               reads=[ps_b[PK]], writes=[kdt_b[i]])
                        if full:
                            S.emit("dve", lambda e: e.tensor_tensor(out=sc[i][:], in0=pk[:, 256:384], in1=MK, op=ALU.mult),
                                   reads=[ps_b[PK], cst_b], writes=[sc_b[i]])

                    def B1(p):
                        d, n, q = seq[p]
                        i = p % 2
                        i4 = p % 4
                        for dc, bank in ((0, PV0), (1, PV1)):
                            S.emit("pe", lambda e, dc=dc, bank=bank: e.matmul(
                                ps_t[bank][:, :], lhsT=kdt[i][:, dc * P:(dc + 1) * P], rhs=vtm[:, n, :], start=True, stop=True),
                                reads=[kdt_b[i], vtm_b[n]], writes=[ps_b[bank]])
                        if not full:
                            for dc, bank in ((0, PV0), (1, PV1)):
                                S.emit("dve", lambda e, dc=dc, bank=bank: e.tensor_tensor(
                                    out=Sacc[:, dc, :], in0=Sacc[:, dc, :], in1=ps_t[bank][:, :], op=ALU.add),
                                    reads=[Sacc_b, ps_b[bank]], writes=[Sacc_b])
                            return
                        pp, pq = q % 2, (q - 1) % 2
                        po, po_b = ps_t[PO0 + i], ps_b[PO0 + i]
                        S.emit("pe", lambda e: e.matmul(po[:, :], lhsT=sc[i][:], rhs=vtm[:, n, :], start=True, stop=False),
                               reads=[sc_b[i], vtm_b[n]], writes=[po_b])
                        for dc in range(2):
                            S.emit("pe", lambda e, dc=dc: e.matmul(
                                po[:, :], lhsT=qe[i4][:, dc, :], rhs=Sbf[d][pq][:, dc, :], start=False, stop=(dc == 1)),
                                reads=[qe_b[i4], Sbf_b[d][pq][dc]], writes=[po_b])
                        lastcol_ = (P - 1) if d == 0 else 0
                        for dc, bank in ((0, PV0), (1, PV1)):
                            dsc = E1[i4][:, dc * P + lastcol_:dc * P + lastcol_ + 1]
                            S.emit("dve", lambda e, dc=dc, bank=bank, dsc=dsc: e.scalar_tensor_tensor(
                                out=Sst[d][:, dc, :], in0=Sst[d][:, dc, :], scalar=dsc, in1=ps_t[bank][:, :],
                                op0=ALU.mult, op1=ALU.add), reads=[Sst_b[d], E1_b[i4], ps_b[bank]], writes=[Sst_b[d]])
                        S.emit("act", lambda e: e.activation(out=Sbf[d][pp][:], in_=Sst[d][:], func=AF.Copy),
                               reads=[Sst_b[d]], writes=Sbf_b[d][pp])

                    def B2(p):
                        d, n, q = seq[p]
                        i = p % 2
                        i3 = p % 3
                        i4 = p % 4
                        po, po_b = ps_t[PO0 + i], ps_b[PO0 + i]
                        if d == 1:
                            S.emit("act", lambda e: e.activation(out=obt[i][:], in_=po[:, :], func=AF.Copy),
                                   reads=[po_b], writes=[obt_b[i]])
                            S.emit("sp", lambda e: e.dma_start(out=ob_d[h, n], in_=obt[i][:]),
                                   reads=[obt_b[i]], writes=[ob_b[h][n]], dma=obt_b[i])
                            return
                        S.emit("sp", lambda e: e.dma_start(out=obt[i][:], in_=ob_d[h, n]),
                               reads=[ob_b[h][n]], writes=[obt_b[i]], dma=obt_b[i])
                        S.emit("dve", lambda e: e.tensor_tensor(out=osum[i3][:], in0=po[:, :], in1=obt[i][:], op=ALU.add),
                               reads=[po_b, obt_b[i]], writes=[osum_b[i3]])
                        r_ = rs_[i4][:, 0:1]
                        S.emit("act", lambda e: e.activation(out=jk[:], in_=osum[i3][:], func=AF.Square, accum_out=r_),
                               reads=[osum_b[i3]], writes=[jk_b, rs_b[i4]])
                        S.emit("act", lambda e: e.activation(out=r_, in_=r_, func=AF.Ln, scale=1.0 / 512.0, bias=EPS),
                               reads=[rs_b[i4]], writes=[rs_b[i4]])
                        S.emit("act", lambda e: e.activation(out=r_, in_=r_, func=AF.Exp, scale=-0.5),
                               reads=[rs_b[i4]], writes=[rs_b[i4]])

                    def B2b(p):
                        d, n, q = seq[p]
                        if d == 1:
                            return
                        i3 = p % 3
                        i4 = p % 4
                        S.emit("dve", lambda e: e.scalar_tensor_tensor(
                            out=osum[i3][:], in0=osum[i3][:], scalar=rs_[i4][:, 0:1], in1=ggb[:], op0=ALU.mult, op1=ALU.mult),
                            reads=[osum_b[i3], rs_b[i4], ggb_b], writes=[osum_b[i3]])

                    def B3(p):
                        d, n, q = seq[p]
                        if d == 1:
                            return
                        i = p % 2
                        i3 = p % 3
                        tk = slice(n * P, (n + 1) * P)
                        pm = ps_t[PM]
                        for j in range(4):
                            S.emit("pe", lambda e, j=j: e.transpose(
                                out=pm[:, j * P:(j + 1) * P], in_=osum[i3][:, j * P:(j + 1) * P], identity=ident),
                                reads=[osum_b[i3], cst_b], writes=[ps_b[PM]])
                        S.emit("act", lambda e: e.activation(
                            out=mT[i][:].rearrange("p a b -> p (a b)"), in_=pm[:, :], func=AF.Copy),
                            reads=[ps_b[PM]], writes=[mT_b[i]])
                        S.emit("sp", lambda e: e.dma_start(
                            out=mgla_d[h * 4:(h + 1) * 4, :, tk].rearrange("j p t -> p j t"), in_=mT[i][:]),
                            reads=[mT_b[i]], writes=[mgla_b[h * 4 + j] for j in range(4)], dma=mT_b[i])

                    if full:
                        stages = ((B1, 4), (A3, 3), (A2b, 2), (A2, 1), (A1, 0), (B2, 5), (B2b, 6), (B3, 7))
                    else:
                        stages = ((B1, 4), (A3, 3), (A2b, 2), (A2, 1), (A1, 0))
                    maxd = max(dl for _, dl in stages)
                    for k in range(NPOS + maxd):
                        for fn, dl in stages:
                            p = k - dl
                            if 0 <= p < NPOS:
                                fn(p)

                if full:
                    seq = [(1, n, q) for q, n in enumerate(range(nch - 1, -1, -1))] + \
                          [(0, n, q) for q, n in enumerate(range(nch))]
                    pipeline(h, seq)
                else:
                    for d in dirs:
                        S.emit("dve", lambda e: e.memset(off[:], 0.0), writes=[off_b])
                        S.emit("dve", lambda e: e.memset(Sacc[:], 0.0), writes=[Sacc_b])
                        rev = range(nch - 1, -1, -1) if d == 0 else range(nch)
                        pipeline(h, [(d, n, q) for q, n in enumerate(rev)])
                        tot, tot_b = dcy[0], dcy_b[0]
                        S.emit("act", lambda e, tot=tot: e.activation(out=tot[:], in_=off[:], func=AF.Exp),
                               reads=[off_b], writes=[tot_b])
                        for dc in range(2):
                            S.emit("dve", lambda e, d=d, dc=dc, tot=tot: e.scalar_tensor_tensor(
                                out=Sst[d][:, dc, :], in0=Sst[d][:, dc, :], scalar=tot[:, dc:dc + 1], in1=Sacc[:, dc, :],
                                op0=ALU.mult, op1=ALU.add), reads=[Sst_b[d], tot_b, Sacc_b], writes=[Sst_b[d]])
                        S.emit("sp", lambda e, d=d, h=h: e.dma_start(
                            out=st_d[d, h], in_=Sst[d][:].rearrange("p a b -> p (a b)")),
                            reads=[Sst_b[d]], writes=[st_b[d][h]], dma=Sst_b[d])

        def lru_stage(ls, c0, T, dirs, full, halo_l, halo_r):
            nd = len(dirs)
            wl = [sb(ls, "wl%d" % i, [P, KC, P], BF16) for i in range(2)]
            wl_b = [Buf("wl%d" % i) for i in range(2)]
            wrj = [sb(ls, "wrj%d" % i, [P, 4, P], BF16) for i in range(3)]
            wrj_b = [Buf("wrj%d" % i) for i in range(3)]
            dg = [sb(ls, "dg%d" % i, [P, 5, P], BF16) for i in range(2)]
            dg_b = [Buf("dg%d" % i) for i in range(2)]
            xin = [sb(ls, "xin%d" % i, [P, T + 4], BF16) for i in range(2)]
            xin_b = [Buf("xin%d" % i) for i in range(2)]
            xcb = [sb(ls, "xcb%d" % i, [P, T], BF16) for i in range(2)]
            xcb_b = [Buf("xcb%d" % i) for i in range(2)]
            NS = 3
            rr = [sb(ls, "rr%d" % i, [P, T]) for i in range(NS)]
            rr_b = [Buf("rr%d" % i) for i in range(NS)]
            ig = [sb(ls, "ig%d" % i, [P, T]) for i in range(NS)]
            ig_b = [Buf("ig%d" % i) for i in range(NS)]
            aa = [sb(ls, "aa%d" % i, [P, T]) for i in range(NS)]
            aa_b = [Buf("aa%d" % i) for i in range(NS)]
            hh = [sb(ls, "hh%d" % i, [P, T], BF16 if full else F32) for i in range(2)]
            hh_b = [Buf("hh%d" % i) for i in range(2)]
            groups = [(g0, min(512, T - g0)) for g0 in range(0, T, 512)]
            groups2 = [(g0, min(1024, T - g0)) for g0 in range(0, T, 1024)]
            units = [(j, d) for j in range(16) for d in dirs]
            NU = len(units)
            ev = [0]

            def evac_copy(dst_ap, src_ap, rbufs, wbufs):
                ev[0] += 1
                if ev[0] % 2 == 0:
                    S.emit("act", lambda e: e.activation(out=dst_ap, in_=src_ap, func=AF.Copy), reads=rbufs, writes=wbufs)
                else:
                    S.emit("dve", lambda e: e.tensor_copy(out=dst_ap, in_=src_ap), reads=rbufs, writes=wbufs)

            def a_setup(j):
                s_ = j % 2
                S.emit("pool", lambda e: e.dma_start(out=wl[s_][:], in_=w_in_v[:, :, LIO + j * P:LIO + (j + 1) * P]),
                       writes=[wl_b[s_]], dma=wl_b[s_])
                w3 = j % 3
                S.emit("pool", lambda e: e.dma_start(out=wrj[w3][:].rearrange("p a b -> p (a b)"), in_=wrg_d[j]),
                       writes=[wrj_b[w3]], dma=wrj_b[w3])
                for t in range(5):
                    col = PF_WCONV + t * 16 + j
                    S.emit("dve", lambda e, t=t, col=col: e.tensor_scalar(
                        out=dg[s_][:, t, :], in0=cst[:, 0:128], scalar1=pf[:, col:col + 1], scalar2=None, op0=ALU.mult),
                        reads=[cst_b, pf_b], writes=[dg_b[s_]])
                if not halo_l:
                    S.emit("dve", lambda e: e.memset(xin[s_][:, 0:2], 0.0), writes=[xin_b[s_]])
                if not halo_r:
                    S.emit("dve", lambda e: e.memset(xin[s_][:, T + 2:T + 4], 0.0), writes=[xin_b[s_]])

            def a_pieces(j):
                s_ = j % 2
                out = []
                specs = [(c0 + g0, n, 2 + g0) for (g0, n) in groups]
                if halo_l:
                    specs.append((c0 - 2, 2, 0))
                if halo_r:
                    specs.append((c0 + T, 2, T + 2))
                for (a, n, o) in specs:
                    def pe(a=a, n=n):
                        pt, pt_b = psum()
                        for kc in range(KC):
                            S.emit("pe", lambda e, kc=kc: e.matmul(
                                pt[:, 0:n], lhsT=wl[s_][:, kc, :], rhs=hnT[:, kc, a:a + n], start=(kc == 0),
                                stop=(kc == KC - 1)), reads=[wl_b[s_]] + hn_bufs(a, a + n), writes=[pt_b])
                        return pt, pt_b

                    def ev(pt, pt_b, n=n, o=o):
                        S.emit("dve", lambda e: e.tensor_copy(out=xin[s_][:, o:o + n], in_=pt[:, 0:n]),
                               reads=[pt_b], writes=[xin_b[s_]])
                    out.append((pe, ev))
                return out

            def c_pieces(j):
                s_ = j % 2
                out = []
                for (g0, n) in groups:
                    def pe(g0=g0, n=n):
                        pt, pt_b = psum()
                        for t in range(5):
                            S.emit("pe", lambda e, t=t: e.matmul(
                                pt[:, 0:n], lhsT=dg[s_][:, t, :], rhs=xin[s_][:, g0 + t:g0 + t + n], start=(t == 0),
                                stop=(t == 4)), reads=[dg_b[s_], xin_b[s_]], writes=[pt_b])
                        return pt, pt_b

                    def ev(pt, pt_b, g0=g0, n=n):
                        S.emit("dve", lambda e: e.tensor_scalar(
                            out=xcb[s_][:, g0:g0 + n], in0=pt[:, 0:n], scalar1=pf[:, PF_BCONV + j:PF_BCONV + j + 1],
                            scalar2=None, op0=ALU.add), reads=[pt_b, pf_b], writes=[xcb_b[s_]])
                    out.append((pe, ev))
                return out

            def run_pieces(pieces):
                pend = [(ev, pe()) for (pe, ev) in pieces]
                return pend

            def run_evacs(pend):
                for ev, (pt, pt_b) in pend:
                    ev(pt, pt_b)

            def stage_g_half(u, hf):
                if hf >= len(groups2):
                    return
                j, d = units[u]
                s_ = j % 2
                i3 = u % NS
                g0, n2 = groups2[hf]
                for (which, dst, dst_b) in ((0, rr[i3], rr_b[i3]), (1, ig[i3], ig_b[i3])):
                    wi = 2 * d + which
                    bcol = wi * 16 + j
                    pt2, pt2_b = psum2()
                    for h0_ in range(0, n2, 512):
                        n = min(512, n2 - h0_)
                        S.emit("pe", lambda e, pt2=pt2, h0_=h0_, n=n, wi=wi: e.matmul(
                            pt2[:, h0_:h0_ + n], lhsT=wrj[j % 3][:, wi, :], rhs=xcb[s_][:, g0 + h0_:g0 + h0_ + n],
                            start=True, stop=True), reads=[wrj_b[j % 3], xcb_b[s_]], writes=[pt2_b[h0_ // 512]])
                    S.emit("act", lambda e, pt2=pt2, dst=dst, bcol=bcol: e.activation(
                        out=dst[:, g0:g0 + n2], in_=pt2[:, 0:n2], func=AF.Tanh, scale=0.5, bias=hbrg[:, bcol:bcol + 1]),
                        reads=pt2_b + [hbrg_b], writes=[dst_b])

            def stage_eq(u):
                j, d = units[u]
                s_ = j % 2
                i3 = u % NS
                hc = hcl[:, d * 16 + j:d * 16 + j + 1]
                c2 = cl[:, d * 16 + j:d * 16 + j + 1]
                S.emit("act", lambda e: e.activation(out=aa[i3][:], in_=rr[i3][:], func=AF.Exp, scale=hc, bias=hc),
                       reads=[rr_b[i3], hcl_b], writes=[aa_b[i3]])
                S.emit("act", lambda e: e.activation(out=rr[i3][:], in_=rr[i3][:], func=AF.Exp, scale=c2, bias=c2),
                       reads=[rr_b[i3], cl_b], writes=[rr_b[i3]])
                S.emit("dve", lambda e: e.scalar_tensor_tensor(
                    out=ig[i3][:], in0=ig[i3][:], scalar=1.0, in1=xcb[s_][:], op0=ALU.add, op1=ALU.mult),
                    reads=[ig_b[i3], xcb_b[s_]], writes=[ig_b[i3]])
                S.emit("act", lambda e: e.activation(out=rr[i3][:], in_=rr[i3][:], func=AF.Sqrt, scale=-0.25, bias=0.25),
                       reads=[rr_b[i3]], writes=[rr_b[i3]])
                S.emit("dve", lambda e: e.tensor_tensor(out=ig[i3][:], in0=ig[i3][:], in1=rr[i3][:], op=ALU.mult),
                       reads=[ig_b[i3], rr_b[i3]], writes=[ig_b[i3]])

            def stage_sc(u):
                j, d = units[u]
                i3 = u % NS
                i2 = u % 2
                h0 = hst[:, d * 16 + j:d * 16 + j + 1]
                if d == 0:
                    S.emit("dve", lambda e: e.tensor_tensor_scan(
                        out=hh[i2][:], data0=aa[i3][:], data1=ig[i3][:], initial=h0, op0=ALU.mult, op1=ALU.add),
                        reads=[aa_b[i3], ig_b[i3], hst_b[0][j]], writes=[hh_b[i2]])
                    if not full:
                        S.emit("dve", lambda e: e.tensor_copy(out=h0, in_=hh[i2][:, T - 1:T]),
                               reads=[hh_b[i2]], writes=[hst_b[0][j]])
                else:
                    S.emit("dve", lambda e: e.tensor_tensor_scan(
                        out=hh[i2][:, ::-1], data0=aa[i3][:, ::-1], data1=ig[i3][:, ::-1], initial=h0, op0=ALU.mult,
                        op1=ALU.add), reads=[aa_b[i3], ig_b[i3], hst_b[1][j]], writes=[hh_b[i2]])
                    if not full:
                        S.emit("dve", lambda e: e.tensor_copy(out=h0, in_=hh[i2][:, 0:1]),
                               reads=[hh_b[i2]], writes=[hst_b[1][j]])
                if full:
                    S.emit("sp", lambda e: e.dma_start(out=mlru_d[d, j], in_=hh[i2][:]), reads=[hh_b[i2]],
                           writes=[mlru_b[d][j]], dma=hh_b[i2])

            for j0 in (0, 1):
                a_setup(j0)
                run_evacs(run_pieces(a_pieces(j0)))
            run_evacs(run_pieces(c_pieces(0)))
            slot_pieces = {}
            for k in range(NU + 2):
                if 0 <= k - 2 < NU:
                    stage_sc(k - 2)
                if 0 <= k - 1 < NU:
                    stage_eq(k - 1)
                li, jb = k % nd, k // nd
                if li == 0:
                    cp = c_pieces(jb + 1) if jb + 1 < 16 else []
                    ap = a_pieces(jb + 2) if jb + 2 < 16 else []
                    if jb + 2 < 16:
                        a_setup(jb + 2)
                    for q in range(nd):
                        slot_pieces[q] = [cp[q::nd], ap[q::nd]]
                if k < NU:
                    stage_g_half(k, 0)
                if k < NU:
                    for batch in slot_pieces.get(li, []):
                        run_evacs(run_pieces(batch))
                    stage_g_half(k, 1)

        def merge_gate_stage(ms, silu_jobs):
            wm = [sb(ms, "wm%d" % i, [P, KC, P], BF16) for i in range(2)]
            wm_b = [Buf("wm%d" % i) for i in range(2)]
            sgb = [sb(ms, "sgb%d" % i, [P, T_OWN], BF16) for i in range(2)]
            sgb_b = [Buf("sgb%d" % i) for i in range(2)]
            yld = [sb(ms, "yld%d" % i, [P, T_OWN], BF16) for i in range(2)]
            yld_b = [Buf("yld%d" % i) for i in range(2)]
            yl2 = [sb(ms, "yl2%d" % i, [P, T_OWN], BF16) for i in range(2)]
            yl2_b = [Buf("yl2%d" % i) for i in range(2)]
            jobs = []
            add2 = {}
            for job in silu_jobs:
                off, dt_, db_ = job[0], job[1], job[2]
                for i in range(16):
                    jobs.append(("silu", off + i * P, None, dt_, db_, i))
                    if len(job) > 3:
                        add2[len(jobs) - 1] = (job[3], job[4])
            for fc in range(16):
                for which, off in ((0, MGO), (1, MLO)):
                    jobs.append(("sig", off + fc * P, PF_BM + which * 16 + fc, sg_d[which], sg_b[which], fc))
            for k, (kind, col, bcol, dt_, db_, i) in enumerate(jobs):
                w, w_b = wm[k % 2], wm_b[k % 2]
                o, o_b = sgb[k % 2], sgb_b[k % 2]
                S.emit("pool", lambda e, w=w, col=col: e.dma_start(out=w[:], in_=w_in_v[:, :, col:col + P]),
                       writes=[w_b], dma=w_b)
                if kind == "silu":
                    yl_, yl_b_ = yld[k % 2], yld_b[k % 2]
                    S.emit("sp", lambda e, yl_=yl_, dt_=dt_, i=i: e.dma_start(out=yl_[:], in_=dt_[i]),
                           reads=[db_[i]], writes=[yl_b_], dma=yl_b_)
                    if k in add2:
                        d2, d2b = add2[k]
                        y2_, y2_b_ = yl2[k % 2], yl2_b[k % 2]
                        S.emit("sp", lambda e, y2_=y2_, d2=d2, i=i: e.dma_start(out=y2_[:], in_=d2[i]),
                               reads=[d2b[i]], writes=[y2_b_], dma=y2_b_)
                        S.emit("dve", lambda e, yl_=yl_, y2_=y2_: e.tensor_tensor(out=yl_[:], in0=yl_[:], in1=y2_[:], op=ALU.add),
                               reads=[yl_b_, y2_b_], writes=[yl_b_])
                for g0 in range(0, T_OWN, 512):
                    pt, pt_b = psum()
                    for kc in range(KC):
                        S.emit("pe", lambda e, pt=pt, w=w, kc=kc, g0=g0: e.matmul(
                            pt[:, :], lhsT=w[:, kc, :], rhs=hnT[:, kc, g0:g0 + 512], start=(kc == 0),
                            stop=(kc == KC - 1)), reads=[w_b] + hn_bufs(g0, g0 + 512), writes=[pt_b])
                    if kind == "sig":
                        S.emit("act", lambda e, pt=pt, o=o, g0=g0, bcol=bcol: e.activation(
                            out=o[:, g0:g0 + 512], in_=pt[:, :], func=AF.Sigmoid, bias=pf[:, bcol:bcol + 1]),
                            reads=[pt_b, pf_b], writes=[o_b])
                    else:
                        S.emit("act", lambda e, pt=pt, o=o, g0=g0: e.activation(
                            out=o[:, g0:g0 + 512], in_=pt[:, :], func=AF.Silu), reads=[pt_b], writes=[o_b])
                if kind == "silu":
                    S.emit("dve", lambda e, o=o, yl_=yl_: e.tensor_tensor(out=o[:], in0=o[:], in1=yl_[:], op=ALU.mult),
                           reads=[o_b, yl_b_], writes=[o_b])
                S.emit("sp", lambda e, o=o, dt_=dt_, i=i: e.dma_start(out=dt_[i], in_=o[:]),
                       reads=[o_b], writes=[db_[i]], dma=o_b)

        S.mark("ctx_hn")
        build_hn(ctxl, 0, 2, 32, 0)
        build_lr(0, 256)
        S.mark("ctx_gla")
        with ExitStack() as st:
            gla_stage(st, 0, 2, (0, 1), False, True)
            S.barrier()
        S.mark("ctx_lru")
        with ExitStack() as st:
            lru_stage(st, 0, 256, (0, 1), False, False, False)
            S.barrier()
        S.mark("oth_hn")
        build_hn(xl, T_OWN - P, 17, 0, 0)
        build_lr(P, T_OWN)
        S.mark("oth_gla")
        with ExitStack() as st:
            gla_stage(st, P, 16, (1,), False, False)
            S.barrier()
        S.mark("oth_lru")
        with ExitStack() as st:
            lru_stage(st, P, T_OWN, (1,), False, True, False)
            S.barrier()
        S.mark("own_hn")
        build_hn(xl, 0, 17, 0, 0)
        build_lr(0, T_OWN)
        S.mark("own_gla")
        with ExitStack() as st:
            gla_stage(st, 0, 16, (0, 1), True, False)
            S.barrier()
        S.mark("own_lru")
        with ExitStack() as st:
            lru_stage(st, 0, T_OWN, (0, 1), True, False, True)
            S.barrier()
        S.mark("mgate")
        with ExitStack() as st:
            merge_gate_stage(st, [(GGO, mgla_d, mgla_b), (LGO, mlru_d[0], mlru_b[0], mlru_d[1], mlru_b[1])])
            S.barrier()
        S.mark("p6")

    with ExitStack() as ph:
        mres = sb(ph, "mres", [P, KC, T_OWN], BF16)
        mres_b = [Buf("mres%d" % i) for i in range(KC)]
        macc = sb(ph, "macc", [P, KC, T_OWN], BF16)
        macc_b = [Buf("macc%d" % i) for i in range(KC)]
        with ExitStack() as p6:
            wo = [sb(p6, "wo%d" % i, [P, KC, P], BF16) for i in range(2)]
            wo_b = [Buf("wo%d" % i) for i in range(2)]
            sgl = [sb(p6, "sgl%d" % i, [P, T_OWN], BF16) for i in range(2)]
            sgl_b = [Buf("sgl%d" % i) for i in range(2)]
            tmp = sb(p6, "tmp6", [P, 512])
            tmp_b = Buf("tmp6")
            for which, (md, mdb, wsrc) in enumerate(((mgla_d, mgla_b, w_o_gla), (mlru_d[0], mlru_b[0], w_o_rnn))):
                wv = wsrc.rearrange("(kc p) n -> p kc n", p=P)
                for ec in range(KC):
                    S.emit("sp", lambda e, md=md, ec=ec: e.dma_start(out=mres[:, ec, :], in_=md[ec]),
                           reads=[mdb[ec]], writes=[mres_b[ec]], dma=mres_b[ec])
                for fc in range(16):
                    w, w_b = wo[fc % 2], wo_b[fc % 2]
                    sgt_, sgt_b_ = sgl[fc % 2], sgl_b[fc % 2]
                    S.emit("pool", lambda e, w=w, wv=wv, fc=fc: e.dma_start(out=w[:], in_=wv[:, :, fc * P:(fc + 1) * P]),
                           writes=[w_b], dma=w_b)
                    S.emit("sp", lambda e, sgt_=sgt_, which=which, fc=fc: e.dma_start(out=sgt_[:], in_=sg_d[which, fc]),
                           reads=[sg_b[which][fc]], writes=[sgt_b_], dma=sgt_b_)
                    for g0 in range(0, T_OWN, 512):
                        pt, pt_b = psum()
                        for ec in range(KC):
                            S.emit("pe", lambda e, pt=pt, w=w, ec=ec, g0=g0: e.matmul(
                                pt[:, :], lhsT=w[:, ec, :], rhs=mres[:, ec, g0:g0 + 512], start=(ec == 0),
                                stop=(ec == KC - 1)), reads=[w_b, mres_b[ec]], writes=[pt_b])
                        if which == 0:
                            S.emit("dve", lambda e, pt=pt, sgt_=sgt_, fc=fc, g0=g0: e.tensor_tensor(
                                out=macc[:, fc, g0:g0 + 512], in0=pt[:, :], in1=sgt_[:, g0:g0 + 512], op=ALU.mult),
                                reads=[pt_b, sgt_b_], writes=[macc_b[fc]])
                        else:
                            S.emit("dve", lambda e, pt=pt, sgt_=sgt_, g0=g0: e.tensor_tensor(
                                out=tmp[:], in0=pt[:, :], in1=sgt_[:, g0:g0 + 512], op=ALU.mult),
                                reads=[pt_b, sgt_b_], writes=[tmp_b])
                            S.emit("dve", lambda e, fc=fc, g0=g0: e.tensor_tensor(
                                out=macc[:, fc, g0:g0 + 512], in0=macc[:, fc, g0:g0 + 512], in1=tmp[:], op=ALU.add),
                                reads=[tmp_b, macc_b[fc]], writes=[macc_b[fc]])
            S.barrier()
        S.mark("p7")
        with ExitStack() as p7:
            wov = w_out.rearrange("(kc p) n -> p kc n", p=P)
            wout_sem_b = [Buf("wout_sem%d" % g) for g in range(4)]
            for g in range(4):
                S.emit("pool", lambda e, g=g: e.dma_start(out=mres[:, :, g * 512:(g + 1) * 512],
                                                          in_=wov[:, :, g * 512:(g + 1) * 512]),
                       writes=mres_b, dma=wout_sem_b[g])
            Gb = sb(p7, "Gb", [P, D])
            Gb_b = Buf("Gb")
            Gf = sb(p7, "Gf", [P, D])
            Gf_b = Buf("Gf")
            gfr = sb(p7, "gfr", [1, D])
            gfr_b = Buf("gfr")
            for (rsrc, rsrc_b, dst, dst_b) in ((grow_d, [growd_b], Gb, Gb_b), (gfin_d, [], Gf, Gf_b)):
                S.emit("sp", lambda e, rsrc=rsrc: e.dma_start(out=gfr[:], in_=rsrc), reads=rsrc_b, writes=[gfr_b],
                       dma=gfr_b)
                row, row_b = gfr, gfr_b
                for g in range(4):
                    pt, pt_b = psum()
                    S.emit("pe", lambda e, pt=pt, row=row, g=g: e.matmul(
                        pt[:, :], lhsT=ones[0:1, :], rhs=row[0:1, g * 512:(g + 1) * 512], start=True, stop=True),
                        reads=[ones_b, row_b], writes=[pt_b])
                    S.emit("act", lambda e, pt=pt, dst=dst, g=g: e.activation(out=dst[:, g * 512:(g + 1) * 512],
                                                                              in_=pt[:, :], func=AF.Copy),
                           reads=[pt_b], writes=[dst_b])
            xo = sb(p7, "xo", [P, D])
            xo_b = Buf("xo")
            rt = sb(p7, "rt", [P, D])
            rt_b = Buf("rt")
            for t in range(NT_OWN):
                S.emit("sp", lambda e, t=t: e.dma_start(out=xo[:], in_=xl[t * P:(t + 1) * P, :]), writes=[xo_b], dma=xo_b)
                for g in range(4):
                    pt, pt_b = psum()
                    for kc in range(KC):
                        S.emit("pe", lambda e, pt=pt, kc=kc, t=t, g=g: e.matmul(
                            pt[:, :], lhsT=macc[:, kc, t * P:(t + 1) * P], rhs=mres[:, kc, g * 512:(g + 1) * 512],
                            start=(kc == 0), stop=(kc == KC - 1)), reads=[macc_b[kc], mres_b[kc]], writes=[pt_b])
                    S.emit("dve", lambda e, pt=pt, g=g: e.tensor_tensor(
                        out=rt[:, g * 512:(g + 1) * 512], in0=pt[:, :], in1=Gb[:, g * 512:(g + 1) * 512], op=ALU.mult),
                        reads=[pt_b, Gb_b], writes=[rt_b])
                S.emit("dve", lambda e: e.tensor_tensor(out=rt[:], in0=rt[:], in1=xo[:], op=ALU.add),
                       reads=[rt_b, xo_b], writes=[rt_b])
                ssq, ssq_b = smallcol()
                S.emit("act", lambda e, ssq=ssq: e.activation(out=xo[:], in_=rt[:], func=AF.Square, accum_out=ssq),
                       reads=[rt_b], writes=[xo_b, ssq_b])
                r, r_b = smallcol()
                S.emit("dve", lambda e, r=r, ssq=ssq: e.tensor_scalar(out=r, in0=ssq, scalar1=1.0 / D, scalar2=EPS,
                                                                      op0=ALU.mult, op1=ALU.add),
                       reads=[ssq_b], writes=[r_b])
                S.emit("act", lambda e, r=r: e.activation(out=r, in_=r, func=AF.Sqrt), reads=[r_b], writes=[r_b])
                S.emit("dve", lambda e, r=r: e.reciprocal(out=r, in_=r), reads=[r_b], writes=[r_b])
                S.emit("dve", lambda e, r=r: e.scalar_tensor_tensor(out=rt[:], in0=rt[:], scalar=r, in1=Gf[:],
                                                                    op0=ALU.mult, op1=ALU.mult),
                       reads=[rt_b, r_b, Gf_b], writes=[rt_b])
                op = S.emit("sp", lambda e, t=t: e.dma_start(out=yl[t * P:(t + 1) * P, :], in_=rt[:]),
                            reads=[rt_b], writes=[yl_b], dma=rt_b)
                S.final_waits.append(op)
            S.barrier()

    S.mark("end")
    S.replay(es)
    es.close()
    nc._marks = S.marks
    return nc


def _fm(v):
    v = np.asarray(v, np.float32).reshape(-1, P)
    return np.ascontiguousarray(v.T)


_NC_CACHE = {}


def kernel(x, c, ctx, c_ctx, w_ada, b_ada, g_norm, w_in, w_gla_a, b_gla_a, g_gla_out, w_conv, b_conv,
           w_rg_a, b_rg_a, w_rg_x, b_rg_x, lam, w_o_gla, w_o_rnn, b_merge, w_out, g_final):
    f = lambda a: np.ascontiguousarray(np.asarray(a, np.float32))
    x, c, ctx, c_ctx = f(x), f(c), f(ctx), f(c_ctx)
    w_ada0, b_ada0, g_norm0, w_in0 = f(w_ada)[0], f(b_ada)[0], f(g_norm)[0], f(w_in)[0]
    w_gla_a0, b_gla_a0, g_gla_out0 = f(w_gla_a)[0], f(b_gla_a)[0], f(g_gla_out)[0]
    w_conv0, b_conv0 = f(w_conv)[0], f(b_conv)[0]
    w_rg_a0, b_rg_a0, w_rg_x0, b_rg_x0, lam0 = f(w_rg_a)[0], f(b_rg_a)[0], f(w_rg_x)[0], f(b_rg_x)[0], f(lam)[0]
    w_o_gla0, w_o_rnn0, b_merge0, w_out0, g_final0 = f(w_o_gla)[0], f(w_o_rnn)[0], f(b_merge)[0], f(w_out)[0], f(g_final)

    consts = np.zeros((P, 384), np.float32)
    consts[:, 0:128] = np.eye(P, dtype=np.float32)
    s_idx = np.arange(P)[:, None]
    c_idx = np.arange(P)[None, :]
    consts[:, 128:256] = (s_idx <= c_idx)
    consts[:, 256:384] = (s_idx >= c_idx)

    per_half = []
    for hf in range(2):
        dF, dB = (0, 1) if hf == 0 else (1, 0)
        w_la = np.zeros((D, 64), np.float32)
        lo = 6144
        w_la[:, 0:16] = w_in0[:, lo + 16 * dF: lo + 16 * dF + 16]
        w_la[:, 32:48] = w_in0[:, lo + 16 * dB: lo + 16 * dB + 16]
        wga = np.zeros((64, 1024), np.float32)
        wga[0:16] = w_gla_a0[dF]
        wga[16] = b_gla_a0[dF]
        wga[32:48] = w_gla_a0[dB]
        wga[48] = b_gla_a0[dB]
        wr = np.stack([w_rg_a0[dF], w_rg_x0[dF], w_rg_a0[dB], w_rg_x0[dB]], 0)
        wrg = np.ascontiguousarray(wr.transpose(1, 2, 0, 3)).reshape(16, P, 4 * P)
        taps = np.zeros((5, D), np.float32)
        if hf == 0:
            taps[0:4] = w_conv0
        else:
            taps[1:5] = w_conv0[::-1]
        pf = np.concatenate([
            _fm(b_ada0), _fm(g_norm0), _fm(b_conv0),
            np.concatenate([_fm(taps[t]) for t in range(5)], 1),
            _fm(b_rg_a0[dF]), _fm(b_rg_x0[dF]), _fm(b_rg_a0[dB]), _fm(b_rg_x0[dB]),
            _fm(lam0[dF]), _fm(lam0[dB]), _fm(b_merge0)], 1)
        assert pf.shape == (P, PF_N)
        per_half.append(dict(w_la=w_la, wga=wga, wrg=wrg, pf=np.ascontiguousarray(pf)))

    shared = dict(consts=consts, w_ada=w_ada0, bgate=np.ascontiguousarray(b_ada0[None, 2 * D:3 * D]), w_in=w_in0,
                  ggo=np.ascontiguousarray(g_gla_out0[None, :]), gfin=np.ascontiguousarray(g_final0[None, :]),
                  w_o_gla=w_o_gla0, w_o_rnn=w_o_rnn0, w_out=w_out0)
    in_maps = []
    for b in range(4):
        for hf in range(2):
            if hf == 0:
                xl_, ctxl_ = x[b], ctx[b]
            else:
                xl_, ctxl_ = np.ascontiguousarray(x[b][::-1]), np.ascontiguousarray(ctx[b][::-1])
            cc = np.stack([c[b], c_ctx], 0)
            ccT = np.ascontiguousarray(cc.reshape(2, KC, P).transpose(2, 1, 0)).reshape(P, 32)
            m = dict(shared)
            m.update(per_half[hf])
            m.update(xl=xl_, ctxl=ctxl_, ccT=ccT)
            in_maps.append(m)

    if "nc" not in _NC_CACHE:
        _NC_CACHE["nc"] = build_nc()
    nc = _NC_CACHE["nc"]
    res = run_bass_kernel_spmd(nc, in_maps, core_ids=list(range(8)))
    out = np.empty((4, 4096, D), np.float32)
    for b in range(4):
        for hf in range(2):
            y = np.asarray(res.results[b * 2 + hf]["yl"], np.float32)
            if hf == 0:
                out[b, 0:T_OWN] = y
            else:
                out[b, T_OWN:] = y[::-1]
    return out
```

```python
from contextlib import ExitStack

import numpy as np
import concourse.bass as bass
import concourse.mybir as mybir
from concourse.bass_utils import run_bass_kernel_spmd

F32 = mybir.dt.float32
BF16 = mybir.dt.bfloat16
AF = mybir.ActivationFunctionType
ALU = mybir.AluOpType

D = 2048
KC = 16
P = 128
D_IN = 14368
QO, KO, VO, GGO, LIO, LGO, MGO, MLO = 0, 1024, 2048, 4096, 6176, 8224, 10272, 12320
EPS = 1e-6
T_OWN = 2048
NT_OWN = 16
PF_BADA, PF_GN, PF_BCONV, PF_WCONV, PF_BRG, PF_LAM, PF_BM, PF_N = 0, 48, 64, 80, 160, 224, 256, 288


class Buf:
    __slots__ = ("name", "lw", "rd", "sem", "cnt", "excl")

    def __init__(self, name, excl=False):
        self.name = name
        self.excl = excl
        self.lw = None
        self.rd = {}
        self.sem = None
        self.cnt = 0


class Op:
    __slots__ = ("eng", "idx", "sig", "tick", "fn", "waits", "dma", "sembuf", "val")


class Sched:
    ENG = ("pe", "act", "dve", "pool", "sp")

    def __init__(self, nc):
        self.nc = nc
        self.ops = {e: [] for e in self.ENG}
        self.waited = {e: {} for e in self.ENG}
        self.dma_bufs = []
        self.final_waits = []
        self.marks = []

    def emit(self, eng, fn, reads=(), writes=(), dma=None):
        op = Op()
        op.eng = eng
        op.idx = len(self.ops[eng])
        op.sig = False
        op.tick = 0
        op.fn = fn
        op.dma = dma is not None
        op.sembuf = dma
        op.val = 0
        deps = []
        for b in reads:
            if b.lw is not None:
                deps.append(b.lw)
            if b.excl:
                deps.extend(o for k, o in b.rd.items() if k != ("e", eng))
        for b in writes:
            if b.lw is not None:
                deps.append(b.lw)
            deps.extend(b.rd.values())
        waits = []
        wd = self.waited[eng]
        for d in deps:
            if d.dma:
                key = ("d", id(d.sembuf))
                v = d.val
            else:
                if d.eng == "pe" and eng == "pe":
                    continue
                key = ("e", d.eng)
                v = d.idx
            if wd.get(key, -1) >= v:
                continue
            wd[key] = v
            waits.append(d)
            if not d.dma:
                d.sig = True
        op.waits = waits
        if dma is not None:
            if dma.cnt == 0:
                self.dma_bufs.append(dma)
            dma.cnt += 16
            op.val = dma.cnt
            rkey = ("d", id(dma))
        else:
            rkey = ("e", eng)
        for b in reads:
            b.rd[rkey] = op
        for b in writes:
            b.lw = op
            b.rd = {}
        self.ops[eng].append(op)
        return op

    def mark(self, label):
        self.marks.append((label, {e: len(self.ops[e]) for e in self.ENG}))

    def barrier(self):
        lasts = {e: (self.ops[e][-1] if self.ops[e] else None) for e in self.ENG}
        for e in self.ENG:
            for o in reversed(self.ops[e]):
                if o.fn is not None and not o.dma:
                    lasts[e] = o
                    break
            else:
                lasts[e] = None
        dmas = [(b, b.cnt) for b in self.dma_bufs]
        for f in self.ENG:
            op = Op()
            op.eng = f
            op.idx = len(self.ops[f])
            op.sig = False
            op.tick = 0
            op.fn = None
            op.dma = False
            op.sembuf = None
            op.val = 0
            waits = []
            wd = self.waited[f]
            for e in self.ENG:
                d = lasts[e]
                if d is None or e == f:
                    continue
                if wd.get(("e", e), -1) >= d.idx:
                    continue
                wd[("e", e)] = d.idx
                d.sig = True
                waits.append(d)
            for b, c in dmas:
                key = ("d", id(b))
                if wd.get(key, -1) >= c:
                    continue
                wd[key] = c
                fake = Op()
                fake.dma = True
                fake.sembuf = b
                fake.val = c
                waits.append(fake)
            op.waits = waits
            self.ops[f].append(op)

    def replay(self, es):
        nc = self.nc
        esem = {e: es.enter_context(nc.semaphore("es_" + e)) for e in self.ENG}
        for i, b in enumerate(self.dma_bufs):
            b.sem = es.enter_context(nc.semaphore("ds%d" % i))
        for e in self.ENG:
            c = 0
            for o in self.ops[e]:
                if o.sig and not o.dma:
                    c += 1
                    o.tick = c
        finals = list(self.final_waits)
        block = es.enter_context(nc.Block())

        def run(e, engh, extra=None):
            for o in self.ops[e]:
                for d in o.waits:
                    if d.dma:
                        engh.wait_ge(d.sembuf.sem, d.val)
                    else:
                        engh.wait_ge(esem[d.eng], d.tick)
                if o.fn is None:
                    continue
                ins = o.fn(engh)
                if o.dma:
                    ins.then_inc(o.sembuf.sem, 16)
                elif o.sig:
                    ins.then_inc(esem[e], 1)
            if extra:
                for d in extra:
                    engh.wait_ge(d.sembuf.sem, d.val)

        @block.tensor
        def _(t):
            run("pe", t)

        @block.scalar
        def _(t):
            run("act", t)

        @block.vector
        def _(t):
            run("dve", t)

        @block.gpsimd
        def _(t):
            run("pool", t)

        @block.sync
        def _(t):
            run("sp", t, finals)


def build_nc():
    nc = bass.Bass("TRN2", target_bir_lowering=False)
    S = Sched(nc)
    es = ExitStack()

    def din(name, shape):
        return nc.dram_tensor(name, list(shape), F32, kind="ExternalInput").ap()

    xl = din("xl", [4096, D])
    ctxl = din("ctxl", [256, D])
    ccT = din("ccT", [P, 32])
    pf_d = din("pf", [P, PF_N])
    consts_d = din("consts", [P, 384])
    w_ada = din("w_ada", [D, 3 * D])
    bgate_d = din("bgate", [1, D])
    w_in = din("w_in", [D, D_IN])
    w_la = din("w_la", [D, 64])
    wga_d = din("wga", [64, 1024])
    wrg_d = din("wrg", [16, P, 4 * P])
    ggo_d = din("ggo", [1, 512])
    gfin_d = din("gfin", [1, D])
    w_o_gla = din("w_o_gla", [D, D])
    w_o_rnn = din("w_o_rnn", [D, D])
    w_out = din("w_out", [D, D])
    yl = nc.dram_tensor("yl", [T_OWN, D], F32, kind="ExternalOutput").ap()

    ob_d = nc.dram_tensor("ob_d", [4, NT_OWN, P, 512], F32, kind="Internal").ap()
    mgla_d = nc.dram_tensor("mgla_d", [16, P, T_OWN], BF16, kind="Internal").ap()
    mlru_d = nc.dram_tensor("mlru_d", [2, 16, P, T_OWN], BF16, kind="Internal").ap()
    sg_d = nc.dram_tensor("sg_d", [2, 16, P, T_OWN], BF16, kind="Internal").ap()
    st_d = nc.dram_tensor("st_d", [2, 4, P, 1024], F32, kind="Internal").ap()
    ob_b = [[Buf("ob%d_%d" % (h, n)) for n in range(NT_OWN)] for h in range(4)]
    mgla_b = [Buf("mgla%d" % i) for i in range(16)]
    mlru_b = [[Buf("mlru%d_%d" % (d, i)) for i in range(16)] for d in range(2)]
    sg_b = [[Buf("sg%d_%d" % (w, i)) for i in range(16)] for w in range(2)]
    st_b = [[Buf("st%d_%d" % (d, h)) for h in range(4)] for d in range(2)]
    yl_b = Buf("yl")

    sb_n = [0]

    def sb(stack, name, shape, dt=F32):
        sb_n[0] += 1
        return stack.enter_context(nc.sbuf_tensor("s%d_%s" % (sb_n[0], name), list(shape), dt))

    w_in_v = w_in.rearrange("(kc p) n -> p kc n", p=P)
    w_la_v = w_la.rearrange("(kc p) n -> p kc n", p=P)
    w_ada_v = w_ada.rearrange("(kc p) n -> p kc n", p=P)

    ps_big = [es.enter_context(nc.psum_tensor("psbig%d" % i, [P, 2048], F32)) for i in range(2)]
    ps_t = [ps_big[i // 4][:, (i % 4) * 512:(i % 4 + 1) * 512] for i in range(8)]
    ps_b = [Buf("psb%d" % i, excl=True) for i in range(8)]
    ps_rr = [0]

    def psum():
        i = ps_rr[0] % 8
        ps_rr[0] += 1
        return ps_t[i], ps_b[i]

    def psum2():
        if ps_rr[0] % 2:
            ps_rr[0] += 1
        i = ps_rr[0] % 8
        ps_rr[0] += 2
        return ps_big[i // 4][:, (i % 4) * 512:(i % 4) * 512 + 1024], [ps_b[i], ps_b[i + 1]]

    pers = es
    cst = sb(pers, "cst", [P, 384])
    cst_b = Buf("cst")
    ident = cst[:, 0:128]
    pf = sb(pers, "pfm", [P, PF_N])
    pf_b = Buf("pf")
    tri = sb(pers, "tri", [P, 256], BF16)
    tri_b = Buf("tri")
    mod = sb(pers, "mod", [P, 96])
    mod_b = Buf("mod")
    Ax = sb(pers, "Ax", [P, 64])
    Ax_b = Buf("Ax")
    cl = sb(pers, "cl", [P, 64])
    cl_b = Buf("cl")
    hcl = sb(pers, "hcl", [P, 32])
    hcl_b = Buf("hcl")
    hbrg = sb(pers, "hbrg", [P, 64])
    hbrg_b = Buf("hbrg")
    hbc = sb(pers, "hbc", [P, 16])
    hbc_b = Buf("hbc")
    identb = sb(pers, "identb", [P, P], BF16)
    identb_b = Buf("identb")
    hst = sb(pers, "hst", [P, 32])
    hst_b = [[Buf("hst%d_%d" % (d, j)) for j in range(16)] for d in range(2)]
    grow_d = nc.dram_tensor("grow_d", [1, D], F32, kind="Internal").ap()
    growd_b = Buf("growd")
    ones = sb(pers, "ones", [P, 128])
    ones_b = Buf("ones")
    wga = sb(pers, "wga", [64, 1024], BF16)
    wga_b = Buf("wga")
    scT = sb(pers, "scT", [P, 32], BF16)
    scT_b = Buf("scT")
    small = sb(pers, "small", [P, 64])
    small_bufs = [Buf("small%d" % i) for i in range(64)]
    small_rr = [0]

    def smallcol():
        i = small_rr[0] % 64
        small_rr[0] += 1
        return small[:, i:i + 1], small_bufs[i]

    S.emit("sp", lambda e: e.dma_start(out=cst[:], in_=consts_d), writes=[cst_b], dma=cst_b)
    S.emit("sp", lambda e: e.dma_start(out=pf[:], in_=pf_d), writes=[pf_b], dma=pf_b)
    S.emit("pool", lambda e: e.dma_start(out=wga[:], in_=wga_d), writes=[wga_b], dma=wga_b)
    S.emit("dve", lambda e: e.tensor_scalar(out=tri[:], in0=cst[:, 128:384], scalar1=-1.0 / 16.0, scalar2=None,
                                            op0=ALU.mult), reads=[cst_b], writes=[tri_b])
    S.emit("dve", lambda e: e.memset(ones[:], 1.0), writes=[ones_b])
    S.emit("dve", lambda e: e.memset(hst[:], 0.0), writes=[b for r in hst_b for b in r])

    with ExitStack() as ph:
        ccs = sb(ph, "ccs", [P, 32])
        ccs_b = Buf("ccs")
        S.emit("sp", lambda e: e.dma_start(out=ccs[:], in_=ccT), writes=[ccs_b], dma=ccs_b)
        S.emit("act", lambda e: e.activation(out=scT[:], in_=ccs[:], func=AF.Silu), reads=[ccs_b], writes=[scT_b])
        scT3 = scT[:].rearrange("p (k r) -> p k r", r=2)
        wb = [sb(ph, "wadab%d" % i, [P, KC, 512], BF16) for i in range(2)]
        wb_b = [Buf("wadab%d" % i) for i in range(2)]
        brow = sb(ph, "brow", [1, D])
        brow_b = Buf("brow")
        grow = sb(ph, "grow", [1, D])
        grow_b = Buf("grow")
        S.emit("sp", lambda e: e.dma_start(out=brow[:], in_=bgate_d), writes=[brow_b], dma=brow_b)
        psA, psA_b = psum()
        for ng in range(12):
            w, w_b = wb[ng % 2], wb_b[ng % 2]
            S.emit("pool", lambda e, w=w, ng=ng: e.dma_start(out=w[:], in_=w_ada_v[:, :, ng * 512:(ng + 1) * 512]),
                   writes=[w_b], dma=w_b)
            for j in range(4):
                n = ng * 4 + j
                for kc in range(KC):
                    S.emit("pe", lambda e, w=w, j=j, kc=kc, n=n: e.matmul(
                        psA[:, 2 * n:2 * n + 2], lhsT=w[:, kc, j * 128:(j + 1) * 128], rhs=scT3[:, kc, :],
                        start=(kc == 0), stop=(kc == KC - 1)), reads=[w_b, scT_b], writes=[psA_b])
            if ng >= 8:
                psG, psG_b = psum()
                for kc in range(KC):
                    S.emit("pe", lambda e, w=w, kc=kc, psG=psG: e.matmul(
                        psG[0:1, :], lhsT=scT3[:, kc, 0:1], rhs=w[:, kc, :],
                        start=(kc == 0), stop=(kc == KC - 1)), reads=[w_b, scT_b], writes=[psG_b])
                c0 = (ng - 8) * 512
                S.emit("dve", lambda e, psG=psG, c0=c0: e.tensor_tensor(
                    out=grow[0:1, c0:c0 + 512], in0=psG[0:1, :], in1=brow[0:1, c0:c0 + 512], op=ALU.add),
                    reads=[psG_b, brow_b], writes=[grow_b])
        S.emit("sp", lambda e: e.dma_start(out=grow_d, in_=grow[:]), reads=[grow_b], writes=[growd_b], dma=grow_b)
        psA3 = psA[:, 0:96].rearrange("p (n r) -> p n r", r=2)
        mod3 = mod[:].rearrange("p (n r) -> p n r", r=2)
        for r in range(2):
            S.emit("dve", lambda e, r=r: e.tensor_tensor(out=mod3[:, :, r], in0=psA3[:, :, r],
                                                         in1=pf[:, PF_BADA:PF_BADA + 48], op=ALU.add),
                   reads=[psA_b, pf_b], writes=[mod_b])
        for r in range(2):
            S.emit("dve", lambda e, r=r: e.scalar_tensor_tensor(
                out=Ax[:, 32 * r:32 * r + 16], in0=mod3[:, 16:32, r], scalar=1.0, in1=pf[:, PF_GN:PF_GN + 16],
                op0=ALU.add, op1=ALU.mult), reads=[mod_b, pf_b], writes=[Ax_b])
            S.emit("dve", lambda e, r=r: e.tensor_copy(out=Ax[:, 32 * r + 16:32 * r + 32], in_=mod3[:, 0:16, r]),
                   reads=[mod_b], writes=[Ax_b])
        tmpl = sb(ph, "tmpl", [P, 32])
        tmpl_b = Buf("tmpl")
        S.emit("act", lambda e: e.activation(out=tmpl[:], in_=pf[:, PF_LAM:PF_LAM + 32], func=AF.Exp, scale=-1.0),
               reads=[pf_b], writes=[tmpl_b])
        S.emit("act", lambda e: e.activation(out=tmpl[:], in_=tmpl[:], func=AF.Ln, bias=1.0),
               reads=[tmpl_b], writes=[tmpl_b])
        S.emit("dve", lambda e: e.tensor_scalar(out=cl[:, 0:32], in0=tmpl[:], scalar1=-8.0, scalar2=None, op0=ALU.mult),
               reads=[tmpl_b], writes=[cl_b])
        S.emit("dve", lambda e: e.tensor_scalar(out=cl[:, 32:64], in0=tmpl[:], scalar1=-16.0, scalar2=None, op0=ALU.mult),
               reads=[tmpl_b], writes=[cl_b])
        S.emit("dve", lambda e: e.tensor_scalar(out=hcl[:], in0=tmpl[:], scalar1=-4.0, scalar2=None, op0=ALU.mult),
               reads=[tmpl_b], writes=[hcl_b])
        S.emit("dve", lambda e: e.tensor_scalar(out=hbrg[:], in0=pf[:, PF_BRG:PF_BRG + 64], scalar1=0.5, scalar2=None,
                                                op0=ALU.mult), reads=[pf_b], writes=[hbrg_b])
        S.emit("dve", lambda e: e.tensor_scalar(out=hbc[:], in0=pf[:, PF_BCONV:PF_BCONV + 16], scalar1=0.5, scalar2=None,
                                                op0=ALU.mult), reads=[pf_b], writes=[hbc_b])
        S.emit("dve", lambda e: e.tensor_copy(out=identb[:], in_=cst[:, 0:128]), reads=[cst_b], writes=[identb_b])
        S.barrier()

    with ExitStack() as ph:
        NTH = 17
        hnT = sb(ph, "hnT", [P, KC, NTH * P], BF16)
        hn_b = [[Buf("hn%d_%d" % (i, g)) for g in range(4)] for i in range(NTH)]
        lra = sb(ph, "lra", [64, NTH * P], BF16)
        lra_b = Buf("lra")
        wla = sb(ph, "wla", [P, KC, 64], BF16)
        wla_b = Buf("wla")
        S.emit("pool", lambda e: e.dma_start(out=wla[:], in_=w_la_v), writes=[wla_b], dma=wla_b)
        xt_rr = [0]

        def rstd_from_ssq(ssq, ssq_b, n):
            r, r_b = smallcol()
            S.emit("dve", lambda e: e.tensor_scalar(out=r, in0=ssq, scalar1=1.0 / n, scalar2=EPS, op0=ALU.mult,
                                                    op1=ALU.add), reads=[ssq_b], writes=[r_b])
            S.emit("act", lambda e: e.activation(out=r, in_=r, func=AF.Sqrt), reads=[r_b], writes=[r_b])
            S.emit("dve", lambda e: e.reciprocal(out=r, in_=r), reads=[r_b], writes=[r_b])
            return r, r_b

        def build_hn(src, row0, ntiles, modoff, tile0):
            with ExitStack() as bs:
                build_hn_(bs, src, row0, ntiles, modoff, tile0)
                S.barrier()

        def build_hn_(bs, src, row0, ntiles, modoff, tile0):
            xt = [sb(bs, "xt%d" % i, [P, D]) for i in range(2)]
            xt_b = [Buf("xt%d" % i) for i in range(2)]
            xns = [sb(bs, "xn%d" % i, [P, D]) for i in range(2)]
            xns_b = [Buf("xn%d" % i) for i in range(2)]
            junk = sb(bs, "junk", [P, D], BF16)
            junk_b = Buf("junk")
            def stats(t):
                xi = xt_rr[0] % 2
                xt_rr[0] += 1
                x_, x_b = xt[xi], xt_b[xi]
                r0 = row0 + t * P
                S.emit("sp", lambda e: e.dma_start(out=x_[:], in_=src[r0:r0 + P, :]), writes=[x_b], dma=x_b)
                ssq, ssq_b = smallcol()
                S.emit("act", lambda e: e.activation(out=junk[:], in_=x_[:], func=AF.Square, accum_out=ssq),
                       reads=[x_b], writes=[junk_b, ssq_b])
                r, r_b = rstd_from_ssq(ssq, ssq_b, D)
                xn, xn_b = xns[t % 2], xns_b[t % 2]
                S.emit("dve", lambda e: e.tensor_scalar(out=xn[:], in0=x_[:], scalar1=r, scalar2=None, op0=ALU.mult),
                       reads=[x_b, r_b], writes=[xn_b])

            def trans(t):
                xn, xn_b = xns[t % 2], xns_b[t % 2]
                col = (tile0 + t) * P
                for g in range(4):
                    hb = hn_b[tile0 + t][g]
                    pt, pt_b = psum()
                    for j in range(4):
                        kc = 4 * g + j
                        S.emit("pe", lambda e, pt=pt, j=j, kc=kc: e.transpose(
                            out=pt[:, j * P:(j + 1) * P], in_=xn[:, kc * P:(kc + 1) * P], identity=ident),
                            reads=[xn_b, cst_b], writes=[pt_b])
                    for j in range(4):
                        kc = 4 * g + j
                        a_ap = Ax[:, modoff + kc:modoff + kc + 1]
                        s_ap = Ax[:, modoff + 16 + kc:modoff + 16 + kc + 1]
                        if g % 2 == 0:
                            S.emit("dve", lambda e, pt=pt, j=j, kc=kc, a_ap=a_ap, s_ap=s_ap: e.tensor_scalar(
                                out=hnT[:, kc, col:col + P], in0=pt[:, j * P:(j + 1) * P], scalar1=a_ap, scalar2=s_ap,
                                op0=ALU.mult, op1=ALU.add), reads=[pt_b, Ax_b], writes=[hb])
                        else:
                            S.emit("act", lambda e, pt=pt, j=j, kc=kc, a_ap=a_ap, s_ap=s_ap: e.activation(
                                out=hnT[:, kc, col:col + P], in_=pt[:, j * P:(j + 1) * P], func=AF.Identity,
                                scale=a_ap, bias=s_ap), reads=[pt_b, Ax_b], writes=[hb])

            stats(0)
            for t in range(ntiles):
                if t + 1 < ntiles:
                    stats(t + 1)
                trans(t)

        def hn_bufs(c0, c1):
            return [b for tb in hn_b[c0 // P:(c1 + P - 1) // P] for b in tb]

        def build_lr(c0, T):
            S.emit("dve", lambda e: e.memset(lra[:, c0:c0 + T], 1.0), writes=[lra_b])
            for g0 in range(0, T, 512):
                n = min(512, T - g0)
                pt, pt_b = psum()
                a = c0 + g0
                for kc in range(KC):
                    S.emit("pe", lambda e, pt=pt, kc=kc, a=a, n=n: e.matmul(
                        pt[0:64, 0:n], lhsT=wla[:, kc, :], rhs=hnT[:, kc, a:a + n], start=(kc == 0),
                        stop=(kc == KC - 1)), reads=[wla_b] + hn_bufs(a, a + n), writes=[pt_b])
                S.emit("act", lambda e, pt=pt, a=a, n=n: e.activation(out=lra[0:16, a:a + n], in_=pt[0:16, 0:n],
                                                                      func=AF.Copy), reads=[pt_b], writes=[lra_b])
                S.emit("act", lambda e, pt=pt, a=a, n=n: e.activation(out=lra[32:48, a:a + n], in_=pt[32:48, 0:n],
                                                                      func=AF.Copy), reads=[pt_b], writes=[lra_b])

        def gla_stage(gs, c0, nch, dirs, full, zero_state):
            WA = sb(gs, "WA", [P, KC, 512], BF16)
            WA_b = Buf("WA")
            WB = sb(gs, "WB", [P, KC, 512], BF16)
            WB_b = Buf("WB")
            T = nch * P
            kfm = sb(gs, "kfm", [P, 2, T], BF16)
            kfm_b = Buf("kfm")
            vtm = sb(gs, "vtm", [P, nch, 512], BF16)
            vtm_b = [Buf("vtm%d" % i) for i in range(nch)]
            def mk(name, shape, dt=F32, n=2):
                return ([sb(gs, "%s%d" % (name, i), shape, dt) for i in range(n)],
                        [Buf("%s%d" % (name, i)) for i in range(n)])

            if full:
                qfm = sb(gs, "qfm", [P, 2, T], BF16)
                qfm_b = Buf("qfm")
                ggb = sb(gs, "ggb", [P, 512])
                ggb_b = Buf("ggb")
                ggr = sb(gs, "ggr", [1, 512])
                ggr_b = Buf("ggr")
                S.emit("sp", lambda e: e.dma_start(out=ggr[:], in_=ggo_d), writes=[ggr_b], dma=ggr_b)
                pt, pt_b = psum()
                S.emit("pe", lambda e: e.matmul(pt[:, :], lhsT=ones[0:1, :], rhs=ggr[0:1, :], start=True, stop=True),
                       reads=[ones_b, ggr_b], writes=[pt_b])
                S.emit("act", lambda e: e.activation(out=ggb[:], in_=pt[:, :], func=AF.Copy), reads=[pt_b],
                       writes=[ggb_b])
                obt, obt_b = mk("obt", [P, 512])
                osum, osum_b = mk("osum", [P, 512], n=3)
                jk = sb(gs, "jk", [P, 512], BF16)
                jk_b = Buf("jk")
                mT, mT_b = mk("mT", [P, 4, P], BF16)
                E1, E1_b = mk("E1", [P, 256], n=4)
                qe, qe_b = mk("qe", [P, 2, P], BF16, n=4)
                ke, ke_b = mk("ke", [P, 2, P], BF16)
                sc, sc_b = mk("sc", [P, P], BF16)
                rs_, rs_b = mk("rs", [P, 1], n=4)
                Sbf = [[sb(gs, "Sbf%d_%d" % (d, pp), [P, 2, 512], BF16) for pp in range(2)] for d in range(2)]
                Sbf_b = [[[Buf("Sbf%d_%d_%d" % (d, pp, dc)) for dc in range(2)] for pp in range(2)] for d in range(2)]
            else:
                off = sb(gs, "off", [P, 2])
                off_b = Buf("off")
                Sacc = sb(gs, "Sacc", [P, 2, 512])
                Sacc_b = Buf("Sacc")
            Sst = [sb(gs, "Sst%d" % d, [P, 2, 512]) for d in range(2)]
            Sst_b = [Buf("Sst%d" % d) for d in range(2)]
            ez, ez_b = mk("ez", [P, 256])
            nl, nl_b = mk("nl", [P, 256], BF16)
            E2, E2_b = mk("E2", [P, 256])
            dcy, dcy_b = mk("dcy", [P, 2], n=4)
            kd, kd_b = mk("kd", [P, 2, P])
            kdt, kdt_b = mk("kdt", [P, 256], BF16)
            PZ, PC, PK, PV0, PV1, PO0, PO1, PM = range(8)

            def proj_fm(dst, dst_b, dcg, wcol, scale):
                for g0 in range(0, T, 512):
                    n = min(512, T - g0)
                    pt, pt_b = psum()
                    a = c0 + g0
                    for kc in range(KC):
                        S.emit("pe", lambda e, pt=pt, kc=kc, a=a, n=n: e.matmul(
                            pt[:, 0:n], lhsT=WA[:, kc, wcol:wcol + P], rhs=hnT[:, kc, a:a + n], start=(kc == 0),
                            stop=(kc == KC - 1)), reads=[WA_b] + hn_bufs(a, a + n), writes=[pt_b])
                    S.emit("act", lambda e, pt=pt, g0=g0, n=n: e.activation(
                        out=dst[:, dcg, g0:g0 + n], in_=pt[:, 0:n], func=AF.Copy, scale=scale),
                        reads=[pt_b], writes=[dst_b])

            for h in range(4):
                if full:
                    S.emit("pool", lambda e, h=h: e.dma_start(out=WA[:, :, 0:256],
                                                              in_=w_in_v[:, :, QO + h * 256:QO + (h + 1) * 256]),
                           writes=[WA_b], dma=WA_b)
                S.emit("pool", lambda e, h=h: e.dma_start(out=WA[:, :, 256:512],
                                                          in_=w_in_v[:, :, KO + h * 256:KO + (h + 1) * 256]),
                       writes=[WA_b], dma=WA_b)
                S.emit("pool", lambda e, h=h: e.dma_start(out=WB[:], in_=w_in_v[:, :, VO + h * 512:VO + (h + 1) * 512]),
                       writes=[WB_b], dma=WB_b)
                if full:
                    for dc in range(2):
                        proj_fm(qfm, qfm_b, dc, dc * P, 1.0)
                for dc in range(2):
                    proj_fm(kfm, kfm_b, dc, 256 + dc * P, 1.0 / 16.0)
                for n in range(nch):
                    pt, pt_b = psum()
                    a = c0 + n * P
                    for kc in range(KC):
                        S.emit("pe", lambda e, pt=pt, kc=kc, a=a: e.matmul(
                            pt[:, :], lhsT=hnT[:, kc, a:a + P], rhs=WB[:, kc, :], start=(kc == 0),
                            stop=(kc == KC - 1)), reads=[WB_b] + hn_bufs(a, a + P), writes=[pt_b])
                    if n % 2 == 0:
                        S.emit("act", lambda e, pt=pt, n=n: e.activation(out=vtm[:, n, :], in_=pt[:, :], func=AF.Copy),
                               reads=[pt_b], writes=[vtm_b[n]])
                    else:
                        S.emit("dve", lambda e, pt=pt, n=n: e.tensor_copy(out=vtm[:, n, :], in_=pt[:, :]),
                               reads=[pt_b], writes=[vtm_b[n]])
                for d in dirs:
                    if zero_state:
                        S.emit("dve", lambda e, d=d: e.memset(Sst[d][:], 0.0), writes=[Sst_b[d]])
                    else:
                        S.emit("sp", lambda e, d=d, h=h: e.dma_start(
                            out=Sst[d][:].rearrange("p a b -> p (a b)"), in_=st_d[d, h]),
                            reads=[st_b[d][h]], writes=[Sst_b[d]], dma=Sst_b[d])
                    if full:
                        S.emit("act", lambda e, d=d: e.activation(out=Sbf[d][1][:], in_=Sst[d][:], func=AF.Copy),
                               reads=[Sst_b[d]], writes=Sbf_b[d][1])

                def pipeline(h, seq):
                    NPOS = len(seq)

                    def A1(p):
                        d, n, q = seq[p]
                        a = c0 + n * P
                        base = 32 * d
                        i = p % 2
                        S.emit("pe", lambda e: e.matmul(
                            ps_t[PZ][:, 0:256], lhsT=lra[base:base + 17, a:a + P],
                            rhs=wga[base:base + 17, h * 256:(h + 1) * 256], start=True, stop=True),
                            reads=[lra_b, wga_b], writes=[ps_b[PZ]])
                        S.emit("act", lambda e: e.activation(out=ez[i][:], in_=ps_t[PZ][:, 0:256], func=AF.Exp, scale=-1.0),
                               reads=[ps_b[PZ]], writes=[ez_b[i]])
                        S.emit("act", lambda e: e.activation(out=nl[i][:], in_=ez[i][:], func=AF.Ln, bias=1.0),
                               reads=[ez_b[i]], writes=[nl_b[i]])

                    def A2(p):
                        d, n, q = seq[p]
                        i = p % 2
                        i4 = p % 4
                        TR = tri[:, d * P:(d + 1) * P]
                        pc = ps_t[PC]
                        for dc in range(2):
                            S.emit("pe", lambda e, dc=dc: e.matmul(
                                pc[:, dc * P:(dc + 1) * P], lhsT=nl[i][:, dc * P:(dc + 1) * P], rhs=TR, start=True, stop=True),
                                reads=[nl_b[i], tri_b], writes=[ps_b[PC]])
                        S.emit("act", lambda e: e.activation(out=E2[i][:], in_=pc[:, 0:256], func=AF.Exp, scale=-1.0),
                               reads=[ps_b[PC]], writes=[E2_b[i]])
                        lastcol = (P - 1) if d == 0 else 0
                        pc3 = pc[:, 0:256].rearrange("p (a b) -> p a b", b=P)
                        if full:
                            S.emit("act", lambda e: e.activation(out=E1[i4][:], in_=pc[:, 0:256], func=AF.Exp),
                                   reads=[ps_b[PC]], writes=[E1_b[i4]])
                        else:
                            for dc in range(2):
                                col = dc * P + lastcol
                                S.emit("act", lambda e, dc=dc, col=col: e.activation(
                                    out=dcy[i4][:, dc:dc + 1], in_=pc[:, col:col + 1], func=AF.Exp, bias=off[:, dc:dc + 1]),
                                    reads=[ps_b[PC], off_b], writes=[dcy_b[i4]])
                            S.emit("act", lambda e: e.activation(out=ez[i][:, 0:2], in_=pc3[:, :, lastcol], func=AF.Copy),
                                   reads=[ps_b[PC]], writes=[ez_b[i]])
                            S.emit("dve", lambda e: e.tensor_tensor(out=off[:], in0=off[:], in1=ez[i][:, 0:2], op=ALU.add),
                                   reads=[off_b, ez_b[i]], writes=[off_b])

                    def A2b(p):
                        d, n, q = seq[p]
                        i = p % 2
                        i4 = p % 4
                        tk = slice(n * P, (n + 1) * P)
                        if full:
                            S.emit("dve", lambda e: e.tensor_tensor(
                                out=qe[i4][:], in0=qfm[:, :, tk], in1=E1[i4][:].rearrange("p (a b) -> p a b", b=P), op=ALU.mult),
                                reads=[qfm_b, E1_b[i4]], writes=[qe_b[i4]])
                            S.emit("dve", lambda e: e.tensor_tensor(
                                out=ke[i][:], in0=kfm[:, :, tk], in1=E2[i][:].rearrange("p (a b) -> p a b", b=P), op=ALU.mult),
                                reads=[kfm_b, E2_b[i]], writes=[ke_b[i]])
                        lastcol_ = (P - 1) if d == 0 else 0
                        for dc in range(2):
                            if full:
                                dsc, dsc_b = E1[i4][:, dc * P + lastcol_:dc * P + lastcol_ + 1], E1_b[i4]
                            else:
                                dsc, dsc_b = dcy[i4][:, dc:dc + 1], dcy_b[i4]
                            S.emit("dve", lambda e, dc=dc, dsc=dsc: e.scalar_tensor_tensor(
                                out=kd[i][:, dc, :], in0=E2[i][:, dc * P:(dc + 1) * P], scalar=dsc,
                                in1=kfm[:, dc, tk], op0=ALU.mult, op1=ALU.mult),
                                reads=[E2_b[i], dsc_b, kfm_b], writes=[kd_b[i]])

                    def A3(p):
                        d, n, q = seq[p]
                        i = p % 2
                        i4 = p % 4
                        MK = cst[:, 128 + d * P:128 + (d + 1) * P]
                        pk = ps_t[PK]
                        for dc in range(2):
                            S.emit("pe", lambda e, dc=dc: e.transpose(
                                out=pk[:, dc * P:(dc + 1) * P], in_=kd[i][:, dc, :], identity=ident),
                                reads=[kd_b[i], cst_b], writes=[ps_b[PK]])
                        if full:
                            for dc in range(2):
                                S.emit("pe", lambda e, dc=dc: e.matmul(
                                    pk[:, 256:384], lhsT=ke[i][:, dc, :], rhs=qe[i4][:, dc, :], start=(dc == 0), stop=(dc == 1)),
                                    reads=[ke_b[i], qe_b[i4]], writes=[ps_b[PK]])
                        S.emit("dve", lambda e: e.tensor_copy(out=kdt[i][:], in_=pk[:, 0:256]),
                               reads=[ps_b[PK]], writes=[kdt_b[i]])
                        if full:
                            S.emit("dve", lambda e: e.tensor_tensor(out=sc[i][:], in0=pk[:, 256:384], in1=MK, op=ALU.mult),
                                   reads=[ps_b[PK], cst_b], writes=[sc_b[i]])

                    def B1(p):
                        d, n, q = seq[p]
                        i = p % 2
                        i4 = p % 4
                        for dc, bank in ((0, PV0), (1, PV1)):
                            S.emit("pe", lambda e, dc=dc, bank=bank: e.matmul(
                                ps_t[bank][:, :], lhsT=kdt[i][:, dc * P:(dc + 1) * P], rhs=vtm[:, n, :], start=True, stop=True),
                                reads=[kdt_b[i], vtm_b[n]], writes=[ps_b[bank]])
                        if not full:
                            for dc, bank in ((0, PV0), (1, PV1)):
                                S.emit("dve", lambda e, dc=dc, bank=bank: e.tensor_tensor(
                                    out=Sacc[:, dc, :], in0=Sacc[:, dc, :], in1=ps_t[bank][:, :], op=ALU.add),
                                    reads=[Sacc_b, ps_b[bank]], writes=[Sacc_b])
                            return
                        pp, pq = q % 2, (q - 1) % 2
                        po, po_b = ps_t[PO0 + i], ps_b[PO0 + i]
                        S.emit("pe", lambda e: e.matmul(po[:, :], lhsT=sc[i][:], rhs=vtm[:, n, :], start=True, stop=False),
                               reads=[sc_b[i], vtm_b[n]], writes=[po_b])
                        for dc in range(2):
                            S.emit("pe", lambda e, dc=dc: e.matmul(
                                po[:, :], lhsT=qe[i4][:, dc, :], rhs=Sbf[d][pq][:, dc, :], start=False, stop=(dc == 1)),
                                reads=[qe_b[i4], Sbf_b[d][pq][dc]], writes=[po_b])
                        lastcol_ = (P - 1) if d == 0 else 0
                        for dc, bank in ((0, PV0), (1, PV1)):
                            dsc = E1[i4][:, dc * P + lastcol_:dc * P + lastcol_ + 1]
                            S.emit("dve", lambda e, dc=dc, bank=bank, dsc=dsc: e.scalar_tensor_tensor(
                                out=Sst[d][:, dc, :], in0=Sst[d][:, dc, :], scalar=dsc, in1=ps_t[bank][:, :],
                                op0=ALU.mult, op1=ALU.add), reads=[Sst_b[d], E1_b[i4], ps_b[bank]], writes=[Sst_b[d]])
                        S.emit("act", lambda e: e.activation(out=Sbf[d][pp][:], in_=Sst[d][:], func=AF.Copy),
                               reads=[Sst_b[d]], writes=Sbf_b[d][pp])

                    def B2(p):
                        d, n, q = seq[p]
                        i = p % 2
                        i3 = p % 3
                        i4 = p % 4
                        po, po_b = ps_t[PO0 + i], ps_b[PO0 + i]
                        if d == 1:
                            S.emit("act", lambda e: e.activation(out=obt[i][:], in_=po[:, :], func=AF.Copy),
                                   reads=[po_b], writes=[obt_b[i]])
                            S.emit("sp", lambda e: e.dma_start(out=ob_d[h, n], in_=obt[i][:]),
                                   reads=[obt_b[i]], writes=[ob_b[h][n]], dma=obt_b[i])
                            return
                        S.emit("sp", lambda e: e.dma_start(out=obt[i][:], in_=ob_d[h, n]),
                               reads=[ob_b[h][n]], writes=[obt_b[i]], dma=obt_b[i])
                        S.emit("dve", lambda e: e.tensor_tensor(out=osum[i3][:], in0=po[:, :], in1=obt[i][:], op=ALU.add),
                               reads=[po_b, obt_b[i]], writes=[osum_b[i3]])
                        r_ = rs_[i4][:, 0:1]
                        S.emit("act", lambda e: e.activation(out=jk[:], in_=osum[i3][:], func=AF.Square, accum_out=r_),
                               reads=[osum_b[i3]], writes=[jk_b, rs_b[i4]])
                        S.emit("act", lambda e: e.activation(out=r_, in_=r_, func=AF.Ln, scale=1.0 / 512.0, bias=EPS),
                               reads=[rs_b[i4]], writes=[rs_b[i4]])
                        S.emit("act", lambda e: e.activation(out=r_, in_=r_, func=AF.Exp, scale=-0.5),
                               reads=[rs_b[i4]], writes=[rs_b[i4]])

                    def B2b(p):
                        d, n, q = seq[p]
                        if d == 1:
                            return
                        i3 = p % 3
                        i4 = p % 4
                        S.emit("dve", lambda e: e.scalar_tensor_tensor(
                            out=osum[i3][:], in0=osum[i3][:], scalar=rs_[i4][:, 0:1], in1=ggb[:], op0=ALU.mult, op1=ALU.mult),
                            reads=[osum_b[i3], rs_b[i4], ggb_b], writes=[osum_b[i3]])

                    def B3(p):
                        d, n, q = seq[p]
                        if d == 1:
                            return
                        i = p % 2
                        i3 = p % 3
                        tk = slice(n * P, (n + 1) * P)
                        pm = ps_t[PM]
                        for j in range(4):
                            S.emit("pe", lambda e, j=j: e.transpose(
                                out=pm[:, j * P:(j + 1) * P], in_=osum[i3][:, j * P:(j + 1) * P], identity=ident),
                                reads=[osum_b[i3], cst_b], writes=[ps_b[PM]])
                        S.emit("act", lambda e: e.activation(
                            out=mT[i][:].rearrange("p a b -> p (a b)"), in_=pm[:, :], func=AF.Copy),
                            reads=[ps_b[PM]], writes=[mT_b[i]])
                        S.emit("sp", lambda e: e.dma_start(
                            out=mgla_d[h * 4:(h + 1) * 4, :, tk].rearrange("j p t -> p j t"), in_=mT[i][:]),
                            reads=[mT_b[i]], writes=[mgla_b[h * 4 + j] for j in range(4)], dma=mT_b[i])

                    if full:
                        stages = ((B1, 4), (A3, 3), (A2b, 2), (A2, 1), (A1, 0), (B2, 5), (B2b, 6), (B3, 7))
                    else:
                        stages = ((B1, 4), (A3, 3), (A2b, 2), (A2, 1), (A1, 0))
                    maxd = max(dl for _, dl in stages)
                    for k in range(NPOS + maxd):
                        for fn, dl in stages:
                            p = k - dl
                            if 0 <= p < NPOS:
                                fn(p)

                if full:
                    seq = [(1, n, q) for q, n in enumerate(range(nch - 1, -1, -1))] + \
                          [(0, n, q) for q, n in enumerate(range(nch))]
                    pipeline(h, seq)
                else:
                    for d in dirs:
                        S.emit("dve", lambda e: e.memset(off[:], 0.0), writes=[off_b])
                        S.emit("dve", lambda e: e.memset(Sacc[:], 0.0), writes=[Sacc_b])
                        rev = range(nch - 1, -1, -1) if d == 0 else range(nch)
                        pipeline(h, [(d, n, q) for q, n in enumerate(rev)])
                        tot, tot_b = dcy[0], dcy_b[0]
                        S.emit("act", lambda e, tot=tot: e.activation(out=tot[:], in_=off[:], func=AF.Exp),
                               reads=[off_b], writes=[tot_b])
                        for dc in range(2):
                            S.emit("dve", lambda e, d=d, dc=dc, tot=tot: e.scalar_tensor_tensor(
                                out=Sst[d][:, dc, :], in0=Sst[d][:, dc, :], scalar=tot[:, dc:dc + 1], in1=Sacc[:, dc, :],
                                op0=ALU.mult, op1=ALU.add), reads=[Sst_b[d], tot_b, Sacc_b], writes=[Sst_b[d]])
                        S.emit("sp", lambda e, d=d, h=h: e.dma_start(
                            out=st_d[d, h], in_=Sst[d][:].rearrange("p a b -> p (a b)")),
                            reads=[Sst_b[d]], writes=[st_b[d][h]], dma=Sst_b[d])

        def lru_stage(ls, c0, T, dirs, full, halo_l, halo_r):
            nd = len(dirs)
            wl = [sb(ls, "wl%d" % i, [P, KC, P], BF16) for i in range(2)]
            wl_b = [Buf("wl%d" % i) for i in range(2)]
            wrj = [sb(ls, "wrj%d" % i, [P, 4, P], BF16) for i in range(3)]
            wrj_b = [Buf("wrj%d" % i) for i in range(3)]
            dg = [sb(ls, "dg%d" % i, [P, 5, P], BF16) for i in range(2)]
            dg_b = [Buf("dg%d" % i) for i in range(2)]
            xin = [sb(ls, "xin%d" % i, [P, T + 4], BF16) for i in range(2)]
            xin_b = [Buf("xin%d" % i) for i in range(2)]
            xcb = [sb(ls, "xcb%d" % i, [P, T], BF16) for i in range(2)]
            xcb_b = [Buf("xcb%d" % i) for i in range(2)]
            NS = 3
            rr = [sb(ls, "rr%d" % i, [P, T]) for i in range(NS)]
            rr_b = [Buf("rr%d" % i) for i in range(NS)]
            ig = [sb(ls, "ig%d" % i, [P, T]) for i in range(NS)]
            ig_b = [Buf("ig%d" % i) for i in range(NS)]
            aa = [sb(ls, "aa%d" % i, [P, T]) for i in range(NS)]
            aa_b = [Buf("aa%d" % i) for i in range(NS)]
            hh = [sb(ls, "hh%d" % i, [P, T], BF16 if full else F32) for i in range(2)]
            hh_b = [Buf("hh%d" % i) for i in range(2)]
            groups = [(g0, min(512, T - g0)) for g0 in range(0, T, 512)]
            groups2 = [(g0, min(1024, T - g0)) for g0 in range(0, T, 1024)]
            units = [(j, d) for j in range(16) for d in dirs]
            NU = len(units)
            ev = [0]

            def evac_copy(dst_ap, src_ap, rbufs, wbufs):
                ev[0] += 1
                if ev[0] % 2 == 0:
                    S.emit("act", lambda e: e.activation(out=dst_ap, in_=src_ap, func=AF.Copy), reads=rbufs, writes=wbufs)
                else:
                    S.emit("dve", lambda e: e.tensor_copy(out=dst_ap, in_=src_ap), reads=rbufs, writes=wbufs)

            def a_setup(j):
                s_ = j % 2
                S.emit("pool", lambda e: e.dma_start(out=wl[s_][:], in_=w_in_v[:, :, LIO + j * P:LIO + (j + 1) * P]),
                       writes=[wl_b[s_]], dma=wl_b[s_])
                w3 = j % 3
                S.emit("pool", lambda e: e.dma_start(out=wrj[w3][:].rearrange("p a b -> p (a b)"), in_=wrg_d[j]),
                       writes=[wrj_b[w3]], dma=wrj_b[w3])
                for t in range(5):
                    col = PF_WCONV + t * 16 + j
                    S.emit("dve", lambda e, t=t, col=col: e.tensor_scalar(
                        out=dg[s_][:, t, :], in0=cst[:, 0:128], scalar1=pf[:, col:col + 1], scalar2=None, op0=ALU.mult),
                        reads=[cst_b, pf_b], writes=[dg_b[s_]])
                if not halo_l:
                    S.emit("dve", lambda e: e.memset(xin[s_][:, 0:2], 0.0), writes=[xin_b[s_]])
                if not halo_r:
                    S.emit("dve", lambda e: e.memset(xin[s_][:, T + 2:T + 4], 0.0), writes=[xin_b[s_]])

            def a_pieces(j):
                s_ = j % 2
                out = []
                specs = [(c0 + g0, n, 2 + g0) for (g0, n) in groups]
                if halo_l:
                    specs.append((c0 - 2, 2, 0))
                if halo_r:
                    specs.append((c0 + T, 2, T + 2))
                for (a, n, o) in specs:
                    def pe(a=a, n=n):
                        pt, pt_b = psum()
                        for kc in range(KC):
                            S.emit("pe", lambda e, kc=kc: e.matmul(
                                pt[:, 0:n], lhsT=wl[s_][:, kc, :], rhs=hnT[:, kc, a:a + n], start=(kc == 0),
                                stop=(kc == KC - 1)), reads=[wl_b[s_]] + hn_bufs(a, a + n), writes=[pt_b])
                        return pt, pt_b

                    def ev(pt, pt_b, n=n, o=o):
                        S.emit("dve", lambda e: e.tensor_copy(out=xin[s_][:, o:o + n], in_=pt[:, 0:n]),
                               reads=[pt_b], writes=[xin_b[s_]])
                    out.append((pe, ev))
                return out

            def c_pieces(j):
                s_ = j % 2
                out = []
                for (g0, n) in groups:
                    def pe(g0=g0, n=n):
                        pt, pt_b = psum()
                        for t in range(5):
                            S.emit("pe", lambda e, t=t: e.matmul(
                                pt[:, 0:n], lhsT=dg[s_][:, t, :], rhs=xin[s_][:, g0 + t:g0 + t + n], start=(t == 0),
                                stop=(t == 4)), reads=[dg_b[s_], xin_b[s_]], writes=[pt_b])
                        return pt, pt_b

                    def ev(pt, pt_b, g0=g0, n=n):
                        S.emit("dve", lambda e: e.tensor_scalar(
                            out=xcb[s_][:, g0:g0 + n], in0=pt[:, 0:n], scalar1=pf[:, PF_BCONV + j:PF_BCONV + j + 1],
                            scalar2=None, op0=ALU.add), reads=[pt_b, pf_b], writes=[xcb_b[s_]])
                    out.append((pe, ev))
                return out

            def run_pieces(pieces):
                pend = [(ev, pe()) for (pe, ev) in pieces]
                return pend

            def run_evacs(pend):
                for ev, (pt, pt_b) in pend:
                    ev(pt, pt_b)

            def stage_g_half(u, hf):
                if hf >= len(groups2):
                    return
                j, d = units[u]
                s_ = j % 2
                i3 = u % NS
                g0, n2 = groups2[hf]
                for (which, dst, dst_b) in ((0, rr[i3], rr_b[i3]), (1, ig[i3], ig_b[i3])):
                    wi = 2 * d + which
                    bcol = wi * 16 + j
                    pt2, pt2_b = psum2()
                    for h0_ in range(0, n2, 512):
                        n = min(512, n2 - h0_)
                        S.emit("pe", lambda e, pt2=pt2, h0_=h0_, n=n, wi=wi: e.matmul(
                            pt2[:, h0_:h0_ + n], lhsT=wrj[j % 3][:, wi, :], rhs=xcb[s_][:, g0 + h0_:g0 + h0_ + n],
                            start=True, stop=True), reads=[wrj_b[j % 3], xcb_b[s_]], writes=[pt2_b[h0_ // 512]])
                    S.emit("act", lambda e, pt2=pt2, dst=dst, bcol=bcol: e.activation(
                        out=dst[:, g0:g0 + n2], in_=pt2[:, 0:n2], func=AF.Tanh, scale=0.5, bias=hbrg[:, bcol:bcol + 1]),
                        reads=pt2_b + [hbrg_b], writes=[dst_b])

            def stage_eq(u):
                j, d = units[u]
                s_ = j % 2
                i3 = u % NS
                hc = hcl[:, d * 16 + j:d * 16 + j + 1]
                c2 = cl[:, d * 16 + j:d * 16 + j + 1]
                S.emit("act", lambda e: e.activation(out=aa[i3][:], in_=rr[i3][:], func=AF.Exp, scale=hc, bias=hc),
                       reads=[rr_b[i3], hcl_b], writes=[aa_b[i3]])
                S.emit("act", lambda e: e.activation(out=rr[i3][:], in_=rr[i3][:], func=AF.Exp, scale=c2, bias=c2),
                       reads=[rr_b[i3], cl_b], writes=[rr_b[i3]])
                S.emit("dve", lambda e: e.scalar_tensor_tensor(
                    out=ig[i3][:], in0=ig[i3][:], scalar=1.0, in1=xcb[s_][:], op0=ALU.add, op1=ALU.mult),
                    reads=[ig_b[i3], xcb_b[s_]], writes=[ig_b[i3]])
                S.emit("act", lambda e: e.activation(out=rr[i3][:], in_=rr[i3][:], func=AF.Sqrt, scale=-0.25, bias=0.25),
                       reads=[rr_b[i3]], writes=[rr_b[i3]])
                S.emit("dve", lambda e: e.tensor_tensor(out=ig[i3][:], in0=ig[i3][:], in1=rr[i3][:], op=ALU.mult),
                       reads=[ig_b[i3], rr_b[i3]], writes=[ig_b[i3]])

            def stage_sc(u):
                j, d = units[u]
                i3 = u % NS
                i2 = u % 2
                h0 = hst[:, d * 16 + j:d * 16 + j + 1]
                if d == 0:
                    S.emit("dve", lambda e: e.tensor_tensor_scan(
                        out=hh[i2][:], data0=aa[i3][:], data1=ig[i3][:], initial=h0, op0=ALU.mult, op1=ALU.add),
                        reads=[aa_b[i3], ig_b[i3], hst_b[0][j]], writes=[hh_b[i2]])
                    if not full:
                        S.emit("dve", lambda e: e.tensor_copy(out=h0, in_=hh[i2][:, T - 1:T]),
                               reads=[hh_b[i2]], writes=[hst_b[0][j]])
                else:
                    S.emit("dve", lambda e: e.tensor_tensor_scan(
                        out=hh[i2][:, ::-1], data0=aa[i3][:, ::-1], data1=ig[i3][:, ::-1], initial=h0, op0=ALU.mult,
                        op1=ALU.add), reads=[aa_b[i3], ig_b[i3], hst_b[1][j]], writes=[hh_b[i2]])
                    if not full:
                        S.emit("dve", lambda e: e.tensor_copy(out=h0, in_=hh[i2][:, 0:1]),
                               reads=[hh_b[i2]], writes=[hst_b[1][j]])
                if full:
                    S.emit("sp", lambda e: e.dma_start(out=mlru_d[d, j], in_=hh[i2][:]), reads=[hh_b[i2]],
                           writes=[mlru_b[d][j]], dma=hh_b[i2])

            for j0 in (0, 1):
                a_setup(j0)
                run_evacs(run_pieces(a_pieces(j0)))
            run_evacs(run_pieces(c_pieces(0)))
            slot_pieces = {}
            for k in range(NU + 2):
                if 0 <= k - 2 < NU:
                    stage_sc(k - 2)
                if 0 <= k - 1 < NU:
                    stage_eq(k - 1)
                li, jb = k % nd, k // nd
                if li == 0:
                    cp = c_pieces(jb + 1) if jb + 1 < 16 else []
                    ap = a_pieces(jb + 2) if jb + 2 < 16 else []
                    if jb + 2 < 16:
                        a_setup(jb + 2)
                    for q in range(nd):
                        slot_pieces[q] = [cp[q::nd], ap[q::nd]]
                if k < NU:
                    stage_g_half(k, 0)
                if k < NU:
                    for batch in slot_pieces.get(li, []):
                        run_evacs(run_pieces(batch))
                    stage_g_half(k, 1)

        def merge_gate_stage(ms, silu_jobs):
            wm = [sb(ms, "wm%d" % i, [P, KC, P], BF16) for i in range(2)]
            wm_b = [Buf("wm%d" % i) for i in range(2)]
            sgb = [sb(ms, "sgb%d" % i, [P, T_OWN], BF16) for i in range(2)]
            sgb_b = [Buf("sgb%d" % i) for i in range(2)]
            yld = [sb(ms, "yld%d" % i, [P, T_OWN], BF16) for i in range(2)]
            yld_b = [Buf("yld%d" % i) for i in range(2)]
            yl2 = [sb(ms, "yl2%d" % i, [P, T_OWN], BF16) for i in range(2)]
            yl2_b = [Buf("yl2%d" % i) for i in range(2)]
            jobs = []
            add2 = {}
            for job in silu_jobs:
                off, dt_, db_ = job[0], job[1], job[2]
                for i in range(16):
                    jobs.append(("silu", off + i * P, None, dt_, db_, i))
                    if len(job) > 3:
                        add2[len(jobs) - 1] = (job[3], job[4])
            for fc in range(16):
                for which, off in ((0, MGO), (1, MLO)):
                    jobs.append(("sig", off + fc * P, PF_BM + which * 16 + fc, sg_d[which], sg_b[which], fc))
            for k, (kind, col, bcol, dt_, db_, i) in enumerate(jobs):
                w, w_b = wm[k % 2], wm_b[k % 2]
                o, o_b = sgb[k % 2], sgb_b[k % 2]
                S.emit("pool", lambda e, w=w, col=col: e.dma_start(out=w[:], in_=w_in_v[:, :, col:col + P]),
                       writes=[w_b], dma=w_b)
                if kind == "silu":
                    yl_, yl_b_ = yld[k % 2], yld_b[k % 2]
                    S.emit("sp", lambda e, yl_=yl_, dt_=dt_, i=i: e.dma_start(out=yl_[:], in_=dt_[i]),
                           reads=[db_[i]], writes=[yl_b_], dma=yl_b_)
                    if k in add2:
                        d2, d2b = add2[k]
                        y2_, y2_b_ = yl2[k % 2], yl2_b[k % 2]
                        S.emit("sp", lambda e, y2_=y2_, d2=d2, i=i: e.dma_start(out=y2_[:], in_=d2[i]),
                               reads=[d2b[i]], writes=[y2_b_], dma=y2_b_)
                        S.emit("dve", lambda e, yl_=yl_, y2_=y2_: e.tensor_tensor(out=yl_[:], in0=yl_[:], in1=y2_[:], op=ALU.add),
                               reads=[yl_b_, y2_b_], writes=[yl_b_])
                for g0 in range(0, T_OWN, 512):
                    pt, pt_b = psum()
                    for kc in range(KC):
                        S.emit("pe", lambda e, pt=pt, w=w, kc=kc, g0=g0: e.matmul(
                            pt[:, :], lhsT=w[:, kc, :], rhs=hnT[:, kc, g0:g0 + 512], start=(kc == 0),
                            stop=(kc == KC - 1)), reads=[w_b] + hn_bufs(g0, g0 + 512), writes=[pt_b])
                    if kind == "sig":
                        S.emit("act", lambda e, pt=pt, o=o, g0=g0, bcol=bcol: e.activation(
                            out=o[:, g0:g0 + 512], in_=pt[:, :], func=AF.Sigmoid, bias=pf[:, bcol:bcol + 1]),
                            reads=[pt_b, pf_b], writes=[o_b])
                    else:
                        S.emit("act", lambda e, pt=pt, o=o, g0=g0: e.activation(
                            out=o[:, g0:g0 + 512], in_=pt[:, :], func=AF.Silu), reads=[pt_b], writes=[o_b])
                if kind == "silu":
                    S.emit("dve", lambda e, o=o, yl_=yl_: e.tensor_tensor(out=o[:], in0=o[:], in1=yl_[:], op=ALU.mult),
                           reads=[o_b, yl_b_], writes=[o_b])
                S.emit("sp", lambda e, o=o, dt_=dt_, i=i: e.dma_start(out=dt_[i], in_=o[:]),
                       reads=[o_b], writes=[db_[i]], dma=o_b)

        S.mark("ctx_hn")
        build_hn(ctxl, 0, 2, 32, 0)
        build_lr(0, 256)
        S.mark("ctx_gla")
        with ExitStack() as st:
            gla_stage(st, 0, 2, (0, 1), False, True)
            S.barrier()
        S.mark("ctx_lru")
        with ExitStack() as st:
            lru_stage(st, 0, 256, (0, 1), False, False, False)
            S.barrier()
        S.mark("oth_hn")
        build_hn(xl, T_OWN - P, 17, 0, 0)
        build_lr(P, T_OWN)
        S.mark("oth_gla")
        with ExitStack() as st:
            gla_stage(st, P, 16, (1,), False, False)
            S.barrier()
        S.mark("oth_lru")
        with ExitStack() as st:
            lru_stage(st, P, T_OWN, (1,), False, True, False)
            S.barrier()
        S.mark("own_hn")
        build_hn(xl, 0, 17, 0, 0)
        build_lr(0, T_OWN)
        S.mark("own_gla")
        with ExitStack() as st:
            gla_stage(st, 0, 16, (0, 1), True, False)
            S.barrier()
        S.mark("own_lru")
        with ExitStack() as st:
            lru_stage(st, 0, T_OWN, (0, 1), True, False, True)
            S.barrier()
        S.mark("mgate")
        with ExitStack() as st:
            merge_gate_stage(st, [(GGO, mgla_d, mgla_b), (LGO, mlru_d[0], mlru_b[0], mlru_d[1], mlru_b[1])])
            S.barrier()
        S.mark("p6")

    with ExitStack() as ph:
        mres = sb(ph, "mres", [P, KC, T_OWN], BF16)
        mres_b = [Buf("mres%d" % i) for i in range(KC)]
        macc = sb(ph, "macc", [P, KC, T_OWN], BF16)
        macc_b = [Buf("macc%d" % i) for i in range(KC)]
        with ExitStack() as p6:
            wo = [sb(p6, "wo%d" % i, [P, KC, P], BF16) for i in range(2)]
            wo_b = [Buf("wo%d" % i) for i in range(2)]
            sgl = [sb(p6, "sgl%d" % i, [P, T_OWN], BF16) for i in range(2)]
            sgl_b = [Buf("sgl%d" % i) for i in range(2)]
            tmp = sb(p6, "tmp6", [P, 512])
            tmp_b = Buf("tmp6")
            for which, (md, mdb, wsrc) in enumerate(((mgla_d, mgla_b, w_o_gla), (mlru_d[0], mlru_b[0], w_o_rnn))):
                wv = wsrc.rearrange("(kc p) n -> p kc n", p=P)
                for ec in range(KC):
                    S.emit("sp", lambda e, md=md, ec=ec: e.dma_start(out=mres[:, ec, :], in_=md[ec]),
                           reads=[mdb[ec]], writes=[mres_b[ec]], dma=mres_b[ec])
                for fc in range(16):
                    w, w_b = wo[fc % 2], wo_b[fc % 2]
                    sgt_, sgt_b_ = sgl[fc % 2], sgl_b[fc % 2]
                    S.emit("pool", lambda e, w=w, wv=wv, fc=fc: e.dma_start(out=w[:], in_=wv[:, :, fc * P:(fc + 1) * P]),
                           writes=[w_b], dma=w_b)
                    S.emit("sp", lambda e, sgt_=sgt_, which=which, fc=fc: e.dma_start(out=sgt_[:], in_=sg_d[which, fc]),
                           reads=[sg_b[which][fc]], writes=[sgt_b_], dma=sgt_b_)
                    for g0 in range(0, T_OWN, 512):
                        pt, pt_b = psum()
                        for ec in range(KC):
                            S.emit("pe", lambda e, pt=pt, w=w, ec=ec, g0=g0: e.matmul(
                                pt[:, :], lhsT=w[:, ec, :], rhs=mres[:, ec, g0:g0 + 512], start=(ec == 0),
                                stop=(ec == KC - 1)), reads=[w_b, mres_b[ec]], writes=[pt_b])
                        if which == 0:
                            S.emit("dve", lambda e, pt=pt, sgt_=sgt_, fc=fc, g0=g0: e.tensor_tensor(
                                out=macc[:, fc, g0:g0 + 512], in0=pt[:, :], in1=sgt_[:, g0:g0 + 512], op=ALU.mult),
                                reads=[pt_b, sgt_b_], writes=[macc_b[fc]])
                        else:
                            S.emit("dve", lambda e, pt=pt, sgt_=sgt_, g0=g0: e.tensor_tensor(
                                out=tmp[:], in0=pt[:, :], in1=sgt_[:, g0:g0 + 512], op=ALU.mult),
                                reads=[pt_b, sgt_b_], writes=[tmp_b])
                            S.emit("dve", lambda e, fc=fc, g0=g0: e.tensor_tensor(
                                out=macc[:, fc, g0:g0 + 512], in0=macc[:, fc, g0:g0 + 512], in1=tmp[:], op=ALU.add),
                                reads=[tmp_b, macc_b[fc]], writes=[macc_b[fc]])
            S.barrier()
        S.mark("p7")
        with ExitStack() as p7:
            wov = w_out.rearrange("(kc p) n -> p kc n", p=P)
            S.emit("pool", lambda e: e.dma_start(out=mres[:], in_=wov), writes=mres_b, dma=wo_b[0])
            Gb = sb(p7, "Gb", [P, D])
            Gb_b = Buf("Gb")
            Gf = sb(p7, "Gf", [P, D])
            Gf_b = Buf("Gf")
            gfr = sb(p7, "gfr", [1, D])
            gfr_b = Buf("gfr")
            for (rsrc, rsrc_b, dst, dst_b) in ((grow_d, [growd_b], Gb, Gb_b), (gfin_d, [], Gf, Gf_b)):
                S.emit("sp", lambda e, rsrc=rsrc: e.dma_start(out=gfr[:], in_=rsrc), reads=rsrc_b, writes=[gfr_b],
                       dma=gfr_b)
                row, row_b = gfr, gfr_b
                for g in range(4):
                    pt, pt_b = psum()
                    S.emit("pe", lambda e, pt=pt, row=row, g=g: e.matmul(
                        pt[:, :], lhsT=ones[0:1, :], rhs=row[0:1, g * 512:(g + 1) * 512], start=True, stop=True),
                        reads=[ones_b, row_b], writes=[pt_b])
                    S.emit("act", lambda e, pt=pt, dst=dst, g=g: e.activation(out=dst[:, g * 512:(g + 1) * 512],
                                                                              in_=pt[:, :], func=AF.Copy),
                           reads=[pt_b], writes=[dst_b])
            xo = sb(p7, "xo", [P, D])
            xo_b = Buf("xo")
            rt = sb(p7, "rt", [P, D])
            rt_b = Buf("rt")
            for t in range(NT_OWN):
                S.emit("sp", lambda e, t=t: e.dma_start(out=xo[:], in_=xl[t * P:(t + 1) * P, :]), writes=[xo_b], dma=xo_b)
                for g in range(4):
                    pt, pt_b = psum()
                    for kc in range(KC):
                        S.emit("pe", lambda e, pt=pt, kc=kc, t=t, g=g: e.matmul(
                            pt[:, :], lhsT=macc[:, kc, t * P:(t + 1) * P], rhs=mres[:, kc, g * 512:(g + 1) * 512],
                            start=(kc == 0), stop=(kc == KC - 1)), reads=[macc_b[kc], mres_b[kc]], writes=[pt_b])
                    S.emit("dve", lambda e, pt=pt, g=g: e.tensor_tensor(
                        out=rt[:, g * 512:(g + 1) * 512], in0=pt[:, :], in1=Gb[:, g * 512:(g + 1) * 512], op=ALU.mult),
                        reads=[pt_b, Gb_b], writes=[rt_b])
                S.emit("dve", lambda e: e.tensor_tensor(out=rt[:], in0=rt[:], in1=xo[:], op=ALU.add),
                       reads=[rt_b, xo_b], writes=[rt_b])
                ssq, ssq_b = smallcol()
                S.emit("act", lambda e, ssq=ssq: e.activation(out=xo[:], in_=rt[:], func=AF.Square, accum_out=ssq),
                       reads=[rt_b], writes=[xo_b, ssq_b])
                r, r_b = smallcol()
                S.emit("dve", lambda e, r=r, ssq=ssq: e.tensor_scalar(out=r, in0=ssq, scalar1=1.0 / D, scalar2=EPS,
                                                                      op0=ALU.mult, op1=ALU.add),
                       reads=[ssq_b], writes=[r_b])
                S.emit("act", lambda e, r=r: e.activation(out=r, in_=r, func=AF.Sqrt), reads=[r_b], writes=[r_b])
                S.emit("dve", lambda e, r=r: e.reciprocal(out=r, in_=r), reads=[r_b], writes=[r_b])
                S.emit("dve", lambda e, r=r: e.scalar_tensor_tensor(out=rt[:], in0=rt[:], scalar=r, in1=Gf[:],
                                                                    op0=ALU.mult, op1=ALU.mult),
                       reads=[rt_b, r_b, Gf_b], writes=[rt_b])
                op = S.emit("sp", lambda e, t=t: e.dma_start(out=yl[t * P:(t + 1) * P, :], in_=rt[:]),
                            reads=[rt_b], writes=[yl_b], dma=rt_b)
                S.final_waits.append(op)
            S.barrier()

    S.mark("end")
    S.replay(es)
    es.close()
    nc._marks = S.marks
    return nc


def _fm(v):
    v = np.asarray(v, np.float32).reshape(-1, P)
    return np.ascontiguousarray(v.T)


_NC_CACHE = {}


def kernel(x, c, ctx, c_ctx, w_ada, b_ada, g_norm, w_in, w_gla_a, b_gla_a, g_gla_out, w_conv, b_conv,
           w_rg_a, b_rg_a, w_rg_x, b_rg_x, lam, w_o_gla, w_o_rnn, b_merge, w_out, g_final):
    f = lambda a: np.ascontiguousarray(np.asarray(a, np.float32))
    x, c, ctx, c_ctx = f(x), f(c), f(ctx), f(c_ctx)
    w_ada0, b_ada0, g_norm0, w_in0 = f(w_ada)[0], f(b_ada)[0], f(g_norm)[0], f(w_in)[0]
    w_gla_a0, b_gla_a0, g_gla_out0 = f(w_gla_a)[0], f(b_gla_a)[0], f(g_gla_out)[0]
    w_conv0, b_conv0 = f(w_conv)[0], f(b_conv)[0]
    w_rg_a0, b_rg_a0, w_rg_x0, b_rg_x0, lam0 = f(w_rg_a)[0], f(b_rg_a)[0], f(w_rg_x)[0], f(b_rg_x)[0], f(lam)[0]
    w_o_gla0, w_o_rnn0, b_merge0, w_out0, g_final0 = f(w_o_gla)[0], f(w_o_rnn)[0], f(b_merge)[0], f(w_out)[0], f(g_final)

    consts = np.zeros((P, 384), np.float32)
    consts[:, 0:128] = np.eye(P, dtype=np.float32)
    s_idx = np.arange(P)[:, None]
    c_idx = np.arange(P)[None, :]
    consts[:, 128:256] = (s_idx <= c_idx)
    consts[:, 256:384] = (s_idx >= c_idx)

    per_half = []
    for hf in range(2):
        dF, dB = (0, 1) if hf == 0 else (1, 0)
        w_la = np.zeros((D, 64), np.float32)
        lo = 6144
        w_la[:, 0:16] = w_in0[:, lo + 16 * dF: lo + 16 * dF + 16]
        w_la[:, 32:48] = w_in0[:, lo + 16 * dB: lo + 16 * dB + 16]
        wga = np.zeros((64, 1024), np.float32)
        wga[0:16] = w_gla_a0[dF]
        wga[16] = b_gla_a0[dF]
        wga[32:48] = w_gla_a0[dB]
        wga[48] = b_gla_a0[dB]
        wr = np.stack([w_rg_a0[dF], w_rg_x0[dF], w_rg_a0[dB], w_rg_x0[dB]], 0)
        wrg = np.ascontiguousarray(wr.transpose(1, 2, 0, 3)).reshape(16, P, 4 * P)
        taps = np.zeros((5, D), np.float32)
        if hf == 0:
            taps[0:4] = w_conv0
        else:
            taps[1:5] = w_conv0[::-1]
        pf = np.concatenate([
            _fm(b_ada0), _fm(g_norm0), _fm(b_conv0),
            np.concatenate([_fm(taps[t]) for t in range(5)], 1),
            _fm(b_rg_a0[dF]), _fm(b_rg_x0[dF]), _fm(b_rg_a0[dB]), _fm(b_rg_x0[dB]),
            _fm(lam0[dF]), _fm(lam0[dB]), _fm(b_merge0)], 1)
        assert pf.shape == (P, PF_N)
        per_half.append(dict(w_la=w_la, wga=wga, wrg=wrg, pf=np.ascontiguousarray(pf)))

    shared = dict(consts=consts, w_ada=w_ada0, bgate=np.ascontiguousarray(b_ada0[None, 2 * D:3 * D]), w_in=w_in0,
                  ggo=np.ascontiguousarray(g_gla_out0[None, :]), gfin=np.ascontiguousarray(g_final0[None, :]),
                  w_o_gla=w_o_gla0, w_o_rnn=w_o_rnn0, w_out=w_out0)
    in_maps = []
    for b in range(4):
        for hf in range(2):
            if hf == 0:
                xl_, ctxl_ = x[b], ctx[b]
            else:
                xl_, ctxl_ = np.ascontiguousarray(x[b][::-1]), np.ascontiguousarray(ctx[b][::-1])
            cc = np.stack([c[b], c_ctx], 0)
            ccT = np.ascontiguousarray(cc.reshape(2, KC, P).transpose(2, 1, 0)).reshape(P, 32)
            m = dict(shared)
            m.update(per_half[hf])
            m.update(xl=xl_, ctxl=ctxl_, ccT=ccT)
            in_maps.append(m)

    if "nc" not in _NC_CACHE:
        _NC_CACHE["nc"] = build_nc()
    nc = _NC_CACHE["nc"]
    res = run_bass_kernel_spmd(nc, in_maps, core_ids=list(range(8)))
    out = np.empty((4, 4096, D), np.float32)
    for b in range(4):
        for hf in range(2):
            y = np.asarray(res.results[b * 2 + hf]["yl"], np.float32)
            if hf == 0:
                out[b, 0:T_OWN] = y
            else:
                out[b, T_OWN:] = y[::-1]
    return out
```
